# Optimizing a Trainium2 kernel written in Bass

```python
import jax, jax.numpy as jnp
from jax import lax
import numpy as np

D_MODEL = 2048
BATCH = 16
SEQ = 256
DEPTH = 2
DEC_BATCH = 2
DEC_SEQ = 1024
PAST_LEN = 512

GRID_W = 64
N_MOD = 9
D_FF = 256 * ((8 * D_MODEL // 3 + 255) // 256)
HG_DK = 128
HG_DV = 128
HG_WIDTH = D_MODEL // 4
HG_HEADS = HG_WIDTH // HG_DK
HG_CHUNK = 16
NA_HEAD_DIM = 128
NA_WIDTH = D_MODEL // 2
NA_HEADS = NA_WIDTH // NA_HEAD_DIM
NA_KH = 8
NA_KW = 16
NA_QBW = 8
NA_BAND = NA_QBW + NA_KW
NA_QBLOCK = 128
NA_SCALE = NA_HEAD_DIM ** -0.5
LRU_WIDTH = D_MODEL // 4
LRU_BLOCKS = 8
LRU_BW = LRU_WIDTH // LRU_BLOCKS
LRU_CONV = 4
LRU_C = 8.0
IN_SPLITS = (HG_WIDTH,) * 5 + (NA_WIDTH,) * 3 + (LRU_WIDTH,) * 2 + (3 * D_MODEL,)
D_IN = sum(IN_SPLITS)
EPS = 1e-6
NEG = -1e30

kernel_name = 'hybrid_flow_prefix_trunk'


def rmsnorm(x, g):
    xf = x.astype(jnp.float32)
    y = xf * lax.rsqrt(jnp.mean(xf * xf, axis=-1, keepdims=True) + EPS)
    return (y * g.astype(jnp.float32)).astype(x.dtype)


def adaln_params(cvec, w, b):
    m = jax.nn.silu(cvec) @ w + b
    return m.reshape(cvec.shape[0], N_MOD, D_MODEL)


def modulated_norm(x, mods, slot, g):
    shift = mods[:, None, 3 * slot]
    scale = mods[:, None, 3 * slot + 1]
    return rmsnorm(x, g) * (1 + scale) + shift


def gate_of(mods, slot):
    return mods[:, None, 3 * slot + 2]


def swiglu(h, w_up, w_down):
    a, u = jnp.split(h @ w_up, 2, axis=-1)
    return (jax.nn.silu(a) * u) @ w_down


def ffn_sublayer(x, mods, slot, g, w_up, w_down):
    h = modulated_norm(x, mods, slot, g)
    return x + 0.5 * gate_of(mods, slot) * swiglu(h, w_up, w_down)


def split_projection(z):
    parts, start = [], 0
    for width in IN_SPLITS:
        parts.append(z[..., start:start + width])
        start += width
    return parts


def hgrn_lower_bounds(logits):
    p = jax.nn.softmax(logits.astype(jnp.float32), axis=0)
    cum = jnp.cumsum(p, axis=0)
    return cum - cum[0:1]


def hgrn2_chunk_scan(q, log_f, k, v, s0):
    B, N, H, DK = q.shape
    DV = v.shape[-1]
    nC = N // HG_CHUNK
    q, log_f, k = [t.reshape(B, nC, HG_CHUNK, H, DK) for t in (q, log_f, k)]
    v = v.reshape(B, nC, HG_CHUNK, H, DV)
    b = jnp.cumsum(log_f, axis=2)
    causal = jnp.tril(jnp.ones((HG_CHUNK, HG_CHUNK), dtype=bool))[:, :, None, None]
    decay = jnp.exp(jnp.where(causal, b[:, :, :, None] - b[:, :, None, :], -jnp.inf))
    scores = jnp.einsum('bntshd,bnshd->bnhts', q[:, :, :, None] * decay, k)
    o_intra = jnp.einsum('bnhts,bnshe->bnthe', scores, v)
    b_last = b[:, :, -1]
    kv = jnp.einsum('bnshd,bnshe->bnhde', k * jnp.exp(b_last[:, :, None] - b), v)

    def step(S, inp):
        dec, kv_c = inp
        return dec[..., None] * S + kv_c, S

    s_final, s_start = lax.scan(step, s0, (jnp.moveaxis(jnp.exp(b_last), 1, 0), jnp.moveaxis(kv, 1, 0)))
    o_inter = jnp.einsum('bnthd,nbhde->bnthe', q * jnp.exp(b), s_start)
    return (o_intra + o_inter).reshape(B, N, H, DV), s_final


def hgrn2_mixer(zq, zf_fwd, zf_bwd, zi, zo, lb, norm_g, s0):
    B, N, _ = zq.shape
    f32 = jnp.float32
    heads = lambda t: t.reshape(B, N, HG_HEADS, -1)
    q = heads(jax.nn.silu(zq.astype(f32)))
    v = heads(zi.astype(f32))

    def forget(zf, lb_d):
        log_f = jnp.logaddexp(jnp.log(lb_d), jnp.log1p(-lb_d) + jax.nn.log_sigmoid(zf.astype(f32)))
        return heads(log_f), heads(-jnp.expm1(log_f))

    lf_f, k_f = forget(zf_fwd, lb[0])
    lf_b, k_b = forget(zf_bwd, lb[1])
    s0 = s0.astype(f32)
    rev = lambda t: jnp.flip(t, axis=1)
    o_f, s_f = hgrn2_chunk_scan(q, lf_f, k_f, v, s0[:, 0])
    o_b, s_b = hgrn2_chunk_scan(rev(q), rev(lf_b), rev(k_b), rev(v), s0[:, 1])
    o = o_f + rev(o_b)
    o = o * lax.rsqrt(jnp.mean(o * o, axis=-1, keepdims=True) + EPS) * norm_g.astype(f32)
    o = o.reshape(B, N, HG_WIDTH) * jax.nn.silu(zo.astype(f32))
    return o.astype(zq.dtype), jnp.stack([s_f, s_b], axis=1)


def context_attention(q, k, v):
    B, N, H, Dh = q.shape
    nb = N // NA_QBLOCK
    qb = jnp.moveaxis(q.reshape(B, nb, NA_QBLOCK, H, Dh), 1, 0)

    def block(qi):
        s = jnp.einsum('bqhd,bkhd->bhqk', qi, k).astype(jnp.float32) * NA_SCALE
        p = jax.nn.softmax(s, axis=-1).astype(v.dtype)
        return jnp.einsum('bhqk,bkhd->bqhd', p, v)

    o = lax.map(block, qb)
    return jnp.moveaxis(o, 0, 1).reshape(B, N, H * Dh)


def neighbourhood_attention(q, k, v, k_ctx, v_ctx, rpb):
    B, N, H, Dh = q.shape
    f32 = jnp.float32
    rows = N // GRID_W
    kh = min(NA_KH, rows)
    nqb = GRID_W // NA_QBW
    r = jnp.arange(rows)
    key_rows = jnp.clip(r - kh // 2, 0, rows - kh)[:, None] + jnp.arange(kh)[None]
    qcol = jnp.arange(GRID_W).reshape(nqb, NA_QBW)
    win_start = jnp.clip(qcol - NA_KW // 2, 0, GRID_W - NA_KW)
    band_start = jnp.clip(jnp.arange(nqb) * NA_QBW - NA_KW // 2, 0, GRID_W - NA_BAND)
    key_cols = band_start[:, None] + jnp.arange(NA_BAND)[None]
    in_win = (key_cols[:, None] >= win_start[..., None]) & (key_cols[:, None] < win_start[..., None] + NA_KW)
    idx = key_rows[:, None, :, None] * GRID_W + key_cols[None, :, None, :]
    dy = key_rows - r[:, None] + NA_KH - 1
    dx = jnp.clip(key_cols[:, None] - qcol[..., None] + NA_KW - 1, 0, 2 * NA_KW - 2)
    bias = rpb.astype(f32)[:, dy[:, None, None, :, None], dx[None, :, :, None, :]]
    bias = jnp.where(in_win[None, None, :, :, None, :], bias, NEG)
    bias = jnp.moveaxis(bias, 1, 0)
    q_rows = jnp.moveaxis(q.reshape(B, rows, nqb, NA_QBW, H, Dh), 1, 0)
    n_loc = kh * NA_BAND

    def row(args):
        q_r, idx_r, bias_r = args
        k_loc = k[:, idx_r]
        v_loc = v[:, idx_r]
        s_loc = jnp.einsum('bjqhd,bjkwhd->bhjqkw', q_r, k_loc).astype(f32) * NA_SCALE + bias_r
        s_ctx = jnp.einsum('bjqhd,bmhd->bhjqm', q_r, k_ctx).astype(f32) * NA_SCALE
        s = jnp.concatenate([s_loc.reshape(B, H, nqb, NA_QBW, n_loc), s_ctx], axis=-1)
        p = jax.nn.softmax(s, axis=-1).astype(v.dtype)
        p_loc = p[..., :n_loc].reshape(B, H, nqb, NA_QBW, kh, NA_BAND)
        o = (jnp.einsum('bhjqkw,bjkwhd->bjqhd', p_loc, v_loc)
             + jnp.einsum('bhjqm,bmhd->bjqhd', p[..., n_loc:], v_ctx))
        return o.reshape(B, GRID_W, H * Dh)

    o = lax.map(row, (q_rows, idx, bias))
    return jnp.moveaxis(o, 0, 1).reshape(B, N, H * Dh)


def centred_dwconv(x, w, b):
    N = x.shape[1]
    left = LRU_CONV // 2
    xp = jnp.pad(x, ((0, 0), (left, LRU_CONV - 1 - left), (0, 0)))
    out = b + w[0] * xp[:, 0:N]
    for j in range(1, LRU_CONV):
        out = out + w[j] * xp[:, j:j + N]
    return out


def rglru_mixer(zx, zg, conv_w, conv_b, w_a, b_a, w_x, b_x, lam, h0):
    B, N, _ = zx.shape
    f32 = jnp.float32
    x = centred_dwconv(zx, conv_w, conv_b).astype(f32)
    xb = x.reshape(B, N, LRU_BLOCKS, LRU_BW)

    def block_linear(w, bias):
        return jnp.einsum('bnki,kij->bnkj', xb, w.astype(f32)).reshape(B, N, LRU_WIDTH) + bias.astype(f32)

    def direction(d, reverse):
        r_gate = jax.nn.sigmoid(block_linear(w_a[d], b_a[d]))
        i_gate = jax.nn.sigmoid(block_linear(w_x[d], b_x[d]))
        log_a = -LRU_C * r_gate * jax.nn.softplus(-lam[d].astype(f32))
        a = jnp.exp(log_a)
        u = jnp.sqrt(-jnp.expm1(2.0 * log_a)) * (i_gate * x)

        def step(h, inp):
            a_t, u_t = inp
            h = a_t * h + u_t
            return h, h

        h_last, hs = lax.scan(step, h0[:, d].astype(f32), (jnp.moveaxis(a, 1, 0), jnp.moveaxis(u, 1, 0)),
                              reverse=reverse)
        return jnp.moveaxis(hs, 0, 1), h_last

    y_f, h_f = direction(0, False)
    y_b, h_b = direction(1, True)
    y = (y_f + y_b) * jax.nn.gelu(zg.astype(f32))
    return y.astype(zx.dtype), jnp.stack([h_f, h_b], axis=1)


def merge_branches(o_a, o_b, o_c, zgate, w_proj_a, w_proj_b, w_proj_c, w_out):
    g = jax.nn.sigmoid(zgate.astype(jnp.float32)).astype(o_a.dtype)
    g_a, g_b, g_c = jnp.split(g, 3, axis=-1)
    m = g_a * (o_a @ w_proj_a) + g_b * (o_b @ w_proj_b) + g_c * (o_c @ w_proj_c)
    return m @ w_out


def trunk_layer(x, mods, lb, norm_g, ffn1_w_up, ffn1_w_down, ffn2_w_up, ffn2_w_down, w_in,
                hgrn_norm_g, na_rpb, lru_conv_w, lru_conv_b, lru_w_a, lru_b_a, lru_w_x, lru_b_x,
                lru_lambda, w_proj_a, w_proj_b, w_proj_c, w_out, hg_state, lru_state, ctx_k, ctx_v):
    B, N, _ = x.shape
    x = ffn_sublayer(x, mods, 0, norm_g[0], ffn1_w_up, ffn1_w_down)
    h = modulated_norm(x, mods, 1, norm_g[1])
    zq, zf_fwd, zf_bwd, zi, zo, nq, nk, nv, lx, lg, zgate = split_projection(h @ w_in)
    o_a, s_hg = hgrn2_mixer(zq, zf_fwd, zf_bwd, zi, zo, lb, hgrn_norm_g, hg_state)
    heads = lambda t: t.reshape(B, N, NA_HEADS, NA_HEAD_DIM)
    q, k, v = heads(nq), heads(nk), heads(nv)
    if ctx_k is None:
        o_b = context_attention(q, k, v)
    else:
        o_b = neighbourhood_attention(q, k, v, ctx_k, ctx_v, na_rpb)
    o_c, s_lru = rglru_mixer(lx, lg, lru_conv_w, lru_conv_b, lru_w_a, lru_b_a, lru_w_x, lru_b_x,
                             lru_lambda, lru_state)
    x = x + gate_of(mods, 1) * merge_branches(o_a, o_b, o_c, zgate, w_proj_a, w_proj_b, w_proj_c, w_out)
    x = ffn_sublayer(x, mods, 2, norm_g[2], ffn2_w_up, ffn2_w_down)
    return x, k, v, s_hg, s_lru


def setup_inputs(seed: int = 0) -> dict:
    key = jax.random.key(seed)
    ks = iter(jax.random.split(key, 40))
    f32 = jnp.float32
    nrm = lambda shape, scale: jax.random.normal(next(ks), shape, f32) * scale
    a0 = jax.random.uniform(next(ks), (DEPTH, 2, LRU_WIDTH), f32, minval=0.9, maxval=0.999)
    return {
        'x_prompt': nrm((BATCH, SEQ, D_MODEL), 1.0),
        'x_sample': nrm((DEC_BATCH, DEC_SEQ, D_MODEL), 1.0),
        'cache_na_k': nrm((DEC_BATCH, DEPTH, PAST_LEN, NA_HEADS, NA_HEAD_DIM), 1.0),
        'cache_na_v': nrm((DEC_BATCH, DEPTH, PAST_LEN, NA_HEADS, NA_HEAD_DIM), 1.0),
        'state_hgrn': nrm((DEC_BATCH, DEPTH, 2, HG_HEADS, HG_DK, HG_DV), 1.0),
        'state_lru': nrm((DEC_BATCH, DEPTH, 2, LRU_WIDTH), 0.5),
        'c': nrm((DEC_BATCH, D_MODEL), 1.0),
        'c_ctx': nrm((D_MODEL,), 1.0),
        'mod_w': nrm((DEPTH, D_MODEL, N_MOD * D_MODEL), 0.5 * D_MODEL ** -0.5),
        'mod_b': nrm((DEPTH, N_MOD * D_MODEL), 0.02),
        'norm_g': 1.0 + nrm((DEPTH, 3, D_MODEL), 0.1),
        'ffn1_w_up': nrm((DEPTH, D_MODEL, 2 * D_FF), D_MODEL ** -0.5),
        'ffn1_w_down': nrm((DEPTH, D_FF, D_MODEL), D_FF ** -0.5),
        'ffn2_w_up': nrm((DEPTH, D_MODEL, 2 * D_FF), D_MODEL ** -0.5),
        'ffn2_w_down': nrm((DEPTH, D_FF, D_MODEL), D_FF ** -0.5),
        'w_in': nrm((DEPTH, D_MODEL, D_IN), D_MODEL ** -0.5),
        'hgrn_lb_logits': nrm((DEPTH, 2, HG_WIDTH), 1.0),
        'hgrn_norm_g': 1.0 + nrm((DEPTH, HG_HEADS, HG_DV), 0.1),
        'na_rpb': nrm((DEPTH, NA_HEADS, 2 * NA_KH - 1, 2 * NA_KW - 1), 0.1),
        'lru_conv_w': nrm((DEPTH, LRU_CONV, LRU_WIDTH), LRU_CONV ** -0.5),
        'lru_conv_b': nrm((DEPTH, LRU_WIDTH), 0.02),
        'lru_w_a': nrm((DEPTH, 2, LRU_BLOCKS, LRU_BW, LRU_BW), LRU_BW ** -0.5),
        'lru_b_a': nrm((DEPTH, 2, LRU_WIDTH), 0.1),
        'lru_w_x': nrm((DEPTH, 2, LRU_BLOCKS, LRU_BW, LRU_BW), LRU_BW ** -0.5),
        'lru_b_x': nrm((DEPTH, 2, LRU_WIDTH), 0.1),
        'lru_lambda': jnp.log(a0) - jnp.log1p(-a0),
        'w_proj_a': nrm((DEPTH, HG_WIDTH, D_MODEL), HG_WIDTH ** -0.5),
        'w_proj_b': nrm((DEPTH, NA_WIDTH, D_MODEL), NA_WIDTH ** -0.5),
        'w_proj_c': nrm((DEPTH, LRU_WIDTH, D_MODEL), LRU_WIDTH ** -0.5),
        'w_out': nrm((DEPTH, D_MODEL, D_MODEL), D_MODEL ** -0.5),
        'final_norm_g': 1.0 + nrm((D_MODEL,), 0.1),
    }


def reference(x_prompt, x_sample, cache_na_k, cache_na_v, state_hgrn, state_lru, c, c_ctx,
              mod_w, mod_b, norm_g, ffn1_w_up, ffn1_w_down, ffn2_w_up, ffn2_w_down, w_in,
              hgrn_lb_logits, hgrn_norm_g, na_rpb, lru_conv_w, lru_conv_b, lru_w_a, lru_b_a,
              lru_w_x, lru_b_x, lru_lambda, w_proj_a, w_proj_b, w_proj_c, w_out, final_norm_g):
    lb_all = hgrn_lower_bounds(hgrn_lb_logits)

    def weights(l):
        return (norm_g[l], ffn1_w_up[l], ffn1_w_down[l], ffn2_w_up[l], ffn2_w_down[l], w_in[l],
                hgrn_norm_g[l], na_rpb[l], lru_conv_w[l], lru_conv_b[l], lru_w_a[l], lru_b_a[l],
                lru_w_x[l], lru_b_x[l], lru_lambda[l], w_proj_a[l], w_proj_b[l], w_proj_c[l], w_out[l])

    b_ctx = x_prompt.shape[0]
    zero_hg = jnp.zeros((b_ctx, 2, HG_HEADS, HG_DK, HG_DV), jnp.float32)
    zero_lru = jnp.zeros((b_ctx, 2, LRU_WIDTH), jnp.float32)
    xc = x_prompt
    ks, vs, hgs, lrus = [], [], [], []
    for l in range(DEPTH):
        mods = adaln_params(c_ctx[None], mod_w[l], mod_b[l])
        xc, k_l, v_l, s_hg, s_lru = trunk_layer(xc, mods, lb_all[l], *weights(l), zero_hg, zero_lru, None, None)
        ks.append(k_l)
        vs.append(v_l)
        hgs.append(s_hg)
        lrus.append(s_lru)
    y_prompt = rmsnorm(xc, final_norm_g)

    xs = x_sample
    for l in range(DEPTH):
        mods = adaln_params(c, mod_w[l], mod_b[l])
        xs = trunk_layer(xs, mods, lb_all[l], *weights(l), state_hgrn[:, l], state_lru[:, l],
                         cache_na_k[:, l], cache_na_v[:, l])[0]
    y_sample = rmsnorm(xs, final_norm_g)

    new_cache_na_k = jnp.stack(ks, axis=1)
    new_cache_na_v = jnp.stack(vs, axis=1)
    new_state_hgrn = jnp.stack(hgs, axis=1)
    new_state_lru = jnp.stack(lrus, axis=1)
    return (y_prompt, y_sample, new_cache_na_k, new_cache_na_v, new_state_hgrn, new_state_lru)
```

```python
import numpy as np
from contextlib import ExitStack
import concourse.bass as bass
import concourse.mybir as mybir
from concourse.bass_utils import run_bass_kernel_spmd

F32 = mybir.dt.float32
BF16 = mybir.dt.bfloat16
AF = mybir.ActivationFunctionType
ALU = mybir.AluOpType

L = 2
D = 2048
NCH = 16
T = 1024
NSEG = 4
SEG = 256
DFF = 5632
DIN = 12800
HC = 32
NEGM = -30000.0
EPS = 1e-6
C_ZQ, C_ZFF, C_ZFB, C_ZI, C_ZO = 0, 512, 1024, 1536, 2048
C_NQ, C_NK, C_NV = 2560, 3584, 4608
C_LX, C_LG = 5632, 6144
C_GATE = 6656
QK_TILES = [(kb, qh) for qh in range(2) for kb in range(8) if (qh == 0 and kb <= 5) or (qh == 1 and kb >= 2)]
DEBUG = False
STAGE = 99
HSTOP = 99


class Tracker:
    def __init__(self, nc, es):
        self.nc = nc
        self.es = es
        self.eng = {'pe': nc.tensor, 'act': nc.scalar, 'dve': nc.vector, 'pool': nc.gpsimd, 'sp': nc.sync}
        self.sem = {}
        self.cnt = {}
        for e in ('pe', 'act', 'dve', 'pool'):
            self.sem[e] = es.enter_context(nc.semaphore("c_" + e))
            self.cnt[e] = 0
        self.waited = {e: {} for e in self.eng}
        self.bufs = {}
        self.dma_sems = {}
        self.all_dma_tokens = {}

    @staticmethod
    def _fix(reads, writes):
        r2, w2 = [], list(writes)
        for k in reads:
            if k == 'PT' or (isinstance(k, tuple) and k[0] == 'P'):
                w2.append(k)
            else:
                r2.append(k)
        return r2, w2

    def _deps(self, reads, writes):
        deps = []
        for r in reads:
            b = self.bufs.get(r)
            if b and b['w'] is not None:
                deps.append(b['w'])
        for w in writes:
            b = self.bufs.get(w)
            if b:
                if b['w'] is not None:
                    deps.append(b['w'])
                deps.extend(b['r'])
        return deps

    def _wait(self, engine, deps, sync_same=False, same_ok=False):
        best = {}
        for (sem, val, src, key) in deps:
            if src == engine and (engine == 'pe' or same_ok):
                continue
            if self.waited[engine].get(key, 0) >= val:
                continue
            if key not in best or best[key][1] < val:
                best[key] = (sem, val)
        for key, (sem, val) in best.items():
            self.eng[engine].wait_ge(sem, val)
            self.waited[engine][key] = val

    def _record(self, token, reads, writes):
        for r in reads:
            b = self.bufs.setdefault(r, {'w': None, 'r': []})
            b['r'].append(token)
            if len(b['r']) > 24:
                latest = {}
                for t in b['r']:
                    if t[3] not in latest or latest[t[3]][1] < t[1]:
                        latest[t[3]] = t
                b['r'] = list(latest.values())
        for w in writes:
            self.bufs[w] = {'w': token, 'r': []}

    def op(self, engine, fn, reads=(), writes=(), sync_same=False, same_ok=False):
        reads, writes = self._fix(reads, writes)
        self._wait(engine, self._deps(reads, writes), sync_same, same_ok)
        inst = fn(self.eng[engine])
        self.cnt[engine] += 1
        inst.then_inc(self.sem[engine], 1)
        token = (self.sem[engine], self.cnt[engine], engine, engine)
        self._record(token, reads, writes)
        return token

    def mm(self, fns, reads=(), writes=()):
        reads, writes = self._fix(reads, writes)
        self._wait('pe', self._deps(reads, writes))
        inst = None
        for fn in fns:
            inst = fn(self.eng['pe'])
        self.cnt['pe'] += 1
        inst.then_inc(self.sem['pe'], 1)
        token = (self.sem['pe'], self.cnt['pe'], 'pe', 'pe')
        self._record(token, reads, writes)
        return token

    def dma(self, queue, slot, out, in_, reads=(), writes=()):
        if slot not in self.dma_sems:
            self.dma_sems[slot] = [self.es.enter_context(self.nc.semaphore("d_" + slot)), 0]
        ds = self.dma_sems[slot]
        self._wait(queue, self._deps(reads, writes), sync_same=True)
        inst = self.eng[queue].dma_start(out=out, in_=in_)
        ds[1] += 16
        inst.then_inc(ds[0], 16)
        token = (ds[0], ds[1], 'dma', 'd_' + slot)
        self._record(token, reads, writes)
        self.all_dma_tokens['d_' + slot] = token
        return token

    def barrier(self):
        toks = [(self.sem[e], self.cnt[e], e, e) for e in ('pe', 'act', 'dve', 'pool') if self.cnt[e] > 0]
        toks += list(self.all_dma_tokens.values())
        for e in ('pe', 'act', 'dve', 'pool', 'sp'):
            self._wait(e, toks)

    def final_wait(self):
        self._wait('sp', list(self.all_dma_tokens.values()) +
                   [(self.sem[e], self.cnt[e], e, e) for e in ('pe', 'act', 'dve', 'pool') if self.cnt[e] > 0])


def build_nc():
    nc = bass.Bass("TRN2", target_bir_lowering=False)

    def din(name, shape):
        return nc.dram_tensor(name, list(shape), F32, kind="ExternalInput").ap()

    def dout(name, shape):
        return nc.dram_tensor(name, list(shape), F32, kind="ExternalOutput").ap()

    xT_d = din("xT", [D, T])
    cond_d = din("condT", [128, NCH])
    modw_d = din("mod_w", [L, D, 9 * D])
    modb_d = din("mod_bT", [128, L * 144])
    normg_d = din("normgT", [128, L * 3 * NCH])
    fng_d = din("fngT", [128, NCH])
    wup_d = [din("ffn1_w_up", [L, D, 2 * DFF]), din("ffn2_w_up", [L, D, 2 * DFF])]
    wdn_d = [din("ffn1_w_down", [L, DFF, D]), din("ffn2_w_down", [L, DFF, D])]
    win_d = din("w_in", [L, D, DIN])
    wpa_d = din("w_proj_a", [L, 512, D])
    wpb_d = din("w_proj_b", [L, 1024, D])
    wpc_d = din("w_proj_c", [L, 512, D])
    wout_d = din("w_out", [L, D, D])
    small_d = din("small", [128, 8])
    ident_d = din("ident", [128, 128])
    hmask_d = din("hmask", [2, 128, 128])
    rowmask_d = din("rowmask", [128, 4])
    hglog_d = din("hglogT", [128, L * 8])
    hgng_d = din("hgngT", [128, L * 4])
    sinit_d = din("s_init", [L, 2, 4, 128, NSEG * 128])
    kctx_d = din("kctxT", [L, 8, 128, 512])
    vctx_d = din("vctx", [L, 8, 128, 4 * 128])
    bias_d = din("na_bias", [L, 8, len(QK_TILES), 128, 512])
    lruv_d = din("lru_vecT", [128, L * 48])
    lrubd_d = din("lru_bd", [L, 2, 2, 4, 128, 128])
    h0_d = din("lru_h0T", [128, L * 2 * NSEG * 4])

    yT_o = dout("yT", [D, T])
    kT_o = dout("kT_out", [L, 1024, T])
    v_o = dout("v_out", [L, T, 1024])
    hg_o = dout("hg_out", [L, 2, 4, 128, NSEG * 128])
    lru_o = dout("lru_out", [128, L * 2 * NSEG * 4])
    dbg_o = dout("dbg", [8, D, T]) if DEBUG else None

    with ExitStack() as es:
        tk = Tracker(nc, es)

        uid = [0]

        def sb(name, shape, dt=F32, stack=es):
            uid[0] += 1
            return stack.enter_context(nc.sbuf_tensor("s%d_%s" % (uid[0], name), list(shape), dt))

        xT = sb("xT", [128, NCH, T])
        hT = sb("hT", [128, NCH, T], BF16)
        ones_bf = sb("ones_bf", [128, 128], BF16)
        ident_bf = sb("ident_bf", [128, 128], BF16)
        hmask = sb("hmask", [128, 2, 128])
        rowmask = sb("rowmask", [128, 4])
        modT = sb("modT", [128, L * 144])
        modb = sb("modb", [128, L * 144])
        normg = sb("normg", [128, L * 3 * NCH])
        fng = sb("fng", [128, NCH])
        Amod = sb("Amod", [128, L * 3 * NCH])
        gmod = sb("gmod", [128, L * 3 * NCH])
        small = sb("small", [128, 8])
        epst = sb("epst", [128, 1])
        condT = sb("condT", [128, NCH])
        cond_bf = sb("cond_bf", [128, NCH], BF16)
        hglog = sb("hglog", [128, L * 8])
        hgng = sb("hgng", [128, L * 4])
        lbt = sb("lbt", [128, L * 8])
        omlb = sb("omlb", [128, L * 8])
        lruv = sb("lruv", [128, L * 48])
        nsp = sb("nsp", [128, L * 16])
        h0t = sb("h0t", [128, L * 2 * NSEG * 4])
        lrufin = sb("lrufin", [128, L * 2 * NSEG * 4])
        rstd_box = [None]
        P = [es.enter_context(nc.psum_tensor("P%d" % i, [128, 512], F32)) for i in range(8)]
        PT = P[7][:, :].bitcast(BF16)

        carry = small[:, 0:1]
        ctxb = small[:, 1:2]

        tk.op('dve', lambda e: e.memset(ones_bf[:], 1.0), writes=['ones'])
        tk.op('dve', lambda e: e.memset(epst[:], EPS), writes=['eps'])
        tk.dma('pool', 'c0', ident_bf[:], ident_d, writes=['ident'])
        tk.dma('sp', 'c1', hmask[:], hmask_d.rearrange("a s t -> s a t"), writes=['hmask'])
        tk.dma('sp', 'c1', rowmask[:], rowmask_d, writes=['rowmask'])
        tk.dma('sp', 'c1', modb[:], modb_d, writes=['modb'])
        tk.dma('sp', 'c1', normg[:], normg_d, writes=['normg'])
        tk.dma('sp', 'c1', fng[:], fng_d, writes=['fng'])
        tk.dma('sp', 'c1', small[:], small_d, writes=['small'])
        tk.dma('sp', 'c1', condT[:], cond_d, writes=['condT'])
        tk.dma('sp', 'c1', hglog[:], hglog_d, writes=['hglog'])
        tk.dma('sp', 'c1', hgng[:], hgng_d, writes=['hgng'])
        tk.dma('sp', 'c1', lruv[:], lruv_d, writes=['lruv'])
        tk.dma('sp', 'c1', h0t[:], h0_d, writes=['h0t'])
        for c in range(NCH):
            tk.dma('sp', 'c2', xT[:, c, :], xT_d[c * 128:(c + 1) * 128, :], writes=[('x', c, 0), ('x', c, 1)])

        tk.barrier()
        tk.op('dve', lambda e: e.memset(lbt[:, 0:8], 0.0), writes=['lbt0'])
        tk.op('dve', lambda e: e.tensor_tensor(out=lbt[:, 8:16], in0=hglog[:, 8:16], in1=hglog[:, 0:8], op=ALU.subtract),
              reads=['hglog'], writes=['lbt1'])
        tk.op('act', lambda e: e.activation(out=lbt[:, 8:16], in_=lbt[:, 8:16], func=AF.Sigmoid), reads=['lbt1'], writes=['lbt1'])
        tk.op('dve', lambda e: e.tensor_scalar(out=omlb[:], in0=lbt[:], scalar1=-1.0, scalar2=1.0, op0=ALU.mult, op1=ALU.add),
              reads=['lbt0', 'lbt1'], writes=['omlb'])
        for l in range(L):
            lam = lruv[:, l * 48 + 36: l * 48 + 44]
            tk.op('act', lambda e: e.activation(out=nsp[:, l * 16: l * 16 + 8], in_=lam, func=AF.Exp, scale=-1.0),
                  reads=['lruv'], writes=[('nsp', l)])
            tk.op('act', lambda e: e.activation(out=nsp[:, l * 16: l * 16 + 8], in_=nsp[:, l * 16: l * 16 + 8], func=AF.Ln, bias=1.0),
                  reads=[('nsp', l)], writes=[('nsp', l)], sync_same=True)
            tk.op('dve', lambda e: e.tensor_scalar(out=nsp[:, l * 16 + 8: l * 16 + 16], in0=nsp[:, l * 16: l * 16 + 8],
                                                   scalar1=-16.0, scalar2=None, op0=ALU.mult),
                  reads=[('nsp', l)], writes=[('nsp2', l)])
            tk.op('dve', lambda e: e.tensor_scalar(out=nsp[:, l * 16: l * 16 + 8], in0=nsp[:, l * 16: l * 16 + 8],
                                                   scalar1=-8.0, scalar2=None, op0=ALU.mult),
                  reads=[('nsp', l), ('nsp2', l)], writes=[('nsp', l)])

        tk.op('act', lambda e: e.activation(out=cond_bf[:], in_=condT[:], func=AF.Silu), reads=['condT'], writes=['cond_bf'])
        MODQ = [(l_, pn_) for l_ in range(L) for pn_ in range(36)]
        mod_next = [0]
        mod_cnt = [0]

        def mod_issue(mw_list):
            l_, pn_ = MODQ[mod_next[0]]
            mod_next[0] += 1
            s_ = mod_cnt[0] % len(mw_list)
            mod_cnt[0] += 1
            wv = modw_d[l_].rearrange("(k p) f -> p k f", p=128)
            tk.dma('pool', 'mw%d' % s_, mw_list[s_][:], wv[:, :, pn_ * 512:(pn_ + 1) * 512], writes=[('mw', s_)])
            return (l_, pn_, s_, mw_list[s_])

        def mod_compute(hd_):
            l_, pn_, s_, mwt = hd_
            base = l_ * 144 + pn_ * 4
            fns = []
            for n in range(4):
                for k in range(NCH):
                    fns.append(lambda e, n=n, k=k: e.matmul(
                        P[7][:, base + n:base + n + 1], lhsT=mwt[:, k, n * 128:(n + 1) * 128], rhs=cond_bf[:, k:k + 1],
                        start=(k == 0), stop=(k == NCH - 1)))
            tk.mm(fns, reads=[('mw', s_), 'cond_bf'], writes=[('P', 7)])
            tk.op('dve', lambda e: e.tensor_tensor(out=modT[:, base:base + 4], in0=P[7][:, base:base + 4],
                                                   in1=modb[:, base:base + 4], op=ALU.add),
                  reads=[('P', 7), 'modb'], writes=[('modT', l_)])
            if pn_ % 12 in (7, 11):
                s3 = pn_ // 12
                o = (l_ * 3 + s3) * NCH
                sc = modT[:, l_ * 144 + (3 * s3 + 1) * NCH: l_ * 144 + (3 * s3 + 2) * NCH]
                gt = modT[:, l_ * 144 + (3 * s3 + 2) * NCH: l_ * 144 + (3 * s3 + 3) * NCH]
                if pn_ % 12 == 7:
                    tk.op('dve', lambda e: e.scalar_tensor_tensor(out=Amod[:, o:o + NCH], in0=sc, scalar=1.0,
                                                                  in1=normg[:, o:o + NCH], op0=ALU.add, op1=ALU.mult),
                          reads=[('modT', l_), 'normg'], writes=[('Amod', l_, s3)])
                else:
                    tk.op('dve', lambda e: e.tensor_scalar(out=gmod[:, o:o + NCH], in0=gt,
                                                           scalar1=(1.0 if s3 == 1 else 0.5), scalar2=None, op0=ALU.mult),
                          reads=[('modT', l_)], writes=[('gmod', l_, s3)])

        with ExitStack() as ms:
            mw = [sb("mw%d" % i, [128, NCH, 512], BF16, ms) for i in range(2)]
            hprev = mod_issue(mw)
            for i in range(8):
                hnext = mod_issue(mw) if i + 1 < 8 else None
                mod_compute(hprev)
                hprev = hnext
            tk.barrier()

        def shift_ap(l, s3, c):
            o = l * 144 + (3 * s3) * NCH + c
            return modT[:, o:o + 1]

        def A_ap(l, s3, c):
            o = (l * 3 + s3) * NCH + c
            return Amod[:, o:o + 1]

        def g_ap(l, s3, c):
            o = (l * 3 + s3) * NCH + c
            return gmod[:, o:o + 1]

        HS = [slice(0, 512), slice(512, 1024)]

        def sumsq_rstd(nfeat_scale):
            rstd = rstd_box[0]
            for h in range(2):
                tk.op('act', lambda e, h=h: e.activation(out=hT[:, :, HS[h]], in_=xT[:, :, HS[h]], func=AF.Square),
                      reads=[('x', c, h) for c in range(NCH)], writes=[('h', c, h) for c in range(NCH)])
            for h in range(2):
                fns = [lambda e, k=k, h=h: e.matmul(P[h][:, :], lhsT=ones_bf[:, :], rhs=hT[:, k, HS[h]],
                                                    start=(k == 0), stop=(k == NCH - 1)) for k in range(NCH)]
                tk.mm(fns, reads=[('h', c, h) for c in range(NCH)] + ['ones'], writes=[('P', h)])
                tk.op('act', lambda e, h=h: e.activation(out=rstd[:, HS[h]], in_=P[h][:, :], func=AF.Ln,
                                                         bias=epst[:, 0:1], scale=nfeat_scale),
                      reads=[('P', h), 'eps'], writes=[('rstd', h)])
                tk.op('act', lambda e, h=h: e.activation(out=rstd[:, HS[h]], in_=rstd[:, HS[h]], func=AF.Exp, scale=-0.5),
                      reads=[('rstd', h)], writes=[('rstd', h)])

        def modnorm(l, s3):
          with ExitStack() as ns_:
            rstd = sb("rstd", [128, T], F32, ns_)
            rstd_box[0] = rstd
            sumsq_rstd(1.0 / D)
            tmpn = [sb("tmpn0", [128, 512], F32, ns_), sb("tmpn1", [128, 512], F32, ns_)]
            i = 0
            for h in range(2):
                for c in range(NCH):
                    tb = i % 2
                    i += 1
                    tk.op('dve', lambda e, c=c, h=h, tb=tb: e.scalar_tensor_tensor(
                        out=tmpn[tb][:, :], in0=xT[:, c, HS[h]], scalar=A_ap(l, s3, c), in1=rstd[:, HS[h]],
                        op0=ALU.mult, op1=ALU.mult),
                        reads=[('x', c, h), ('rstd', h), ('Amod', l, s3)], writes=[('tmpn', tb)])
                    tk.op('act', lambda e, c=c, h=h, tb=tb: e.activation(
                        out=hT[:, c, HS[h]], in_=tmpn[tb][:, :], func=AF.Identity, bias=shift_ap(l, s3, c), scale=1.0),
                        reads=[('tmpn', tb), ('modT', l)], writes=[('h', c, h)])
            tk.barrier()

        def dbg_dump(idx):
            if DEBUG:
                for c in range(NCH):
                    tk.dma('sp', 'dbg', dbg_o[idx, c * 128:(c + 1) * 128, :], xT[:, c, :],
                           reads=[('x', c, 0), ('x', c, 1)])

        def ffn(l, s3, wup, wdn, nmods=0):
            modnorm(l, s3)
            NG = 22
            with ExitStack() as fs:
                wa = [sb("wa%d" % i, [128, NCH, 256], BF16, fs) for i in range(2)]
                wu = [sb("wu%d" % i, [128, NCH, 256], BF16, fs) for i in range(2)]
                wd = [sb("wd%d" % i, [128, 2, D], BF16, fs) for i in range(3)]
                hid = [sb("hid%d" % i, [128, 2, T], BF16, fs) for i in range(2)]
                sa = [sb("sa%d" % i, [128, 512], F32, fs) for i in range(2)]
                mwf = [sb("mwf%d" % i, [128, NCH, 512], BF16, fs) for i in range(2)] if nmods > 0 else None
                mods_left = [nmods]
                wupv = wup[l].rearrange("(k p) f -> p k f", p=128)

                def load(g):
                    s2, s3_ = g % 2, g % 3
                    tk.dma('pool', 'wa%d' % s2, wa[s2][:], wupv[:, :, g * 256:(g + 1) * 256], writes=[('wa', s2)])
                    tk.dma('pool', 'wu%d' % s2, wu[s2][:], wupv[:, :, DFF + g * 256: DFF + (g + 1) * 256], writes=[('wu', s2)])
                    tk.dma('pool', 'wd%d' % s3_, wd[s3_][:],
                           wdn[l][g * 256:(g + 1) * 256, :].rearrange("(k p) d -> p k d", p=128), writes=[('wd', s3_)])

                cnt = [0, 0]

                def up(g):
                    s2 = g % 2
                    for h in range(2):
                        for j in range(2):
                            pa = cnt[0] % 2
                            cnt[0] += 1
                            tk.mm([lambda e, k=k, j=j, h=h, pa=pa: e.matmul(
                                P[pa][:, :], lhsT=wa[s2][:, k, j * 128:(j + 1) * 128], rhs=hT[:, k, HS[h]],
                                start=(k == 0), stop=(k == NCH - 1)) for k in range(NCH)],
                                reads=[('wa', s2)] + [('h', c, h) for c in range(NCH)], writes=[('P', pa)])
                            tk.mm([lambda e, k=k, j=j, h=h, pa=pa: e.matmul(
                                P[2 + pa][:, :], lhsT=wu[s2][:, k, j * 128:(j + 1) * 128], rhs=hT[:, k, HS[h]],
                                start=(k == 0), stop=(k == NCH - 1)) for k in range(NCH)],
                                reads=[('wu', s2)] + [('h', c, h) for c in range(NCH)], writes=[('P', 2 + pa)])
                            tk.op('act', lambda e, pa=pa: e.activation(out=sa[pa][:, :], in_=P[pa][:, :], func=AF.Silu),
                                  reads=[('P', pa)], writes=[('sa', pa)])
                            tk.op('dve', lambda e, pa=pa, j=j, h=h: e.tensor_tensor(
                                out=hid[s2][:, j, HS[h]], in0=sa[pa][:, :], in1=P[2 + pa][:, :], op=ALU.mult),
                                reads=[('sa', pa), ('P', 2 + pa)], writes=[('hid', s2, j, h)])

                def down(g):
                    s2, s3_ = g % 2, g % 3
                    for h in range(2):
                        for dc in range(NCH):
                            pd = 4 + cnt[1] % 3
                            cnt[1] += 1
                            tk.mm([lambda e, k=k, dc=dc, h=h, pd=pd: e.matmul(
                                P[pd][:, :], lhsT=wd[s3_][:, k, dc * 128:(dc + 1) * 128], rhs=hid[s2][:, k, HS[h]],
                                start=(k == 0), stop=(k == 1)) for k in range(2)],
                                reads=[('wd', s3_), ('hid', s2, 0, h), ('hid', s2, 1, h)], writes=[('P', pd)])
                            tk.op('dve', lambda e, dc=dc, h=h, pd=pd: e.scalar_tensor_tensor(
                                out=xT[:, dc, HS[h]], in0=P[pd][:, :], scalar=g_ap(l, s3, dc), in1=xT[:, dc, HS[h]],
                                op0=ALU.mult, op1=ALU.add),
                                reads=[('P', pd), ('x', dc, h), ('gmod', l, s3)], writes=[('x', dc, h)])

                load(0)
                for g in range(NG + 1):
                    if g + 1 < NG:
                        load(g + 1)
                    mh = []
                    if g < NG and mods_left[0] > 0:
                        kq = min(2, -(-mods_left[0] // (NG - g)))
                        for _ in range(kq):
                            mh.append(mod_issue(mwf))
                        mods_left[0] -= kq
                    if g < NG:
                        up(g)
                    for hd_ in mh:
                        mod_compute(hd_)
                    if g >= 1:
                        down(g - 1)
                assert mods_left[0] == 0
                tk.barrier()

        def load_thin(wt, slotname, key, src_cols):
            tk.dma('pool', slotname, wt[:], src_cols, writes=[key])

        def mixer(l):
            modnorm(l, 1)
            winv = win_d[l].rearrange("(k p) f -> p k f", p=128)
            with ExitStack() as mx:
                oabc = sb("oabc", [128, 16, T], BF16, mx)
                class Panels:
                    def __init__(self, name, n, width, stack):
                        self.t = [sb(name + str(i), [128, NCH, width], BF16, stack) for i in range(n)]
                        self.n, self.c, self.name = n, 0, name

                    def load(self, col0, ncols):
                        s_ = self.c % self.n
                        self.c += 1
                        tk.dma('pool', '%s%d' % (self.name, s_), self.t[s_][:, :, 0:ncols], winv[:, :, col0:col0 + ncols],
                               writes=[(self.name, s_)])
                        return s_

                    def feat(self, s_, coff, h, pb):
                        w = self.t[s_]
                        tk.mm([lambda e, k=k: e.matmul(P[pb][:, :], lhsT=w[:, k, coff:coff + 128], rhs=hT[:, k, HS[h]],
                                                       start=(k == 0), stop=(k == NCH - 1)) for k in range(NCH)],
                              reads=[(self.name, s_)] + [('h', c, h) for c in range(NCH)], writes=[('P', pb)])

                    def tok(self, s_, coff, blk4, pb):
                        w = self.t[s_]
                        fns = []
                        for j in range(4):
                            b_ = blk4 * 4 + j
                            for k in range(NCH):
                                fns.append(lambda e, k=k, j=j, b_=b_: e.matmul(
                                    P[pb][:, j * 128:(j + 1) * 128], lhsT=hT[:, k, b_ * 128:(b_ + 1) * 128], rhs=w[:, k, coff:coff + 128],
                                    start=(k == 0), stop=(k == NCH - 1)))
                        tk.mm(fns, reads=[(self.name, s_)] + [('h', c, blk4) for c in range(NCH)], writes=[('P', pb)])

                with ExitStack() as hs:
                  if STAGE >= 2:
                      hp = Panels("hp", 2, 256, hs)
                      q = sb("hq", [128, T], BF16, hs)
                      vtok = sb("hvtok", [128, 5, 8, 128], BF16, hs)
                      ft = sb("hf", [128, T], F32, hs)
                      bt_ = sb("hb", [128, T], F32, hs)
                      ex = sb("hex", [128, T], F32, hs)
                      Sbf = [sb("hSbf%d" % i, [128, 128], BF16, hs) for i in range(2)]
                      qtb = sb("hqtb", [128, T], BF16, hs)
                      ktb = sb("hktb", [128, T], BF16, hs)
                      khT = sb("hkhT", [128, T], BF16, hs)
                      khtok = sb("hkhtok", [128, 8, 128], BF16, hs)
                      atm = sb("hatm", [128, 8, 128], BF16, hs)
                      oacc = sb("hoacc", [128, T], F32, hs)
                      dec = sb("hdec", [128, T // HC], F32, hs)
                      cmask = sb("hcm0", [128, T], F32, hs)
                      S = [sb("hS%d" % i, [128, 128], F32, hs) for i in range(3)]
                      sini = sb("hsini", [128, NSEG * 128], F32, hs)
                      sstage = sb("hsstage", [128, NSEG * 128], F32, hs)
                      tk.op('dve', lambda e: e.memset(cmask[:], 1.0), writes=['cm0'])
                      tk.op('dve', lambda e: e.memset(cmask[:, 0:T:HC], 0.0), writes=['cm0'], sync_same=True)
                      PIK = []

                      for hh in range(4 if HSTOP >= 99 else 1):
                          sA = hp.load(hh * 640, 256)
                          sB = hp.load(hh * 640 + 256, 256)
                          for h in range(2):
                              hp.feat(sA, 0, h, h)
                              tk.op('act', lambda e, h=h: e.activation(out=q[:, HS[h]], in_=P[h][:, :], func=AF.Silu),
                                    reads=[('P', h)], writes=[('hq', h)])
                          if HSTOP <= 1:
                              continue
                          for b4 in range(2):
                              hp.tok(sA, 128, b4, 4 + b4)
                              src = P[4 + b4][:, :].rearrange("p (j e) -> p j e", e=128)
                              tk.op('act', lambda e, b4=b4, src=src: e.activation(
                                  out=vtok[:, 0, b4 * 4:(b4 + 1) * 4, :], in_=src, func=AF.Copy),
                                  reads=[('P', 4 + b4)], writes=[('vtok', 0, b4)])
                              for m in range(4):
                                  tk.op('dve', lambda e, b4=b4, m=m, src=src: e.tensor_scalar(
                                      out=vtok[:, 1 + m, b4 * 4:(b4 + 1) * 4, :], in0=src, scalar1=rowmask[:, m:m + 1],
                                      scalar2=None, op0=ALU.mult),
                                      reads=[('P', 4 + b4), 'rowmask'], writes=[('vtok', 1 + m, b4)])
                          if HSTOP <= 2:
                              continue
                          for d in range(2 if HSTOP >= 99 else 1):
                              li = l * 8 + d * 4 + hh
                              if d == 0:
                                  sC = hp.load(hh * 640 + 512, 128)
                              for h in range(2):
                                  hp.feat(sB, 128 * d, h, h)
                                  tk.op('act', lambda e, h=h: e.activation(out=ft[:, HS[h]], in_=P[h][:, :], func=AF.Sigmoid),
                                        reads=[('P', h)], writes=['hf'])
                              tk.op('dve', lambda e, li=li: e.tensor_scalar(out=ft[:, :], in0=ft[:, :], scalar1=omlb[:, li:li + 1],
                                                                            scalar2=lbt[:, li:li + 1], op0=ALU.mult, op1=ALU.add),
                                    reads=['hf', 'omlb', 'lbt0', 'lbt1'], writes=['hf'])
                              tk.op('act', lambda e: e.activation(out=ex[:, :], in_=ft[:, :], func=AF.Ln), reads=['hf'], writes=['hex'])
                              tk.op('dve', lambda e: e.tensor_scalar(out=ft[:, :], in0=ft[:, :], scalar1=-1.0, scalar2=1.0,
                                                                     op0=ALU.mult, op1=ALU.add), reads=['hf'], writes=['hf'])
                              if d == 0:
                                  tk.op('dve', lambda e: e.tensor_tensor_scan(out=bt_[:, :], data0=cmask[:, :], data1=ex[:, :],
                                                                              initial=0.0, op0=ALU.mult, op1=ALU.add),
                                        reads=['hex', 'cm0'], writes=['hb'])
                                  blast = bt_[:, :].rearrange("p (c i) -> p c i", i=HC)[:, :, HC - 1:HC]
                              else:
                                  tk.op('dve', lambda e: e.tensor_tensor_scan(out=bt_[:, ::-1], data0=cmask[:, :], data1=ex[:, ::-1],
                                                                              initial=0.0, op0=ALU.mult, op1=ALU.add),
                                        reads=['hex', 'cm0'], writes=['hb'])
                                  blast = bt_[:, :].rearrange("p (c i) -> p c i", i=HC)[:, :, 0:1]
                              tk.op('act', lambda e: e.activation(out=ex[:, :], in_=bt_[:, :], func=AF.Exp), reads=['hb'], writes=['hex'])
                              tk.op('dve', lambda e: e.tensor_tensor(out=qtb[:, :], in0=q[:, :], in1=ex[:, :], op=ALU.mult),
                                    reads=['hex', ('hq', 0), ('hq', 1)], writes=['hqtb'])
                              tk.op('act', lambda e, blast=blast: e.activation(out=dec[:, :].rearrange("p (c o) -> p c o", o=1), in_=blast, func=AF.Exp),
                                    reads=['hb'], writes=['hdec'])
                              tk.op('act', lambda e: e.activation(out=ex[:, :], in_=bt_[:, :], func=AF.Exp, scale=-1.0),
                                    reads=['hb'], writes=['hex'])
                              tk.op('dve', lambda e: e.tensor_tensor(out=ktb[:, :], in0=ft[:, :], in1=ex[:, :], op=ALU.mult),
                                    reads=['hex', 'hf'], writes=['hktb'])
                              tk.op('dve', lambda e: e.tensor_tensor(
                                  out=khT[:, :].rearrange("p (c i) -> p c i", i=HC),
                                  in0=ktb[:, :].rearrange("p (c i) -> p c i", i=HC),
                                  in1=dec[:, :].rearrange("p (c o) -> p c o", o=1).to_broadcast([128, T // HC, HC]), op=ALU.mult),
                                  reads=['hktb', 'hdec'], writes=['hkhT'], sync_same=True)
                              if HSTOP <= 3:
                                  continue
                              tk.mm([lambda e, b=b: e.transpose(PT[:, b * 128:(b + 1) * 128], khT[:, b * 128:(b + 1) * 128], ident_bf[:, :])
                                     for b in range(8)], reads=['hkhT', 'ident'], writes=['PT'])
                              tk.op('act', lambda e: e.activation(out=khtok[:, :, :], in_=PT[:, :].rearrange("p (b d) -> p b d", d=128), func=AF.Copy),
                                    reads=['PT'], writes=['hkhtok'])
                              if HSTOP <= 4:
                                  continue
                              for b4 in range(2):
                                  tk.mm([lambda e, j=j, b4=b4: e.matmul(
                                      P[b4][:, j * 128:(j + 1) * 128], lhsT=ktb[:, (b4 * 4 + j) * 128:(b4 * 4 + j + 1) * 128],
                                      rhs=qtb[:, (b4 * 4 + j) * 128:(b4 * 4 + j + 1) * 128], start=True, stop=True) for j in range(4)],
                                      reads=['hktb', 'hqtb'], writes=[('P', b4)])
                                  tk.op('dve', lambda e, b4=b4, d=d: e.tensor_tensor(
                                      out=atm[:, b4 * 4:(b4 + 1) * 4, :], in0=P[b4][:, :].rearrange("p (j t) -> p j t", t=128),
                                      in1=hmask[:, d:d + 1, :].to_broadcast([128, 4, 128]), op=ALU.mult),
                                      reads=[('P', b4), 'hmask'], writes=[('hatm', b4)])
                                  tk.mm([lambda e, j=j, b4=b4: e.matmul(
                                      P[4 + b4][:, j * 128:(j + 1) * 128], lhsT=vtok[:, 0, b4 * 4 + j, :], rhs=atm[:, b4 * 4 + j, :],
                                      start=True, stop=True) for j in range(4)],
                                      reads=[('hatm', b4), ('vtok', 0, b4)], writes=[('P', 4 + b4)])
                                  if d == 0:
                                      tk.op('act', lambda e, b4=b4: e.activation(out=oacc[:, HS[b4]], in_=P[4 + b4][:, :], func=AF.Copy),
                                            reads=[('P', 4 + b4)], writes=[('hoacc', b4)])
                                  else:
                                      tk.op('dve', lambda e, b4=b4: e.tensor_tensor(out=oacc[:, HS[b4]], in0=oacc[:, HS[b4]], in1=P[4 + b4][:, :], op=ALU.add),
                                            reads=[('P', 4 + b4), ('hoacc', b4)], writes=[('hoacc', b4)])
                              if HSTOP <= 5:
                                  continue
                              tk.dma('sp', 'sini', sini[:], sinit_d[l, d, hh], writes=['hsini'])
                              tk._wait('pe', tk._deps((), [('P', 2), ('P', 3)] + PIK))
                              cur = 0
                              nchk = T // HC
                              cps = SEG // HC
                              order = list(range(nchk)) if d == 0 else list(range(nchk - 1, -1, -1))
                              KB = [6, 5, 4, 0]
                              LA = 3

                              def emit_kv(step_):
                                  c_ = order[step_]
                                  b_ = (c_ * HC) // 128
                                  m_ = ((c_ * HC) % 128) // HC
                                  kb_ = KB[step_ % 4]
                                  tk.mm([lambda e: e.matmul(P[kb_][:, 0:128], lhsT=khtok[:, b_, :], rhs=vtok[:, 1 + m_, b_, :],
                                                            start=True, stop=True)],
                                        reads=['hkhtok', ('vtok', 1 + m_, b_ // 4)], writes=[('P', kb_)])

                              for s0_ in range(LA):
                                  emit_kv(s0_)
                              for step, c in enumerate(order):
                                  seg = c // cps
                                  if step + LA < nchk:
                                      emit_kv(step + LA)
                                  if step % cps == 0:
                                      if step == 0:
                                          tk.op('dve', lambda e, seg=seg: e.tensor_copy(out=S[cur][:, :], in_=sini[:, seg * 128:(seg + 1) * 128]),
                                                reads=['hsini'], writes=[('hS', cur)])
                                      else:
                                          nxt = (cur + 1) % 3
                                          tk.op('dve', lambda e, seg=seg, cur=cur, nxt=nxt: e.scalar_tensor_tensor(
                                              out=S[nxt][:, :], in0=S[cur][:, :], scalar=carry, in1=sini[:, seg * 128:(seg + 1) * 128],
                                              op0=ALU.mult, op1=ALU.add), reads=[('hS', cur), 'hsini', 'small'], writes=[('hS', nxt)])
                                          cur = nxt
                                  sbi = step % 2
                                  tk.op('act', lambda e, cur=cur, sbi=sbi: e.activation(out=Sbf[sbi][:, :], in_=S[cur][:, :], func=AF.Copy),
                                        reads=[('hS', cur)], writes=[('hSbf', sbi)])
                                  pi = 2 + (c * HC) // 512
                                  col = (c * HC) % 512
                                  tk.mm([lambda e, c=c, sbi=sbi, pi=pi, col=col: e.matmul(
                                      P[pi][:, col:col + HC], lhsT=Sbf[sbi][:, :], rhs=qtb[:, c * HC:(c + 1) * HC], start=True, stop=True)],
                                      reads=[('hSbf', sbi), 'hqtb'], writes=[('P', pi)])
                                  kb = KB[step % 4]
                                  nxt = (cur + 1) % 3
                                  tk.op('dve', lambda e, c=c, cur=cur, nxt=nxt, kb=kb: e.scalar_tensor_tensor(
                                      out=S[nxt][:, :], in0=S[cur][:, :], scalar=dec[:, c:c + 1], in1=P[kb][:, 0:128],
                                      op0=ALU.mult, op1=ALU.add), reads=[('hS', cur), ('P', kb), 'hdec'], writes=[('hS', nxt)], same_ok=True)
                                  cur = nxt
                                  if step % cps == cps - 1:
                                      tk.op('dve', lambda e, seg=seg, cur=cur: e.tensor_copy(out=sstage[:, seg * 128:(seg + 1) * 128], in_=S[cur][:, :]),
                                            reads=[('hS', cur)], writes=['hsstage'])
                              tk.dma('sp', 'hgo', hg_o[l, d, hh], sstage[:], reads=['hsstage'])
                              for h in range(2):
                                  pk = []
                                  tk.op('dve', lambda e, h=h: e.tensor_tensor(out=oacc[:, HS[h]], in0=oacc[:, HS[h]], in1=P[2 + h][:, :], op=ALU.add),
                                        reads=pk + [('hoacc', h)], writes=[('hoacc', h), ('P', 2 + h)] + pk)
                          if HSTOP <= 6:
                              continue
                          for h in range(2):
                              hp.feat(sC, 0, h, 2 + h)
                              tk.op('act', lambda e, h=h: e.activation(out=ft[:, HS[h]], in_=P[2 + h][:, :], func=AF.Silu),
                                    reads=[('P', 2 + h)], writes=['hf'])
                          for h in range(2):
                              tk.op('act', lambda e, h=h: e.activation(out=khT[:, HS[h]], in_=oacc[:, HS[h]], func=AF.Square),
                                    reads=[('hoacc', h)], writes=['hkhT'])
                              tk.mm([lambda e, h=h: e.matmul(P[h][:, :], lhsT=ones_bf[:, :], rhs=khT[:, HS[h]], start=True, stop=True)],
                                    reads=['hkhT', 'ones'], writes=[('P', h)])
                              tk.op('act', lambda e, h=h: e.activation(out=ex[:, HS[h]], in_=P[h][:, :], func=AF.Ln, bias=epst[:, 0:1], scale=1.0 / 128),
                                    reads=[('P', h), 'eps'], writes=['hex'])
                              tk.op('act', lambda e, h=h: e.activation(out=ex[:, HS[h]], in_=ex[:, HS[h]], func=AF.Exp, scale=-0.5), reads=['hex'], writes=['hex'])
                              tk.op('dve', lambda e, h=h, hh=hh: e.scalar_tensor_tensor(
                                  out=oacc[:, HS[h]], in0=oacc[:, HS[h]], scalar=hgng[:, l * 4 + hh: l * 4 + hh + 1], in1=ex[:, HS[h]],
                                  op0=ALU.mult, op1=ALU.mult), reads=['hex', ('hoacc', h), 'hgng'], writes=[('hoacc', h)])
                              tk.op('dve', lambda e, h=h, hh=hh: e.tensor_tensor(out=oabc[:, hh, HS[h]], in0=oacc[:, HS[h]], in1=ft[:, HS[h]], op=ALU.mult),
                                    reads=[('hoacc', h), 'hf'], writes=[('oabc', hh, h)])
                      tk.barrier()

                with ExitStack() as ns:
                  if STAGE >= 3:
                      npn = Panels("np", 2, 384, ns)
                      nsl = {0: npn.load(2560, 384)}
                      qT = sb("nqT", [128, T], BF16, ns)
                      kTb = sb("nkT", [128, T], BF16, ns)
                      kTf = sb("nkTf", [128, T], F32, ns)
                      vt = sb("nvt", [128, 8, 128], BF16, ns)
                      vtf = sb("nvtf", [128, 8, 128], F32, ns)
                      kc = sb("nkc", [128, 512], BF16, ns)
                      vc = sb("nvc", [128, 4, 128], BF16, ns)
                      bias = sb("nbias", [128, len(QK_TILES), 512], BF16, ns)
                      stmp = [sb("nst%d" % i, [128, 512], F32, ns) for i in range(3)]
                      pT = [sb("npT%d" % i, [128, 512], BF16, ns) for i in range(4)]
                      SBK = [0, 1, 6]
                      rden = sb("nrden", [128, 512], F32, ns)
                      pc = [0, 0]
                      for hd in range(8):
                          tk.dma('pool', 'nb', bias[:], bias_d[l, hd].rearrange("t p q -> p t q"), writes=['nbias'])
                          tk.dma('pool', 'nkc', kc[:], kctx_d[l, hd], writes=['nkc'])
                          tk.dma('pool', 'nvc', vc[:], vctx_d[l, hd].rearrange("p (b e) -> p b e", e=128), writes=['nvc'])
                          if hd + 1 < 8:
                              nsl[hd + 1] = npn.load(2560 + (hd + 1) * 384, 384)
                          s = nsl[hd]
                          for h in range(2):
                              npn.feat(s, 0, h, h)
                              tk.op('act', lambda e, h=h: e.activation(out=qT[:, HS[h]], in_=P[h][:, :], func=AF.Copy),
                                    reads=[('P', h)], writes=[('nq', h)])
                          for h in range(2):
                              npn.feat(s, 128, h, 2 + h)
                              tk.op('act', lambda e, h=h: e.activation(out=kTb[:, HS[h]], in_=P[2 + h][:, :], func=AF.Copy),
                                    reads=[('P', 2 + h)], writes=[('nk', h)])
                              tk.op('dve', lambda e, h=h: e.tensor_copy(out=kTf[:, HS[h]], in_=P[2 + h][:, :]),
                                    reads=[('P', 2 + h)], writes=[('nkf', h)])
                          tk.dma('sp', 'ko', kT_o[l, hd * 128:(hd + 1) * 128, :], kTf[:], reads=[('nkf', 0), ('nkf', 1)])
                          for b4 in range(2):
                              npn.tok(s, 256, b4, 4 + b4)
                              src = P[4 + b4][:, :].rearrange("p (j e) -> p j e", e=128)
                              tk.op('act', lambda e, b4=b4, src=src: e.activation(out=vt[:, b4 * 4:(b4 + 1) * 4, :], in_=src, func=AF.Copy),
                                    reads=[('P', 4 + b4)], writes=[('nv', b4)])
                              tk.op('dve', lambda e, b4=b4, src=src: e.tensor_copy(out=vtf[:, b4 * 4:(b4 + 1) * 4, :], in_=src),
                                    reads=[('P', 4 + b4)], writes=[('nvf', b4)])
                          tk.dma('sp', 'vo', v_o[l].rearrange("(b p) f -> p b f", p=128)[:, :, hd * 128:(hd + 1) * 128], vtf[:],
                                 reads=[('nvf', 0), ('nvf', 1)])
                          for qh in range(2):
                              tiles = [(i, kb) for i, (kb, qh2) in enumerate(QK_TILES) if qh2 == qh]
                              items = [('loc', i, kb) for (i, kb) in tiles] + [('ctx', None, cb) for cb in range(4)]
                              n_it = len(items)
                              slots_ = []
                              for ii in range(n_it):
                                  slots_.append((pc[0] % 3, pc[1] % 4))
                                  pc[0] += 1
                                  pc[1] += 1
                              A0, A1 = (2, 3) if qh == 0 else (4, 5)

                              def qk(ii):
                                  kind, bi, kb = items[ii]
                                  ps = SBK[slots_[ii][0]]
                                  if kind == 'loc':
                                      tk.mm([lambda e: e.matmul(P[ps][:, :], lhsT=kTb[:, kb * 128:(kb + 1) * 128], rhs=qT[:, HS[qh]],
                                                                start=True, stop=True)],
                                            reads=[('nk', kb // 4), ('nq', qh)], writes=[('P', ps)])
                                  else:
                                      tk.mm([lambda e: e.matmul(P[ps][:, :], lhsT=kc[:, kb * 128:(kb + 1) * 128], rhs=qT[:, HS[qh]],
                                                                start=True, stop=True)],
                                            reads=['nkc', ('nq', qh)], writes=[('P', ps)])

                              qk(0)
                              qk(1)
                              for ii, (kind, bi, kb) in enumerate(items):
                                  si_, pp = slots_[ii]
                                  ps = SBK[si_]
                                  if ii + 2 < n_it:
                                      qk(ii + 2)
                                  if kind == 'loc':
                                      tk.op('dve', lambda e, ps=ps, bi=bi, si_=si_: e.scalar_tensor_tensor(
                                          out=stmp[si_][:, :], in0=P[ps][:, :], scalar=128.0 ** -0.5, in1=bias[:, bi, :], op0=ALU.mult, op1=ALU.add),
                                          reads=[('P', ps), 'nbias'], writes=[('nst', si_)])
                                      tk.op('act', lambda e, si_=si_, pp=pp: e.activation(out=pT[pp][:, :], in_=stmp[si_][:, :], func=AF.Exp),
                                            reads=[('nst', si_)], writes=[('npT', pp)])
                                      vl = vt[:, kb, :]
                                      vkey = ('nv', kb // 4)
                                  else:
                                      tk.op('act', lambda e, ps=ps, pp=pp: e.activation(out=pT[pp][:, :], in_=P[ps][:, :], func=AF.Exp,
                                                                                        bias=ctxb, scale=128.0 ** -0.5),
                                            reads=[('P', ps), 'small'], writes=[('npT', pp)])
                                      vl = vc[:, kb, :]
                                      vkey = 'nvc'
                                  tk.mm([lambda e, vl=vl, pp=pp, ii=ii: e.matmul(P[A0][:, :], lhsT=vl, rhs=pT[pp][:, :], start=(ii == 0), stop=(ii == n_it - 1)),
                                         lambda e, pp=pp, ii=ii: e.matmul(P[A1][:, :], lhsT=ones_bf[:, :], rhs=pT[pp][:, :], start=(ii == 0), stop=(ii == n_it - 1))],
                                        reads=[('npT', pp), vkey, 'ones'], writes=[('P', A0), ('P', A1)])
                              tk.op('dve', lambda e: e.reciprocal(out=rden[:, :], in_=P[A1][:, :]), reads=[('P', A1)], writes=['nrden'])
                              tk.op('dve', lambda e, qh=qh, hd=hd: e.tensor_tensor(out=oabc[:, 4 + hd, HS[qh]], in0=P[A0][:, :], in1=rden[:, :], op=ALU.mult),
                                    reads=[('P', A0), 'nrden'], writes=[('oabc', 4 + hd, qh)])
                      tk.barrier()

                with ExitStack() as ls:
                  if STAGE >= 4:
                      PADW = SEG + 3
                      lpn = Panels("lp", 2, 256, ls)
                      lsl = {0: lpn.load(5632, 256)}
                      lxp = sb("llxp", [128, NSEG, PADW], F32, ls)
                      xc = sb("lxc", [128, T], F32, ls)
                      xcb = sb("lxcb", [128, T], BF16, ls)
                      lgt = sb("llg", [128, T], F32, ls)
                      gl = sb("lgl", [128, T], F32, ls)
                      rg = sb("lrg", [128, T], F32, ls)
                      ig = sb("lig", [128, T], F32, ls)
                      at_ = sb("lat", [128, T], F32, ls)
                      ut = sb("lut", [128, T], F32, ls)
                      hsf = sb("lhsf", [128, T], F32, ls)
                      hsb = sb("lhsb", [128, T], F32, ls)
                      ini = sb("lini", [128, 1], F32, ls)
                      bd_bf = sb("bd_bf", [128, 16, 128], BF16, ls)
                      tk.dma('pool', 'c0', bd_bf[:], lrubd_d[l].rearrange("a b c p q -> p (a b c) q"), writes=['bd'])
                      tk.op('dve', lambda e: e.memset(lxp[:], 0.0), writes=['llxp'])
                      for cc in range(4):
                          vb = l * 48
                          if cc + 1 < 4:
                              lsl[cc + 1] = lpn.load(5632 + (cc + 1) * 256, 256)
                          s = lsl[cc]
                          for h in range(2):
                              lpn.feat(s, 0, h, h)
                              tk.op('act', lambda e, h=h: e.activation(out=lxp[:, 2 * h:2 * h + 2, 2:2 + SEG],
                                                                       in_=P[h][:, :].rearrange("p (s t) -> p s t", t=SEG), func=AF.Copy),
                                    reads=[('P', h)], writes=['llxp'])
                          for h in range(2):
                              lpn.feat(s, 128, h, 2 + h)
                              tk.op('act', lambda e, h=h: e.activation(out=lgt[:, HS[h]], in_=P[2 + h][:, :], func=AF.Copy),
                                    reads=[('P', 2 + h)], writes=['llg'])
                          tk.op('dve', lambda e: e.tensor_scalar(out=lxp[:, 1:NSEG, 0:2], in0=lxp[:, 0:NSEG - 1, SEG:SEG + 2], scalar1=carry,
                                                                 scalar2=None, op0=ALU.mult), reads=['llxp', 'small'], writes=['llxp'], sync_same=True)
                          tk.op('dve', lambda e: e.tensor_scalar(out=lxp[:, 0:NSEG - 1, SEG + 2:SEG + 3], in0=lxp[:, 1:NSEG, 2:3], scalar1=carry,
                                                                 scalar2=None, op0=ALU.mult), reads=['llxp', 'small'], writes=['llxp'], sync_same=True)
                          xc3 = xc[:, :].rearrange("p (s t) -> p s t", t=SEG)
                          cw = lambda j: lruv[:, vb + j * 4 + cc: vb + j * 4 + cc + 1]
                          cb = lruv[:, vb + 16 + cc: vb + 16 + cc + 1]
                          tk.op('dve', lambda e: e.tensor_scalar(out=xc3, in0=lxp[:, :, 0:SEG], scalar1=cw(0), scalar2=cb, op0=ALU.mult, op1=ALU.add),
                                reads=['llxp', 'lruv'], writes=['lxc'], sync_same=True)
                          for j in range(1, 4):
                              tk.op('dve', lambda e, j=j: e.scalar_tensor_tensor(out=xc3, in0=lxp[:, :, j:j + SEG], scalar=cw(j), in1=xc3,
                                                                                op0=ALU.mult, op1=ALU.add), reads=['llxp', 'lxc', 'lruv'], writes=['lxc'])
                          tk.op('act', lambda e: e.activation(out=xcb[:, :], in_=xc[:, :], func=AF.Copy), reads=['lxc'], writes=['lxcb'])
                          tk.op('pool', lambda e: e.tensor_tensor(out=gl[:, :], in0=lgt[:, :], in1=lgt[:, :], op=ALU.mult), reads=['llg'], writes=['lgl'])
                          tk.op('pool', lambda e: e.tensor_scalar(out=gl[:, :], in0=gl[:, :], scalar1=0.044715, scalar2=1.0, op0=ALU.mult, op1=ALU.add),
                                reads=['lgl'], writes=['lgl'], sync_same=True)
                          tk.op('pool', lambda e: e.tensor_tensor(out=gl[:, :], in0=gl[:, :], in1=lgt[:, :], op=ALU.mult), reads=['lgl', 'llg'], writes=['lgl'], sync_same=True)
                          tk.op('act', lambda e: e.activation(out=gl[:, :], in_=gl[:, :], func=AF.Sigmoid, scale=1.5957691216057308), reads=['lgl'], writes=['lgl'])
                          tk.op('pool', lambda e: e.tensor_tensor(out=gl[:, :], in0=gl[:, :], in1=lgt[:, :], op=ALU.mult), reads=['lgl', 'llg'], writes=['lgl'])
                          for d in range(2):
                              ba = lruv[:, vb + 20 + d * 4 + cc: vb + 20 + d * 4 + cc + 1]
                              bx = lruv[:, vb + 28 + d * 4 + cc: vb + 28 + d * 4 + cc + 1]
                              n8 = nsp[:, l * 16 + d * 4 + cc: l * 16 + d * 4 + cc + 1]
                              n16 = nsp[:, l * 16 + 8 + d * 4 + cc: l * 16 + 8 + d * 4 + cc + 1]
                              bda = bd_bf[:, (d * 2 + 0) * 4 + cc, :]
                              bdx = bd_bf[:, (d * 2 + 1) * 4 + cc, :]
                              for h in range(2):
                                  tk.mm([lambda e, h=h: e.matmul(P[4][:, :], lhsT=bda, rhs=xcb[:, HS[h]], start=True, stop=True)],
                                        reads=['lxcb', 'bd'], writes=[('P', 4)])
                                  tk.op('act', lambda e, h=h: e.activation(out=rg[:, HS[h]], in_=P[4][:, :], func=AF.Sigmoid, bias=ba, scale=1.0),
                                        reads=[('P', 4), 'lruv'], writes=['lrg'])
                                  tk.mm([lambda e, h=h: e.matmul(P[5][:, :], lhsT=bdx, rhs=xcb[:, HS[h]], start=True, stop=True)],
                                        reads=['lxcb', 'bd'], writes=[('P', 5)])
                                  tk.op('act', lambda e, h=h: e.activation(out=ig[:, HS[h]], in_=P[5][:, :], func=AF.Sigmoid, bias=bx, scale=1.0),
                                        reads=[('P', 5), 'lruv'], writes=['lig'])
                              tk.op('act', lambda e: e.activation(out=at_[:, :], in_=rg[:, :], func=AF.Exp, scale=n8), reads=['lrg', ('nsp', l)], writes=['lat'])
                              tk.op('act', lambda e: e.activation(out=ut[:, :], in_=rg[:, :], func=AF.Exp, scale=n16), reads=['lrg', ('nsp2', l)], writes=['lut'])
                              tk.op('dve', lambda e: e.tensor_scalar(out=ut[:, :], in0=ut[:, :], scalar1=-1.0, scalar2=1.0, op0=ALU.mult, op1=ALU.add),
                                    reads=['lut'], writes=['lut'])
                              tk.op('act', lambda e: e.activation(out=ut[:, :], in_=ut[:, :], func=AF.Sqrt), reads=['lut'], writes=['lut'])
                              tk.op('dve', lambda e: e.tensor_tensor(out=ut[:, :], in0=ut[:, :], in1=ig[:, :], op=ALU.mult), reads=['lut', 'lig'], writes=['lut'])
                              tk.op('dve', lambda e: e.tensor_tensor(out=ut[:, :], in0=ut[:, :], in1=xc[:, :], op=ALU.mult), reads=['lut', 'lxc'], writes=['lut'], sync_same=True)
                              hs_ = hsf if d == 0 else hsb
                              hkey = 'lhs%d' % d
                              segs = list(range(NSEG)) if d == 0 else list(range(NSEG - 1, -1, -1))
                              for si, sg in enumerate(segs):
                                  h0c = h0t[:, ((l * 2 + d) * NSEG + sg) * 4 + cc: ((l * 2 + d) * NSEG + sg) * 4 + cc + 1]
                                  fcol = ((l * 2 + d) * NSEG + sg) * 4 + cc
                                  if si == 0:
                                      tk.op('dve', lambda e, h0c=h0c: e.tensor_copy(out=ini[:, :], in_=h0c), reads=['h0t', hkey], writes=['lini'], sync_same=True)
                                  else:
                                      psg = segs[si - 1]
                                      pcol = (psg * SEG + SEG - 1) if d == 0 else (psg * SEG)
                                      tk.op('dve', lambda e, h0c=h0c, pcol=pcol, hs_=hs_: e.scalar_tensor_tensor(
                                          out=ini[:, :], in0=hs_[:, pcol:pcol + 1], scalar=carry, in1=h0c, op0=ALU.mult, op1=ALU.add),
                                          reads=['h0t', hkey, 'small'], writes=['lini'], sync_same=True)
                                  sl = slice(sg * SEG, (sg + 1) * SEG)
                                  if d == 0:
                                      tk.op('dve', lambda e, sl=sl, hs_=hs_: e.tensor_tensor_scan(out=hs_[:, sl], data0=at_[:, sl], data1=ut[:, sl],
                                                                                                  initial=ini[:, 0:1], op0=ALU.mult, op1=ALU.add),
                                            reads=['lat', 'lut', 'lini'], writes=[hkey], sync_same=True)
                                      lcol = sg * SEG + SEG - 1
                                  else:
                                      rs_ = slice((sg + 1) * SEG - 1, sg * SEG - 1 if sg > 0 else None, -1)
                                      tk.op('dve', lambda e, rs_=rs_, hs_=hs_: e.tensor_tensor_scan(out=hs_[:, rs_], data0=at_[:, rs_], data1=ut[:, rs_],
                                                                                                    initial=ini[:, 0:1], op0=ALU.mult, op1=ALU.add),
                                            reads=['lat', 'lut', 'lini'], writes=[hkey], sync_same=True)
                                      lcol = sg * SEG
                                  tk.op('pool', lambda e, fcol=fcol, lcol=lcol, hs_=hs_: e.tensor_copy(out=lrufin[:, fcol:fcol + 1], in_=hs_[:, lcol:lcol + 1]),
                                        reads=[hkey], writes=['lrufin'])
                          tk.op('dve', lambda e: e.tensor_tensor(out=hsf[:, :], in0=hsf[:, :], in1=hsb[:, :], op=ALU.add), reads=['lhs0', 'lhs1'], writes=['lhs0'])
                          tk.op('dve', lambda e, cc=cc: e.tensor_tensor(out=oabc[:, 12 + cc, :], in0=hsf[:, :], in1=gl[:, :], op=ALU.mult),
                                reads=['lhs0', 'lgl'], writes=[('oabc', 12 + cc, 0), ('oabc', 12 + cc, 1)], sync_same=True)
                      tk.barrier()

                if DEBUG:
                    for c in range(NCH):
                        tk.dma('pool', 'dbg2', dbg_o[1, c * 128:(c + 1) * 128, :], oabc[:, c, :], reads=[('oabc', c, 0), ('oabc', c, 1)])
                    tk.barrier()
                with ExitStack() as go_:
                  if STAGE >= 5:
                    mT = sb("mT", [128, NCH, T], BF16, go_)
                    with ExitStack() as gs:
                      gpn = Panels("gp", 2, 384, gs)
                      gsl = {0: gpn.load(6656, 384)}
                      wp = sb("wp", [128, NCH, 128], BF16, gs)
                      gsb = [sb("gsb%d" % i, [128, 512], F32, gs) for i in range(2)]
                      macc = [sb("macc%d" % i, [128, 512], F32, gs) for i in range(2)]
                      gc = [0]
                      wsrc = [wpa_d[l].rearrange("(k p) d -> p k d", p=128), wpb_d[l].rearrange("(k p) d -> p k d", p=128),
                              wpc_d[l].rearrange("(k p) d -> p k d", p=128)]
                      krange = [(0, 4), (4, 12), (12, 16)]

                      def wp_load(br, dc):
                          k0, k1 = krange[br]
                          tk.dma('pool', 'wp' + 'abc'[br], wp[:, k0:k1, :], wsrc[br][:, :, dc * 128:(dc + 1) * 128], writes=[('wp', 0, br)])

                      for br in range(3):
                          wp_load(br, 0)
                      for dc in range(NCH):
                          if dc + 1 < NCH:
                              gsl[dc + 1] = gpn.load(6656 + (dc + 1) * 384, 384)
                          for br, (k0, k1) in enumerate(krange):
                              s = gsl[dc]
                              for h in range(2):
                                  gi = gc[0] % 2
                                  gc[0] += 1
                                  gpn.feat(s, br * 128, h, gi)
                                  tk.op('act', lambda e, gi=gi: e.activation(out=gsb[gi][:, :], in_=P[gi][:, :], func=AF.Sigmoid),
                                        reads=[('P', gi)], writes=[('gsb', gi)])
                                  tk.mm([lambda e, k=k, h=h, gi=gi: e.matmul(P[2 + gi][:, :], lhsT=wp[:, k, :], rhs=oabc[:, k, HS[h]],
                                                                            start=(k == k0), stop=(k == k1 - 1)) for k in range(k0, k1)],
                                        reads=[('wp', 0, br)] + [('oabc', k, h) for k in range(k0, k1)], writes=[('P', 2 + gi)])
                                  if br == 0:
                                      tk.op('dve', lambda e, gi=gi, h=h: e.tensor_tensor(out=macc[h][:, :], in0=gsb[gi][:, :], in1=P[2 + gi][:, :], op=ALU.mult),
                                            reads=[('gsb', gi), ('P', 2 + gi)], writes=[('macc', h)])
                                  else:
                                      tk.op('dve', lambda e, gi=gi: e.tensor_tensor(out=gsb[gi][:, :], in0=gsb[gi][:, :], in1=P[2 + gi][:, :], op=ALU.mult),
                                            reads=[('gsb', gi), ('P', 2 + gi)], writes=[('gsb', gi)])
                                      if br == 1:
                                          tk.op('pool', lambda e, gi=gi, h=h: e.tensor_tensor(out=macc[h][:, :], in0=macc[h][:, :], in1=gsb[gi][:, :], op=ALU.add),
                                                reads=[('gsb', gi), ('macc', h)], writes=[('macc', h)])
                                      else:
                                          tk.op('pool', lambda e, gi=gi, h=h, dc=dc: e.tensor_tensor(out=mT[:, dc, HS[h]], in0=macc[h][:, :], in1=gsb[gi][:, :], op=ALU.add),
                                                reads=[('gsb', gi), ('macc', h)], writes=[('mT', dc, h)])
                              if dc + 1 < NCH:
                                  wp_load(br, dc + 1)
                      tk.barrier()
                    with ExitStack() as ws_:
                      wo = [sb("wo%d" % i, [128, NCH, 128], BF16, ws_) for i in range(2)]
                      woutv = wout_d[l].rearrange("(k p) d -> p k d", p=128)
                      oc = [0]
                      tk.dma('pool', 'wo0', wo[0][:], woutv[:, :, 0:128], writes=[('wo', 0)])
                      for dc in range(NCH):
                          sp_ = dc % 2
                          if dc + 1 < NCH:
                              tk.dma('pool', 'wo%d' % (1 - sp_), wo[1 - sp_][:], woutv[:, :, (dc + 1) * 128:(dc + 2) * 128], writes=[('wo', 1 - sp_)])
                          for h in range(2):
                              pd = 4 + oc[0] % 3
                              oc[0] += 1
                              tk.mm([lambda e, k=k, h=h, pd=pd: e.matmul(P[pd][:, :], lhsT=wo[sp_][:, k, :], rhs=mT[:, k, HS[h]],
                                                                        start=(k == 0), stop=(k == NCH - 1)) for k in range(NCH)],
                                    reads=[('wo', sp_)] + [('mT', k, h) for k in range(NCH)], writes=[('P', pd)])
                              tk.op('dve', lambda e, dc=dc, h=h, pd=pd: e.scalar_tensor_tensor(
                                  out=xT[:, dc, HS[h]], in0=P[pd][:, :], scalar=g_ap(l, 1, dc), in1=xT[:, dc, HS[h]], op0=ALU.mult, op1=ALU.add),
                                  reads=[('P', pd), ('x', dc, h), ('gmod', l, 1)], writes=[('x', dc, h)])
                      tk.barrier()

        if DEBUG:
            tk.dma('sp', 'dbg', dbg_o[7, 0:128, 0:L * 144], modT[:], reads=[('modT', 0), ('modT', 1)])
        for l in range(L):
            if STAGE < 6 and l > 0:
                break
            if STAGE >= 1:
                ffn(l, 0, wup_d[0], wdn_d[0], nmods=(32 if l == 0 else 0))
                if l == 0:
                    dbg_dump(0)
            if STAGE >= 2:
                mixer(l)
                if l == 0:
                    dbg_dump(2)
            if STAGE >= 6:
                ffn(l, 2, wup_d[1], wdn_d[1], nmods=(32 if l == 0 else 0))
                if l == 0:
                    dbg_dump(3)

        with ExitStack() as fs_:
            rstd = sb("rstd", [128, T], F32, fs_)
            rstd_box[0] = rstd
            sumsq_rstd(1.0 / D)
            yst = [sb("yst%d" % i, [128, T], F32, fs_) for i in range(2)]
            for c in range(NCH):
                yb = c % 2
                tk.op('dve', lambda e, c=c, yb=yb: e.scalar_tensor_tensor(out=yst[yb][:, :], in0=xT[:, c, :], scalar=fng[:, c:c + 1], in1=rstd[:, :],
                                                                           op0=ALU.mult, op1=ALU.mult),
                      reads=[('x', c, 0), ('x', c, 1), ('rstd', 0), ('rstd', 1), 'fng'], writes=[('yst', yb)])
                tk.dma('sp', 'yo%d' % yb, yT_o[c * 128:(c + 1) * 128, :], yst[yb][:], reads=[('yst', yb)])
            tk.dma('sp', 'lo', lru_o, lrufin[:], reads=['lrufin'])
            tk.final_wait()
    return nc


def _fm(v):
    v = np.asarray(v, np.float32)
    lead = v.shape[:-1]
    n = v.shape[-1] // 128
    a = v.reshape(lead + (n, 128))
    a = np.moveaxis(a, -1, 0)
    return np.ascontiguousarray(a.reshape(128, -1))


def _na_bias_sample(rpb_l):
    rows, W, KH, KW = 16, 64, 8, 16
    r = np.arange(rows)
    kr0 = np.clip(r - KH // 2, 0, rows - KH)
    qc = np.arange(W)
    ws = np.clip(qc - KW // 2, 0, W - KW)
    kr = np.arange(rows)
    kcol = np.arange(W)
    row_ok = (kr[None, :] >= kr0[:, None]) & (kr[None, :] < kr0[:, None] + KH)
    col_ok = (kcol[None, :] >= ws[:, None]) & (kcol[None, :] < ws[:, None] + KW)
    dy = np.clip(kr[None, :] - r[:, None] + KH - 1, 0, 2 * KH - 2)
    dx = np.clip(kcol[None, :] - qc[:, None] + KW - 1, 0, 2 * KW - 2)
    b = rpb_l[:, dy[:, None, :, None], dx[None, :, None, :]]
    ok = row_ok[:, None, :, None] & col_ok[None, :, None, :]
    b = np.where(ok[None], b, np.float32(NEGM)).astype(np.float32)
    b = b.reshape(8, 1024, 1024)
    return np.ascontiguousarray(b.transpose(0, 2, 1))


def _na_bias_prompt():
    seg = np.arange(1024) // 256
    ok = seg[:, None] == seg[None, :]
    b = np.where(ok, np.float32(0.0), np.float32(NEGM)).astype(np.float32)
    return np.broadcast_to(b[None], (8, 1024, 1024))


def _tile_bias(b):
    out = np.empty((8, len(QK_TILES), 128, 512), np.float32)
    for i, (kb, qh) in enumerate(QK_TILES):
        out[:, i] = b[:, kb * 128:(kb + 1) * 128, qh * 512:(qh + 1) * 512]
    return out


_NC_CACHE = {}


def _win_perm():
    p = []
    for hh in range(4):
        for off in (C_ZQ, C_ZI, C_ZFF, C_ZFB, C_ZO):
            p.extend(range(off + hh * 128, off + (hh + 1) * 128))
    for hd in range(8):
        for off in (C_NQ, C_NK, C_NV):
            p.extend(range(off + hd * 128, off + (hd + 1) * 128))
    for cc in range(4):
        for off in (C_LX, C_LG):
            p.extend(range(off + cc * 128, off + (cc + 1) * 128))
    for dc in range(16):
        for br in range(3):
            p.extend(range(C_GATE + br * D + dc * 128, C_GATE + br * D + (dc + 1) * 128))
    assert len(p) == DIN and len(set(p)) == DIN
    return np.asarray(p)


ROLES = [('p', 0), ('p', 1), ('p', 2), ('p', 3), ('s', 0), ('s', 1), ('p', 0), ('p', 1)]


def _build_in_maps(roles, x_prompt, x_sample, cache_na_k, cache_na_v, state_hgrn, state_lru, c, c_ctx,
                   mod_w, mod_b, norm_g, ffn1_w_up, ffn1_w_down, ffn2_w_up, ffn2_w_down, w_in,
                   hgrn_lb_logits, hgrn_norm_g, na_rpb, lru_conv_w, lru_conv_b, lru_w_a, lru_b_a,
                   lru_w_x, lru_b_x, lru_lambda, w_proj_a, w_proj_b, w_proj_c, w_out, final_norm_g):
    f32 = np.float32
    A = lambda a: np.ascontiguousarray(np.asarray(a, f32))
    x_prompt, x_sample = A(x_prompt), A(x_sample)
    cache_na_k, cache_na_v = A(cache_na_k), A(cache_na_v)
    state_hgrn, state_lru = A(state_hgrn), A(state_lru)
    c, c_ctx = A(c), A(c_ctx)
    na_rpb = A(na_rpb)

    shared = {
        "mod_w": A(mod_w), "ffn1_w_up": A(ffn1_w_up), "ffn1_w_down": A(ffn1_w_down),
        "ffn2_w_up": A(ffn2_w_up), "ffn2_w_down": A(ffn2_w_down), "w_in": np.ascontiguousarray(A(w_in)[:, :, _win_perm()]),
        "w_proj_a": A(w_proj_a), "w_proj_b": A(w_proj_b), "w_proj_c": A(w_proj_c), "w_out": A(w_out),
        "mod_bT": _fm(A(mod_b)).reshape(128, L, 144).reshape(128, L * 144),
        "normgT": _fm(A(norm_g)),
        "fngT": _fm(A(final_norm_g)),
        "ident": np.eye(128, dtype=f32),
        "hgngT": _fm(A(hgrn_norm_g)),
    }
    shared["hglogT"] = _fm(A(hgrn_lb_logits))
    s_i = np.arange(128)
    same = (s_i[:, None] // HC) == (s_i[None, :] // HC)
    hm = np.stack([same & (s_i[:, None] <= s_i[None, :]), same & (s_i[:, None] >= s_i[None, :])]).astype(f32)
    shared["hmask"] = hm
    rm = np.zeros((128, 4), f32)
    for m in range(4):
        rm[m * HC:(m + 1) * HC, m] = 1.0
    shared["rowmask"] = rm
    lv = np.zeros((128, L, 48), f32)
    cw = A(lru_conv_w)
    lv[:, :, 0:16] = _fm(cw).reshape(128, L, 16)
    lv[:, :, 16:20] = _fm(A(lru_conv_b)).reshape(128, L, 4)
    lv[:, :, 20:28] = _fm(A(lru_b_a)).reshape(128, L, 8)
    lv[:, :, 28:36] = _fm(A(lru_b_x)).reshape(128, L, 8)
    lv[:, :, 36:44] = _fm(A(lru_lambda)).reshape(128, L, 8)
    shared["lru_vecT"] = np.ascontiguousarray(lv.reshape(128, L * 48))
    bd = np.zeros((L, 2, 2, 4, 128, 128), f32)
    for wi, w in enumerate((A(lru_w_a), A(lru_w_x))):
        for cc in range(4):
            for kk in range(2):
                bd[:, :, wi, cc, kk * 64:(kk + 1) * 64, kk * 64:(kk + 1) * 64] = w[:, :, cc * 2 + kk]
    shared["lru_bd"] = bd

    bias_prompt = None
    bias_sample = None
    in_maps = []
    for kind, idx in roles:
        m = dict(shared)
        sm = np.zeros((128, 8), f32)
        if kind == 'p':
            xs = x_prompt[idx * 4:(idx + 1) * 4].reshape(T, D)
            cond = c_ctx
            sm[:, 0] = 0.0
            sm[:, 1] = NEGM
            s_init = np.zeros((L, 2, 4, 128, NSEG, 128), f32)
            kctx = np.zeros((L, 8, 128, 512), f32)
            vctx = np.zeros((L, 8, 128, 4, 128), f32)
            if bias_prompt is None:
                bp = _tile_bias(_na_bias_prompt())
                bias_prompt = np.ascontiguousarray(np.broadcast_to(bp[None], (L,) + bp.shape))
            nb = bias_prompt
            h0 = np.zeros((128, L, 2, NSEG, 4), f32)
        else:
            xs = x_sample[idx]
            cond = c[idx]
            sm[:, 0] = 1.0
            sm[:, 1] = 0.0
            s_init = np.zeros((L, 2, 4, 128, NSEG, 128), f32)
            st = state_hgrn[idx]
            s_init[:, 0, :, :, 0, :] = st[:, 0]
            s_init[:, 1, :, :, NSEG - 1, :] = st[:, 1]
            kctx = np.ascontiguousarray(cache_na_k[idx].transpose(0, 2, 3, 1))
            vv = cache_na_v[idx].reshape(L, 4, 128, 8, 128)
            vctx = np.ascontiguousarray(vv.transpose(0, 3, 2, 1, 4))
            if bias_sample is None:
                bias_sample = np.stack([_tile_bias(_na_bias_sample(na_rpb[l])) for l in range(L)])
            nb = bias_sample
            h0 = np.zeros((128, L, 2, NSEG, 4), f32)
            sl = state_lru[idx]
            slf = _fm(sl).reshape(128, L, 2, 4)
            h0[:, :, 0, 0, :] = slf[:, :, 0]
            h0[:, :, 1, NSEG - 1, :] = slf[:, :, 1]
        m["xT"] = np.ascontiguousarray(xs.T)
        m["condT"] = _fm(cond)
        m["small"] = sm
        m["s_init"] = s_init.reshape(L, 2, 4, 128, NSEG * 128)
        m["kctxT"] = kctx
        m["vctx"] = vctx.reshape(L, 8, 128, 512)
        m["na_bias"] = nb
        m["lru_h0T"] = np.ascontiguousarray(h0.reshape(128, -1))
        in_maps.append(m)
    return in_maps


def kernel(**inputs):
    f32 = np.float32
    in_maps = _build_in_maps(ROLES, **inputs)
    if 'nc' not in _NC_CACHE:
        _NC_CACHE['nc'] = build_nc()
    nc = _NC_CACHE['nc']
    res = run_bass_kernel_spmd(nc, in_maps, core_ids=list(range(8)))
    R = res.results

    y_prompt = np.empty((16, 256, D), f32)
    y_sample = np.empty((2, T, D), f32)
    nk = np.empty((16, L, 256, 8, 128), f32)
    nv = np.empty((16, L, 256, 8, 128), f32)
    nhg = np.empty((16, L, 2, 4, 128, 128), f32)
    nlru = np.empty((16, L, 2, 512), f32)
    for ci in range(4):
        r = R[ci]
        y = np.asarray(r["yT"]).T.reshape(4, 256, D)
        y_prompt[ci * 4:(ci + 1) * 4] = y
        kT = np.asarray(r["kT_out"])
        nk[ci * 4:(ci + 1) * 4] = kT.transpose(2, 0, 1).reshape(4, 256, L, 8, 128).transpose(0, 2, 1, 3, 4)
        vo = np.asarray(r["v_out"])
        nv[ci * 4:(ci + 1) * 4] = vo.reshape(L, 4, 256, 8, 128).transpose(1, 0, 2, 3, 4)
        hg = np.asarray(r["hg_out"]).reshape(L, 2, 4, 128, NSEG, 128)
        nhg[ci * 4:(ci + 1) * 4] = hg.transpose(4, 0, 1, 2, 3, 5)
        lo = np.asarray(r["lru_out"]).reshape(128, L, 2, NSEG, 4)
        nlru[ci * 4:(ci + 1) * 4] = lo.transpose(3, 1, 2, 4, 0).reshape(NSEG, L, 2, 512)
    for b in range(2):
        y_sample[b] = np.asarray(R[4 + b]["yT"]).T
    if DEBUG:
        kernel.debug = [np.asarray(R[i]["dbg"]) for i in range(8)]
    return (y_prompt, y_sample, nk, nv, nhg, nlru)
```

```python
import numpy as np
from contextlib import ExitStack
import concourse.bass as bass
import concourse.mybir as mybir
from concourse.bass_utils import run_bass_kernel_spmd

F32 = mybir.dt.float32
BF16 = mybir.dt.bfloat16
AF = mybir.ActivationFunctionType
ALU = mybir.AluOpType

L = 2
D = 2048
NCH = 16
T = 1024
NSEG = 4
SEG = 256
DFF = 5632
DIN = 12800
HC = 32
NEGM = -30000.0
EPS = 1e-6
C_ZQ, C_ZFF, C_ZFB, C_ZI, C_ZO = 0, 512, 1024, 1536, 2048
C_NQ, C_NK, C_NV = 2560, 3584, 4608
C_LX, C_LG = 5632, 6144
C_GATE = 6656
QK_TILES = [(kb, qh) for qh in range(2) for kb in range(8) if (qh == 0 and kb <= 5) or (qh == 1 and kb >= 2)]
DEBUG = False
STAGE = 99
HSTOP = 99


class Tracker:
    def __init__(self, nc, es):
        self.nc = nc
        self.es = es
        self.eng = {'pe': nc.tensor, 'act': nc.scalar, 'dve': nc.vector, 'pool': nc.gpsimd, 'sp': nc.sync}
        self.sem = {}
        self.cnt = {}
        for e in ('pe', 'act', 'dve', 'pool'):
            self.sem[e] = es.enter_context(nc.semaphore("c_" + e))
            self.cnt[e] = 0
        self.waited = {e: {} for e in self.eng}
        self.bufs = {}
        self.dma_sems = {}
        self.all_dma_tokens = {}

    @staticmethod
    def _fix(reads, writes):
        r2, w2 = [], list(writes)
        for k in reads:
            if k == 'PT' or (isinstance(k, tuple) and k[0] == 'P'):
                w2.append(k)
            else:
                r2.append(k)
        return r2, w2

    def _deps(self, reads, writes):
        deps = []
        for r in reads:
            b = self.bufs.get(r)
            if b and b['w'] is not None:
                deps.append(b['w'])
        for w in writes:
            b = self.bufs.get(w)
            if b:
                if b['w'] is not None:
                    deps.append(b['w'])
                deps.extend(b['r'])
        return deps

    def _wait(self, engine, deps, sync_same=False, same_ok=False):
        best = {}
        for (sem, val, src, key) in deps:
            if src == engine and (engine == 'pe' or same_ok):
                continue
            if self.waited[engine].get(key, 0) >= val:
                continue
            if key not in best or best[key][1] < val:
                best[key] = (sem, val)
        for key, (sem, val) in best.items():
            self.eng[engine].wait_ge(sem, val)
            self.waited[engine][key] = val

    def _record(self, token, reads, writes):
        for r in reads:
            b = self.bufs.setdefault(r, {'w': None, 'r': []})
            b['r'].append(token)
            if len(b['r']) > 24:
                latest = {}
                for t in b['r']:
                    if t[3] not in latest or latest[t[3]][1] < t[1]:
                        latest[t[3]] = t
                b['r'] = list(latest.values())
        for w in writes:
            self.bufs[w] = {'w': token, 'r': []}

    def op(self, engine, fn, reads=(), writes=(), sync_same=False, same_ok=False):
        reads, writes = self._fix(reads, writes)
        self._wait(engine, self._deps(reads, writes), sync_same, same_ok)
        inst = fn(self.eng[engine])
        self.cnt[engine] += 1
        inst.then_inc(self.sem[engine], 1)
        token = (self.sem[engine], self.cnt[engine], engine, engine)
        self._record(token, reads, writes)
        return token

    def mm(self, fns, reads=(), writes=()):
        reads, writes = self._fix(reads, writes)
        self._wait('pe', self._deps(reads, writes))
        inst = None
        for fn in fns:
            inst = fn(self.eng['pe'])
        self.cnt['pe'] += 1
        inst.then_inc(self.sem['pe'], 1)
        token = (self.sem['pe'], self.cnt['pe'], 'pe', 'pe')
        self._record(token, reads, writes)
        return token

    def dma(self, queue, slot, out, in_, reads=(), writes=()):
        if slot not in self.dma_sems:
            self.dma_sems[slot] = [self.es.enter_context(self.nc.semaphore("d_" + slot)), 0]
        ds = self.dma_sems[slot]
        self._wait(queue, self._deps(reads, writes), sync_same=True)
        inst = self.eng[queue].dma_start(out=out, in_=in_)
        ds[1] += 16
        inst.then_inc(ds[0], 16)
        token = (ds[0], ds[1], 'dma', 'd_' + slot)
        self._record(token, reads, writes)
        self.all_dma_tokens['d_' + slot] = token
        return token

    def barrier(self):
        toks = [(self.sem[e], self.cnt[e], e, e) for e in ('pe', 'act', 'dve', 'pool') if self.cnt[e] > 0]
        toks += list(self.all_dma_tokens.values())
        for e in ('pe', 'act', 'dve', 'pool', 'sp'):
            self._wait(e, toks)

    def final_wait(self):
        self._wait('sp', list(self.all_dma_tokens.values()) +
                   [(self.sem[e], self.cnt[e], e, e) for e in ('pe', 'act', 'dve', 'pool') if self.cnt[e] > 0])


def build_nc():
    nc = bass.Bass("TRN2", target_bir_lowering=False)

    def din(name, shape):
        return nc.dram_tensor(name, list(shape), F32, kind="ExternalInput").ap()

    def dout(name, shape):
        return nc.dram_tensor(name, list(shape), F32, kind="ExternalOutput").ap()

    xT_d = din("xT", [D, T])
    cond_d = din("condT", [128, NCH])
    modw_d = din("mod_w", [L, D, 9 * D])
    modb_d = din("mod_bT", [128, L * 144])
    normg_d = din("normgT", [128, L * 3 * NCH])
    fng_d = din("fngT", [128, NCH])
    wup_d = [din("ffn1_w_up", [L, D, 2 * DFF]), din("ffn2_w_up", [L, D, 2 * DFF])]
    wdn_d = [din("ffn1_w_down", [L, DFF, D]), din("ffn2_w_down", [L, DFF, D])]
    win_d = din("w_in", [L, D, DIN])
    wpa_d = din("w_proj_a", [L, 512, D])
    wpb_d = din("w_proj_b", [L, 1024, D])
    wpc_d = din("w_proj_c", [L, 512, D])
    wout_d = din("w_out", [L, D, D])
    small_d = din("small", [128, 8])
    ident_d = din("ident", [128, 128])
    hmask_d = din("hmask", [2, 128, 128])
    rowmask_d = din("rowmask", [128, 4])
    hglog_d = din("hglogT", [128, L * 8])
    hgng_d = din("hgngT", [128, L * 4])
    sinit_d = din("s_init", [L, 2, 4, 128, NSEG * 128])
    kctx_d = din("kctxT", [L, 8, 128, 512])
    vctx_d = din("vctx", [L, 8, 128, 4 * 128])
    bias_d = din("na_bias", [L, 8, len(QK_TILES), 128, 512])
    lruv_d = din("lru_vecT", [128, L * 48])
    lrubd_d = din("lru_bd", [L, 2, 2, 4, 128, 128])
    h0_d = din("lru_h0T", [128, L * 2 * NSEG * 4])

    yT_o = dout("yT", [D, T])
    kT_o = dout("kT_out", [L, 1024, T])
    v_o = dout("v_out", [L, T, 1024])
    hg_o = dout("hg_out", [L, 2, 4, 128, NSEG * 128])
    lru_o = dout("lru_out", [128, L * 2 * NSEG * 4])
    dbg_o = dout("dbg", [8, D, T]) if DEBUG else None

    with ExitStack() as es:
        tk = Tracker(nc, es)

        uid = [0]

        def sb(name, shape, dt=F32, stack=es):
            uid[0] += 1
            return stack.enter_context(nc.sbuf_tensor("s%d_%s" % (uid[0], name), list(shape), dt))

        xT = sb("xT", [128, NCH, T])
        hT = sb("hT", [128, NCH, T], BF16)
        ones_bf = sb("ones_bf", [128, 128], BF16)
        ident_bf = sb("ident_bf", [128, 128], BF16)
        hmask = sb("hmask", [128, 2, 128])
        rowmask = sb("rowmask", [128, 4])
        modT = sb("modT", [128, L * 144])
        modb = sb("modb", [128, L * 144])
        normg = sb("normg", [128, L * 3 * NCH])
        fng = sb("fng", [128, NCH])
        Amod = sb("Amod", [128, L * 3 * NCH])
        gmod = sb("gmod", [128, L * 3 * NCH])
        small = sb("small", [128, 8])
        epst = sb("epst", [128, 1])
        condT = sb("condT", [128, NCH])
        cond_bf = sb("cond_bf", [128, NCH], BF16)
        hglog = sb("hglog", [128, L * 8])
        hgng = sb("hgng", [128, L * 4])
        lbt = sb("lbt", [128, L * 8])
        omlb = sb("omlb", [128, L * 8])
        lruv = sb("lruv", [128, L * 48])
        nsp = sb("nsp", [128, L * 16])
        h0t = sb("h0t", [128, L * 2 * NSEG * 4])
        lrufin = sb("lrufin", [128, L * 2 * NSEG * 4])
        rstd_box = [None]
        P = [es.enter_context(nc.psum_tensor("P%d" % i, [128, 512], F32)) for i in range(8)]
        PT = P[7][:, :].bitcast(BF16)

        carry = small[:, 0:1]
        ctxb = small[:, 1:2]

        tk.op('dve', lambda e: e.memset(ones_bf[:], 1.0), writes=['ones'])
        tk.op('dve', lambda e: e.memset(epst[:], EPS), writes=['eps'])
        tk.dma('pool', 'c0', ident_bf[:], ident_d, writes=['ident'])
        tk.dma('sp', 'c1', hmask[:], hmask_d.rearrange("a s t -> s a t"), writes=['hmask'])
        tk.dma('sp', 'c1', rowmask[:], rowmask_d, writes=['rowmask'])
        tk.dma('sp', 'c1', modb[:], modb_d, writes=['modb'])
        tk.dma('sp', 'c1', normg[:], normg_d, writes=['normg'])
        tk.dma('sp', 'c1', fng[:], fng_d, writes=['fng'])
        tk.dma('sp', 'c1', small[:], small_d, writes=['small'])
        tk.dma('sp', 'c1', condT[:], cond_d, writes=['condT'])
        tk.dma('sp', 'c1', hglog[:], hglog_d, writes=['hglog'])
        tk.dma('sp', 'c1', hgng[:], hgng_d, writes=['hgng'])
        tk.dma('sp', 'c1', lruv[:], lruv_d, writes=['lruv'])
        tk.dma('sp', 'c1', h0t[:], h0_d, writes=['h0t'])
        for c in range(NCH):
            tk.dma('sp', 'c2', xT[:, c, :], xT_d[c * 128:(c + 1) * 128, :], writes=[('x', c, 0), ('x', c, 1)])

        tk.barrier()
        tk.op('dve', lambda e: e.memset(lbt[:, 0:8], 0.0), writes=['lbt0'])
        tk.op('dve', lambda e: e.tensor_tensor(out=lbt[:, 8:16], in0=hglog[:, 8:16], in1=hglog[:, 0:8], op=ALU.subtract),
              reads=['hglog'], writes=['lbt1'])
        tk.op('act', lambda e: e.activation(out=lbt[:, 8:16], in_=lbt[:, 8:16], func=AF.Sigmoid), reads=['lbt1'], writes=['lbt1'])
        tk.op('dve', lambda e: e.tensor_scalar(out=omlb[:], in0=lbt[:], scalar1=-1.0, scalar2=1.0, op0=ALU.mult, op1=ALU.add),
              reads=['lbt0', 'lbt1'], writes=['omlb'])
        for l in range(L):
            lam = lruv[:, l * 48 + 36: l * 48 + 44]
            tk.op('act', lambda e: e.activation(out=nsp[:, l * 16: l * 16 + 8], in_=lam, func=AF.Exp, scale=-1.0),
                  reads=['lruv'], writes=[('nsp', l)])
            tk.op('act', lambda e: e.activation(out=nsp[:, l * 16: l * 16 + 8], in_=nsp[:, l * 16: l * 16 + 8], func=AF.Ln, bias=1.0),
                  reads=[('nsp', l)], writes=[('nsp', l)], sync_same=True)
            tk.op('dve', lambda e: e.tensor_scalar(out=nsp[:, l * 16 + 8: l * 16 + 16], in0=nsp[:, l * 16: l * 16 + 8],
                                                   scalar1=-16.0, scalar2=None, op0=ALU.mult),
                  reads=[('nsp', l)], writes=[('nsp2', l)])
            tk.op('dve', lambda e: e.tensor_scalar(out=nsp[:, l * 16: l * 16 + 8], in0=nsp[:, l * 16: l * 16 + 8],
                                                   scalar1=-8.0, scalar2=None, op0=ALU.mult),
                  reads=[('nsp', l), ('nsp2', l)], writes=[('nsp', l)])

        tk.op('act', lambda e: e.activation(out=cond_bf[:], in_=condT[:], func=AF.Silu), reads=['condT'], writes=['cond_bf'])
        MODQ = [(l_, pn_) for l_ in range(L) for pn_ in range(36)]
        mod_next = [0]
        mod_cnt = [0]

        def mod_issue(mw_list):
            l_, pn_ = MODQ[mod_next[0]]
            mod_next[0] += 1
            s_ = mod_cnt[0] % len(mw_list)
            mod_cnt[0] += 1
            wv = modw_d[l_].rearrange("(k p) f -> p k f", p=128)
            tk.dma('pool', 'mw%d' % s_, mw_list[s_][:], wv[:, :, pn_ * 512:(pn_ + 1) * 512], writes=[('mw', s_)])
            return (l_, pn_, s_, mw_list[s_])

        def mod_compute(hd_):
            l_, pn_, s_, mwt = hd_
            base = l_ * 144 + pn_ * 4
            fns = []
            for n in range(4):
                for k in range(NCH):
                    fns.append(lambda e, n=n, k=k: e.matmul(
                        P[7][:, base + n:base + n + 1], lhsT=mwt[:, k, n * 128:(n + 1) * 128], rhs=cond_bf[:, k:k + 1],
                        start=(k == 0), stop=(k == NCH - 1)))
            tk.mm(fns, reads=[('mw', s_), 'cond_bf'], writes=[('P', 7)])
            tk.op('dve', lambda e: e.tensor_tensor(out=modT[:, base:base + 4], in0=P[7][:, base:base + 4],
                                                   in1=modb[:, base:base + 4], op=ALU.add),
                  reads=[('P', 7), 'modb'], writes=[('modT', l_)])
            if pn_ % 12 in (7, 11):
                s3 = pn_ // 12
                o = (l_ * 3 + s3) * NCH
                sc = modT[:, l_ * 144 + (3 * s3 + 1) * NCH: l_ * 144 + (3 * s3 + 2) * NCH]
                gt = modT[:, l_ * 144 + (3 * s3 + 2) * NCH: l_ * 144 + (3 * s3 + 3) * NCH]
                if pn_ % 12 == 7:
                    tk.op('dve', lambda e: e.scalar_tensor_tensor(out=Amod[:, o:o + NCH], in0=sc, scalar=1.0,
                                                                  in1=normg[:, o:o + NCH], op0=ALU.add, op1=ALU.mult),
                          reads=[('modT', l_), 'normg'], writes=[('Amod', l_, s3)])
                else:
                    tk.op('dve', lambda e: e.tensor_scalar(out=gmod[:, o:o + NCH], in0=gt,
                                                           scalar1=(1.0 if s3 == 1 else 0.5), scalar2=None, op0=ALU.mult),
                          reads=[('modT', l_)], writes=[('gmod', l_, s3)])

        with ExitStack() as ms:
            mw = [sb("mw%d" % i, [128, NCH, 512], BF16, ms) for i in range(2)]
            hprev = mod_issue(mw)
            for i in range(8):
                hnext = mod_issue(mw) if i + 1 < 8 else None
                mod_compute(hprev)
                hprev = hnext
            tk.barrier()

        def shift_ap(l, s3, c):
            o = l * 144 + (3 * s3) * NCH + c
            return modT[:, o:o + 1]

        def A_ap(l, s3, c):
            o = (l * 3 + s3) * NCH + c
            return Amod[:, o:o + 1]

        def g_ap(l, s3, c):
            o = (l * 3 + s3) * NCH + c
            return gmod[:, o:o + 1]

        HS = [slice(0, 512), slice(512, 1024)]

        def sumsq_rstd(nfeat_scale):
            rstd = rstd_box[0]
            for h in range(2):
                tk.op('act', lambda e, h=h: e.activation(out=hT[:, :, HS[h]], in_=xT[:, :, HS[h]], func=AF.Square),
                      reads=[('x', c, h) for c in range(NCH)], writes=[('h', c, h) for c in range(NCH)])
            for h in range(2):
                fns = [lambda e, k=k, h=h: e.matmul(P[h][:, :], lhsT=ones_bf[:, :], rhs=hT[:, k, HS[h]],
                                                    start=(k == 0), stop=(k == NCH - 1)) for k in range(NCH)]
                tk.mm(fns, reads=[('h', c, h) for c in range(NCH)] + ['ones'], writes=[('P', h)])
                tk.op('act', lambda e, h=h: e.activation(out=rstd[:, HS[h]], in_=P[h][:, :], func=AF.Ln,
                                                         bias=epst[:, 0:1], scale=nfeat_scale),
                      reads=[('P', h), 'eps'], writes=[('rstd', h)])
                tk.op('act', lambda e, h=h: e.activation(out=rstd[:, HS[h]], in_=rstd[:, HS[h]], func=AF.Exp, scale=-0.5),
                      reads=[('rstd', h)], writes=[('rstd', h)])

        def modnorm(l, s3):
          with ExitStack() as ns_:
            rstd = sb("rstd", [128, T], F32, ns_)
            rstd_box[0] = rstd
            sumsq_rstd(1.0 / D)
            tmpn = [sb("tmpn0", [128, 512], F32, ns_), sb("tmpn1", [128, 512], F32, ns_)]
            i = 0
            for h in range(2):
                for c in range(NCH):
                    tb = i % 2
                    i += 1
                    tk.op('dve', lambda e, c=c, h=h, tb=tb: e.scalar_tensor_tensor(
                        out=tmpn[tb][:, :], in0=xT[:, c, HS[h]], scalar=A_ap(l, s3, c), in1=rstd[:, HS[h]],
                        op0=ALU.mult, op1=ALU.mult),
                        reads=[('x', c, h), ('rstd', h), ('Amod', l, s3)], writes=[('tmpn', tb)])
                    tk.op('act', lambda e, c=c, h=h, tb=tb: e.activation(
                        out=hT[:, c, HS[h]], in_=tmpn[tb][:, :], func=AF.Identity, bias=shift_ap(l, s3, c), scale=1.0),
                        reads=[('tmpn', tb), ('modT', l)], writes=[('h', c, h)])
            tk.barrier()

        def dbg_dump(idx):
            if DEBUG:
                for c in range(NCH):
                    tk.dma('sp', 'dbg', dbg_o[idx, c * 128:(c + 1) * 128, :], xT[:, c, :],
                           reads=[('x', c, 0), ('x', c, 1)])

        def ffn(l, s3, wup, wdn, nmods=0):
            modnorm(l, s3)
            NG = 22
            with ExitStack() as fs:
                wa = [sb("wa%d" % i, [128, NCH, 256], BF16, fs) for i in range(2)]
                wu = [sb("wu%d" % i, [128, NCH, 256], BF16, fs) for i in range(2)]
                wd = [sb("wd%d" % i, [128, 2, D], BF16, fs) for i in range(3)]
                hid = [sb("hid%d" % i, [128, 2, T], BF16, fs) for i in range(2)]
                sa = [sb("sa%d" % i, [128, 512], F32, fs) for i in range(2)]
                mwf = [sb("mwf%d" % i, [128, NCH, 512], BF16, fs) for i in range(2)] if nmods > 0 else None
                mods_left = [nmods]
                wupv = wup[l].rearrange("(k p) f -> p k f", p=128)

                def load(g):
                    s2, s3_ = g % 2, g % 3
                    tk.dma('pool', 'wa%d' % s2, wa[s2][:], wupv[:, :, g * 256:(g + 1) * 256], writes=[('wa', s2)])
                    tk.dma('pool', 'wu%d' % s2, wu[s2][:], wupv[:, :, DFF + g * 256: DFF + (g + 1) * 256], writes=[('wu', s2)])
                    tk.dma('pool', 'wd%d' % s3_, wd[s3_][:],
                           wdn[l][g * 256:(g + 1) * 256, :].rearrange("(k p) d -> p k d", p=128), writes=[('wd', s3_)])

                cnt = [0, 0]

                def up(g):
                    s2 = g % 2
                    for h in range(2):
                        for j in range(2):
                            pa = cnt[0] % 2
                            cnt[0] += 1
                            tk.mm([lambda e, k=k, j=j, h=h, pa=pa: e.matmul(
                                P[pa][:, :], lhsT=wa[s2][:, k, j * 128:(j + 1) * 128], rhs=hT[:, k, HS[h]],
                                start=(k == 0), stop=(k == NCH - 1)) for k in range(NCH)],
                                reads=[('wa', s2)] + [('h', c, h) for c in range(NCH)], writes=[('P', pa)])
                            tk.mm([lambda e, k=k, j=j, h=h, pa=pa: e.matmul(
                                P[2 + pa][:, :], lhsT=wu[s2][:, k, j * 128:(j + 1) * 128], rhs=hT[:, k, HS[h]],
                                start=(k == 0), stop=(k == NCH - 1)) for k in range(NCH)],
                                reads=[('wu', s2)] + [('h', c, h) for c in range(NCH)], writes=[('P', 2 + pa)])
                            tk.op('act', lambda e, pa=pa: e.activation(out=sa[pa][:, :], in_=P[pa][:, :], func=AF.Silu),
                                  reads=[('P', pa)], writes=[('sa', pa)])
                            tk.op('dve', lambda e, pa=pa, j=j, h=h: e.tensor_tensor(
                                out=hid[s2][:, j, HS[h]], in0=sa[pa][:, :], in1=P[2 + pa][:, :], op=ALU.mult),
                                reads=[('sa', pa), ('P', 2 + pa)], writes=[('hid', s2, j, h)])

                def down(g):
                    s2, s3_ = g % 2, g % 3
                    for h in range(2):
                        for dc in range(NCH):
                            pd = 4 + cnt[1] % 3
                            cnt[1] += 1
                            tk.mm([lambda e, k=k, dc=dc, h=h, pd=pd: e.matmul(
                                P[pd][:, :], lhsT=wd[s3_][:, k, dc * 128:(dc + 1) * 128], rhs=hid[s2][:, k, HS[h]],
                                start=(k == 0), stop=(k == 1)) for k in range(2)],
                                reads=[('wd', s3_), ('hid', s2, 0, h), ('hid', s2, 1, h)], writes=[('P', pd)])
                            tk.op('dve', lambda e, dc=dc, h=h, pd=pd: e.scalar_tensor_tensor(
                                out=xT[:, dc, HS[h]], in0=P[pd][:, :], scalar=g_ap(l, s3, dc), in1=xT[:, dc, HS[h]],
                                op0=ALU.mult, op1=ALU.add),
                                reads=[('P', pd), ('x', dc, h), ('gmod', l, s3)], writes=[('x', dc, h)])

                load(0)
                for g in range(NG + 1):
                    if g + 1 < NG:
                        load(g + 1)
                    mh = []
                    if g < NG and mods_left[0] > 0:
                        kq = min(2, mods_left[0]) if g < 2 else min(2, -(-mods_left[0] // (NG - g)))
                        for _ in range(kq):
                            mh.append(mod_issue(mwf))
                        mods_left[0] -= kq
                    if g < NG:
                        up(g)
                    for hd_ in mh:
                        mod_compute(hd_)
                    if g >= 1:
                        down(g - 1)
                assert mods_left[0] == 0
                tk.barrier()

        def load_thin(wt, slotname, key, src_cols):
            tk.dma('pool', slotname, wt[:], src_cols, writes=[key])

        def mixer(l):
            modnorm(l, 1)
            winv = win_d[l].rearrange("(k p) f -> p k f", p=128)
            with ExitStack() as mx:
                oabc = sb("oabc", [128, 16, T], BF16, mx)
                class Panels:
                    def __init__(self, name, n, width, stack):
                        self.t = [sb(name + str(i), [128, NCH, width], BF16, stack) for i in range(n)]
                        self.n, self.c, self.name = n, 0, name

                    def load(self, col0, ncols):
                        s_ = self.c % self.n
                        self.c += 1
                        tk.dma('pool', '%s%d' % (self.name, s_), self.t[s_][:, :, 0:ncols], winv[:, :, col0:col0 + ncols],
                               writes=[(self.name, s_)])
                        return s_

                    def feat(self, s_, coff, h, pb):
                        w = self.t[s_]
                        tk.mm([lambda e, k=k: e.matmul(P[pb][:, :], lhsT=w[:, k, coff:coff + 128], rhs=hT[:, k, HS[h]],
                                                       start=(k == 0), stop=(k == NCH - 1)) for k in range(NCH)],
                              reads=[(self.name, s_)] + [('h', c, h) for c in range(NCH)], writes=[('P', pb)])

                    def tok(self, s_, coff, blk4, pb):
                        w = self.t[s_]
                        fns = []
                        for j in range(4):
                            b_ = blk4 * 4 + j
                            for k in range(NCH):
                                fns.append(lambda e, k=k, j=j, b_=b_: e.matmul(
                                    P[pb][:, j * 128:(j + 1) * 128], lhsT=hT[:, k, b_ * 128:(b_ + 1) * 128], rhs=w[:, k, coff:coff + 128],
                                    start=(k == 0), stop=(k == NCH - 1)))
                        tk.mm(fns, reads=[(self.name, s_)] + [('h', c, blk4) for c in range(NCH)], writes=[('P', pb)])

                with ExitStack() as hs:
                  if STAGE >= 2:
                      hp = Panels("hp", 2, 256, hs)
                      q = sb("hq", [128, T], BF16, hs)
                      vtok = sb("hvtok", [128, 5, 8, 128], BF16, hs)
                      ft = sb("hf", [128, T], F32, hs)
                      bt_ = sb("hb", [128, T], F32, hs)
                      ex = sb("hex", [128, T], F32, hs)
                      Sbf = [sb("hSbf%d" % i, [128, 128], BF16, hs) for i in range(2)]
                      qtb = sb("hqtb", [128, T], BF16, hs)
                      ktb = sb("hktb", [128, T], BF16, hs)
                      khT = sb("hkhT", [128, T], BF16, hs)
                      khtok = sb("hkhtok", [128, 8, 128], BF16, hs)
                      atm = sb("hatm", [128, 8, 128], BF16, hs)
                      oacc = sb("hoacc", [128, T], F32, hs)
                      dec = sb("hdec", [128, T // HC], F32, hs)
                      cmask = sb("hcm0", [128, T], F32, hs)
                      S = [sb("hS%d" % i, [128, 128], F32, hs) for i in range(3)]
                      sini = sb("hsini", [128, NSEG * 128], F32, hs)
                      sstage = sb("hsstage", [128, NSEG * 128], F32, hs)
                      tk.op('dve', lambda e: e.memset(cmask[:], 1.0), writes=['cm0'])
                      tk.op('dve', lambda e: e.memset(cmask[:, 0:T:HC], 0.0), writes=['cm0'], sync_same=True)
                      PIK = []

                      for hh in range(4 if HSTOP >= 99 else 1):
                          sA = hp.load(hh * 640, 256)
                          sB = hp.load(hh * 640 + 256, 256)
                          for h in range(2):
                              hp.feat(sA, 0, h, h)
                              tk.op('act', lambda e, h=h: e.activation(out=q[:, HS[h]], in_=P[h][:, :], func=AF.Silu),
                                    reads=[('P', h)], writes=[('hq', h)])
                          if HSTOP <= 1:
                              continue
                          for b4 in range(2):
                              hp.tok(sA, 128, b4, 4 + b4)
                              src = P[4 + b4][:, :].rearrange("p (j e) -> p j e", e=128)
                              tk.op('act', lambda e, b4=b4, src=src: e.activation(
                                  out=vtok[:, 0, b4 * 4:(b4 + 1) * 4, :], in_=src, func=AF.Copy),
                                  reads=[('P', 4 + b4)], writes=[('vtok', 0, b4)])
                              for m in range(4):
                                  tk.op('dve', lambda e, b4=b4, m=m, src=src: e.tensor_scalar(
                                      out=vtok[:, 1 + m, b4 * 4:(b4 + 1) * 4, :], in0=src, scalar1=rowmask[:, m:m + 1],
                                      scalar2=None, op0=ALU.mult),
                                      reads=[('P', 4 + b4), 'rowmask'], writes=[('vtok', 1 + m, b4)])
                          if HSTOP <= 2:
                              continue
                          for d in range(2 if HSTOP >= 99 else 1):
                              li = l * 8 + d * 4 + hh
                              if d == 0:
                                  sC = hp.load(hh * 640 + 512, 128)
                              for h in range(2):
                                  hp.feat(sB, 128 * d, h, h)
                                  tk.op('act', lambda e, h=h: e.activation(out=ft[:, HS[h]], in_=P[h][:, :], func=AF.Sigmoid),
                                        reads=[('P', h)], writes=['hf'])
                              tk.op('dve', lambda e, li=li: e.tensor_scalar(out=ft[:, :], in0=ft[:, :], scalar1=omlb[:, li:li + 1],
                                                                            scalar2=lbt[:, li:li + 1], op0=ALU.mult, op1=ALU.add),
                                    reads=['hf', 'omlb', 'lbt0', 'lbt1'], writes=['hf'])
                              tk.op('act', lambda e: e.activation(out=ex[:, :], in_=ft[:, :], func=AF.Ln), reads=['hf'], writes=['hex'])
                              tk.op('dve', lambda e: e.tensor_scalar(out=ft[:, :], in0=ft[:, :], scalar1=-1.0, scalar2=1.0,
                                                                     op0=ALU.mult, op1=ALU.add), reads=['hf'], writes=['hf'])
                              if d == 0:
                                  tk.op('dve', lambda e: e.tensor_tensor_scan(out=bt_[:, :], data0=cmask[:, :], data1=ex[:, :],
                                                                              initial=0.0, op0=ALU.mult, op1=ALU.add),
                                        reads=['hex', 'cm0'], writes=['hb'])
                                  blast = bt_[:, :].rearrange("p (c i) -> p c i", i=HC)[:, :, HC - 1:HC]
                              else:
                                  tk.op('dve', lambda e: e.tensor_tensor_scan(out=bt_[:, ::-1], data0=cmask[:, :], data1=ex[:, ::-1],
                                                                              initial=0.0, op0=ALU.mult, op1=ALU.add),
                                        reads=['hex', 'cm0'], writes=['hb'])
                                  blast = bt_[:, :].rearrange("p (c i) -> p c i", i=HC)[:, :, 0:1]
                              tk.op('act', lambda e: e.activation(out=ex[:, :], in_=bt_[:, :], func=AF.Exp), reads=['hb'], writes=['hex'])
                              tk.op('dve', lambda e: e.tensor_tensor(out=qtb[:, :], in0=q[:, :], in1=ex[:, :], op=ALU.mult),
                                    reads=['hex', ('hq', 0), ('hq', 1)], writes=['hqtb'])
                              tk.op('act', lambda e, blast=blast: e.activation(out=dec[:, :].rearrange("p (c o) -> p c o", o=1), in_=blast, func=AF.Exp),
                                    reads=['hb'], writes=['hdec'])
                              tk.op('act', lambda e: e.activation(out=ex[:, :], in_=bt_[:, :], func=AF.Exp, scale=-1.0),
                                    reads=['hb'], writes=['hex'])
                              tk.op('dve', lambda e: e.tensor_tensor(out=ktb[:, :], in0=ft[:, :], in1=ex[:, :], op=ALU.mult),
                                    reads=['hex', 'hf'], writes=['hktb'])
                              tk.op('dve', lambda e: e.tensor_tensor(
                                  out=khT[:, :].rearrange("p (c i) -> p c i", i=HC),
                                  in0=ktb[:, :].rearrange("p (c i) -> p c i", i=HC),
                                  in1=dec[:, :].rearrange("p (c o) -> p c o", o=1).to_broadcast([128, T // HC, HC]), op=ALU.mult),
                                  reads=['hktb', 'hdec'], writes=['hkhT'], sync_same=True)
                              if HSTOP <= 3:
                                  continue
                              tk.mm([lambda e, b=b: e.transpose(PT[:, b * 128:(b + 1) * 128], khT[:, b * 128:(b + 1) * 128], ident_bf[:, :])
                                     for b in range(8)], reads=['hkhT', 'ident'], writes=['PT'])
                              tk.op('act', lambda e: e.activation(out=khtok[:, :, :], in_=PT[:, :].rearrange("p (b d) -> p b d", d=128), func=AF.Copy),
                                    reads=['PT'], writes=['hkhtok'])
                              if HSTOP <= 4:
                                  continue
                              for b4 in range(2):
                                  tk.mm([lambda e, j=j, b4=b4: e.matmul(
                                      P[b4][:, j * 128:(j + 1) * 128], lhsT=ktb[:, (b4 * 4 + j) * 128:(b4 * 4 + j + 1) * 128],
                                      rhs=qtb[:, (b4 * 4 + j) * 128:(b4 * 4 + j + 1) * 128], start=True, stop=True) for j in range(4)],
                                      reads=['hktb', 'hqtb'], writes=[('P', b4)])
                                  tk.op('dve', lambda e, b4=b4, d=d: e.tensor_tensor(
                                      out=atm[:, b4 * 4:(b4 + 1) * 4, :], in0=P[b4][:, :].rearrange("p (j t) -> p j t", t=128),
                                      in1=hmask[:, d:d + 1, :].to_broadcast([128, 4, 128]), op=ALU.mult),
                                      reads=[('P', b4), 'hmask'], writes=[('hatm', b4)])
                                  tk.mm([lambda e, j=j, b4=b4: e.matmul(
                                      P[4 + b4][:, j * 128:(j + 1) * 128], lhsT=vtok[:, 0, b4 * 4 + j, :], rhs=atm[:, b4 * 4 + j, :],
                                      start=True, stop=True) for j in range(4)],
                                      reads=[('hatm', b4), ('vtok', 0, b4)], writes=[('P', 4 + b4)])
                                  if d == 0:
                                      tk.op('act', lambda e, b4=b4: e.activation(out=oacc[:, HS[b4]], in_=P[4 + b4][:, :], func=AF.Copy),
                                            reads=[('P', 4 + b4)], writes=[('hoacc', b4)])
                                  else:
                                      tk.op('dve', lambda e, b4=b4: e.tensor_tensor(out=oacc[:, HS[b4]], in0=oacc[:, HS[b4]], in1=P[4 + b4][:, :], op=ALU.add),
                                            reads=[('P', 4 + b4), ('hoacc', b4)], writes=[('hoacc', b4)])
                              if HSTOP <= 5:
                                  continue
                              tk.dma('sp', 'sini', sini[:], sinit_d[l, d, hh], writes=['hsini'])
                              tk._wait('pe', tk._deps((), [('P', 2), ('P', 3)] + PIK))
                              cur = 0
                              nchk = T // HC
                              cps = SEG // HC
                              order = list(range(nchk)) if d == 0 else list(range(nchk - 1, -1, -1))
                              KB = [6, 5, 4, 0]
                              LA = 3

                              def emit_kv(step_):
                                  c_ = order[step_]
                                  b_ = (c_ * HC) // 128
                                  m_ = ((c_ * HC) % 128) // HC
                                  kb_ = KB[step_ % 4]
                                  tk.mm([lambda e: e.matmul(P[kb_][:, 0:128], lhsT=khtok[:, b_, :], rhs=vtok[:, 1 + m_, b_, :],
                                                            start=True, stop=True)],
                                        reads=['hkhtok', ('vtok', 1 + m_, b_ // 4)], writes=[('P', kb_)])

                              for s0_ in range(LA):
                                  emit_kv(s0_)
                              for step, c in enumerate(order):
                                  seg = c // cps
                                  if step + LA < nchk:
                                      emit_kv(step + LA)
                                  if step % cps == 0:
                                      if step == 0:
                                          tk.op('dve', lambda e, seg=seg: e.tensor_copy(out=S[cur][:, :], in_=sini[:, seg * 128:(seg + 1) * 128]),
                                                reads=['hsini'], writes=[('hS', cur)])
                                      else:
                                          nxt = (cur + 1) % 3
                                          tk.op('dve', lambda e, seg=seg, cur=cur, nxt=nxt: e.scalar_tensor_tensor(
                                              out=S[nxt][:, :], in0=S[cur][:, :], scalar=carry, in1=sini[:, seg * 128:(seg + 1) * 128],
                                              op0=ALU.mult, op1=ALU.add), reads=[('hS', cur), 'hsini', 'small'], writes=[('hS', nxt)])
                                          cur = nxt
                                  sbi = step % 2
                                  tk.op('act', lambda e, cur=cur, sbi=sbi: e.activation(out=Sbf[sbi][:, :], in_=S[cur][:, :], func=AF.Copy),
                                        reads=[('hS', cur)], writes=[('hSbf', sbi)])
                                  pi = 2 + (c * HC) // 512
                                  col = (c * HC) % 512
                                  tk.mm([lambda e, c=c, sbi=sbi, pi=pi, col=col: e.matmul(
                                      P[pi][:, col:col + HC], lhsT=Sbf[sbi][:, :], rhs=qtb[:, c * HC:(c + 1) * HC], start=True, stop=True)],
                                      reads=[('hSbf', sbi), 'hqtb'], writes=[('P', pi)])
                                  kb = KB[step % 4]
                                  nxt = (cur + 1) % 3
                                  tk.op('dve', lambda e, c=c, cur=cur, nxt=nxt, kb=kb: e.scalar_tensor_tensor(
                                      out=S[nxt][:, :], in0=S[cur][:, :], scalar=dec[:, c:c + 1], in1=P[kb][:, 0:128],
                                      op0=ALU.mult, op1=ALU.add), reads=[('hS', cur), ('P', kb), 'hdec'], writes=[('hS', nxt)], same_ok=True)
                                  cur = nxt
                                  if step % cps == cps - 1:
                                      tk.op('dve', lambda e, seg=seg, cur=cur: e.tensor_copy(out=sstage[:, seg * 128:(seg + 1) * 128], in_=S[cur][:, :]),
                                            reads=[('hS', cur)], writes=['hsstage'])
                              tk.dma('sp', 'hgo', hg_o[l, d, hh], sstage[:], reads=['hsstage'])
                              for h in range(2):
                                  pk = []
                                  tk.op('dve', lambda e, h=h: e.tensor_tensor(out=oacc[:, HS[h]], in0=oacc[:, HS[h]], in1=P[2 + h][:, :], op=ALU.add),
                                        reads=pk + [('hoacc', h)], writes=[('hoacc', h), ('P', 2 + h)] + pk)
                          if HSTOP <= 6:
                              continue
                          for h in range(2):
                              hp.feat(sC, 0, h, 2 + h)
                              tk.op('act', lambda e, h=h: e.activation(out=ft[:, HS[h]], in_=P[2 + h][:, :], func=AF.Silu),
                                    reads=[('P', 2 + h)], writes=['hf'])
                          for h in range(2):
                              tk.op('act', lambda e, h=h: e.activation(out=khT[:, HS[h]], in_=oacc[:, HS[h]], func=AF.Square),
                                    reads=[('hoacc', h)], writes=['hkhT'])
                              tk.mm([lambda e, h=h: e.matmul(P[h][:, :], lhsT=ones_bf[:, :], rhs=khT[:, HS[h]], start=True, stop=True)],
                                    reads=['hkhT', 'ones'], writes=[('P', h)])
                              tk.op('act', lambda e, h=h: e.activation(out=ex[:, HS[h]], in_=P[h][:, :], func=AF.Ln, bias=epst[:, 0:1], scale=1.0 / 128),
                                    reads=[('P', h), 'eps'], writes=['hex'])
                              tk.op('act', lambda e, h=h: e.activation(out=ex[:, HS[h]], in_=ex[:, HS[h]], func=AF.Exp, scale=-0.5), reads=['hex'], writes=['hex'])
                              tk.op('dve', lambda e, h=h, hh=hh: e.scalar_tensor_tensor(
                                  out=oacc[:, HS[h]], in0=oacc[:, HS[h]], scalar=hgng[:, l * 4 + hh: l * 4 + hh + 1], in1=ex[:, HS[h]],
                                  op0=ALU.mult, op1=ALU.mult), reads=['hex', ('hoacc', h), 'hgng'], writes=[('hoacc', h)])
                              tk.op('dve', lambda e, h=h, hh=hh: e.tensor_tensor(out=oabc[:, hh, HS[h]], in0=oacc[:, HS[h]], in1=ft[:, HS[h]], op=ALU.mult),
                                    reads=[('hoacc', h), 'hf'], writes=[('oabc', hh, h)])
                      tk.barrier()

                with ExitStack() as ns:
                  if STAGE >= 3:
                      npn = Panels("np", 2, 384, ns)
                      nsl = {0: npn.load(2560, 384)}
                      qT = sb("nqT", [128, T], BF16, ns)
                      kTb = sb("nkT", [128, T], BF16, ns)
                      kTf = sb("nkTf", [128, T], F32, ns)
                      vt = sb("nvt", [128, 8, 128], BF16, ns)
                      vtf = sb("nvtf", [128, 8, 128], F32, ns)
                      kc = sb("nkc", [128, 512], BF16, ns)
                      vc = sb("nvc", [128, 4, 128], BF16, ns)
                      bias = sb("nbias", [128, len(QK_TILES), 512], BF16, ns)
                      stmp = [sb("nst%d" % i, [128, 512], F32, ns) for i in range(3)]
                      pT = [sb("npT%d" % i, [128, 512], BF16, ns) for i in range(4)]
                      SBK = [0, 1, 6]
                      rden = sb("nrden", [128, 512], F32, ns)
                      pc = [0, 0]
                      for hd in range(8):
                          tk.dma('pool', 'nb', bias[:], bias_d[l, hd].rearrange("t p q -> p t q"), writes=['nbias'])
                          tk.dma('pool', 'nkc', kc[:], kctx_d[l, hd], writes=['nkc'])
                          tk.dma('pool', 'nvc', vc[:], vctx_d[l, hd].rearrange("p (b e) -> p b e", e=128), writes=['nvc'])
                          if hd + 1 < 8:
                              nsl[hd + 1] = npn.load(2560 + (hd + 1) * 384, 384)
                          s = nsl[hd]
                          for h in range(2):
                              npn.feat(s, 0, h, h)
                              tk.op('act', lambda e, h=h: e.activation(out=qT[:, HS[h]], in_=P[h][:, :], func=AF.Copy),
                                    reads=[('P', h)], writes=[('nq', h)])
                          for h in range(2):
                              npn.feat(s, 128, h, 2 + h)
                              tk.op('act', lambda e, h=h: e.activation(out=kTb[:, HS[h]], in_=P[2 + h][:, :], func=AF.Copy),
                                    reads=[('P', 2 + h)], writes=[('nk', h)])
                              tk.op('dve', lambda e, h=h: e.tensor_copy(out=kTf[:, HS[h]], in_=P[2 + h][:, :]),
                                    reads=[('P', 2 + h)], writes=[('nkf', h)])
                          tk.dma('sp', 'ko', kT_o[l, hd * 128:(hd + 1) * 128, :], kTf[:], reads=[('nkf', 0), ('nkf', 1)])
                          for b4 in range(2):
                              npn.tok(s, 256, b4, 4 + b4)
                              src = P[4 + b4][:, :].rearrange("p (j e) -> p j e", e=128)
                              tk.op('act', lambda e, b4=b4, src=src: e.activation(out=vt[:, b4 * 4:(b4 + 1) * 4, :], in_=src, func=AF.Copy),
                                    reads=[('P', 4 + b4)], writes=[('nv', b4)])
                              tk.op('dve', lambda e, b4=b4, src=src: e.tensor_copy(out=vtf[:, b4 * 4:(b4 + 1) * 4, :], in_=src),
                                    reads=[('P', 4 + b4)], writes=[('nvf', b4)])
                          tk.dma('sp', 'vo', v_o[l].rearrange("(b p) f -> p b f", p=128)[:, :, hd * 128:(hd + 1) * 128], vtf[:],
                                 reads=[('nvf', 0), ('nvf', 1)])
                          for qh in range(2):
                              tiles = [(i, kb) for i, (kb, qh2) in enumerate(QK_TILES) if qh2 == qh]
                              items = [('loc', i, kb) for (i, kb) in tiles] + [('ctx', None, cb) for cb in range(4)]
                              n_it = len(items)
                              slots_ = []
                              for ii in range(n_it):
                                  slots_.append((pc[0] % 3, pc[1] % 4))
                                  pc[0] += 1
                                  pc[1] += 1
                              A0, A1 = (2, 3) if qh == 0 else (4, 5)

                              def qk(ii):
                                  kind, bi, kb = items[ii]
                                  ps = SBK[slots_[ii][0]]
                                  if kind == 'loc':
                                      tk.mm([lambda e: e.matmul(P[ps][:, :], lhsT=kTb[:, kb * 128:(kb + 1) * 128], rhs=qT[:, HS[qh]],
                                                                start=True, stop=True)],
                                            reads=[('nk', kb // 4), ('nq', qh)], writes=[('P', ps)])
                                  else:
                                      tk.mm([lambda e: e.matmul(P[ps][:, :], lhsT=kc[:, kb * 128:(kb + 1) * 128], rhs=qT[:, HS[qh]],
                                                                start=True, stop=True)],
                                            reads=['nkc', ('nq', qh)], writes=[('P', ps)])

                              qk(0)
                              qk(1)
                              for ii, (kind, bi, kb) in enumerate(items):
                                  si_, pp = slots_[ii]
                                  ps = SBK[si_]
                                  if ii + 2 < n_it:
                                      qk(ii + 2)
                                  if kind == 'loc':
                                      tk.op('dve', lambda e, ps=ps, bi=bi, si_=si_: e.scalar_tensor_tensor(
                                          out=stmp[si_][:, :], in0=P[ps][:, :], scalar=128.0 ** -0.5, in1=bias[:, bi, :], op0=ALU.mult, op1=ALU.add),
                                          reads=[('P', ps), 'nbias'], writes=[('nst', si_)])
                                      tk.op('act', lambda e, si_=si_, pp=pp: e.activation(out=pT[pp][:, :], in_=stmp[si_][:, :], func=AF.Exp),
                                            reads=[('nst', si_)], writes=[('npT', pp)])
                                      vl = vt[:, kb, :]
                                      vkey = ('nv', kb // 4)
                                  else:
                                      tk.op('act', lambda e, ps=ps, pp=pp: e.activation(out=pT[pp][:, :], in_=P[ps][:, :], func=AF.Exp,
                                                                                        bias=ctxb, scale=128.0 ** -0.5),
                                            reads=[('P', ps), 'small'], writes=[('npT', pp)])
                                      vl = vc[:, kb, :]
                                      vkey = 'nvc'
                                  tk.mm([lambda e, vl=vl, pp=pp, ii=ii: e.matmul(P[A0][:, :], lhsT=vl, rhs=pT[pp][:, :], start=(ii == 0), stop=(ii == n_it - 1)),
                                         lambda e, pp=pp, ii=ii: e.matmul(P[A1][:, :], lhsT=ones_bf[:, :], rhs=pT[pp][:, :], start=(ii == 0), stop=(ii == n_it - 1))],
                                        reads=[('npT', pp), vkey, 'ones'], writes=[('P', A0), ('P', A1)])
                              tk.op('dve', lambda e: e.reciprocal(out=rden[:, :], in_=P[A1][:, :]), reads=[('P', A1)], writes=['nrden'])
                              tk.op('dve', lambda e, qh=qh, hd=hd: e.tensor_tensor(out=oabc[:, 4 + hd, HS[qh]], in0=P[A0][:, :], in1=rden[:, :], op=ALU.mult),
                                    reads=[('P', A0), 'nrden'], writes=[('oabc', 4 + hd, qh)])
                      tk.barrier()

                with ExitStack() as ls:
                  if STAGE >= 4:
                      PADW = SEG + 3
                      lpn = Panels("lp", 2, 256, ls)
                      lsl = {0: lpn.load(5632, 256)}
                      lxp = sb("llxp", [128, NSEG, PADW], F32, ls)
                      xc = sb("lxc", [128, T], F32, ls)
                      xcb = sb("lxcb", [128, T], BF16, ls)
                      lgt = sb("llg", [128, T], F32, ls)
                      gl = sb("lgl", [128, T], F32, ls)
                      rg = sb("lrg", [128, T], F32, ls)
                      ig = sb("lig", [128, T], F32, ls)
                      at_ = sb("lat", [128, T], F32, ls)
                      ut = sb("lut", [128, T], F32, ls)
                      hsf = sb("lhsf", [128, T], F32, ls)
                      hsb = sb("lhsb", [128, T], F32, ls)
                      ini = sb("lini", [128, 1], F32, ls)
                      bd_bf = sb("bd_bf", [128, 16, 128], BF16, ls)
                      tk.dma('pool', 'c0', bd_bf[:], lrubd_d[l].rearrange("a b c p q -> p (a b c) q"), writes=['bd'])
                      tk.op('dve', lambda e: e.memset(lxp[:], 0.0), writes=['llxp'])
                      for cc in range(4):
                          vb = l * 48
                          if cc + 1 < 4:
                              lsl[cc + 1] = lpn.load(5632 + (cc + 1) * 256, 256)
                          s = lsl[cc]
                          for h in range(2):
                              lpn.feat(s, 0, h, h)
                              tk.op('act', lambda e, h=h: e.activation(out=lxp[:, 2 * h:2 * h + 2, 2:2 + SEG],
                                                                       in_=P[h][:, :].rearrange("p (s t) -> p s t", t=SEG), func=AF.Copy),
                                    reads=[('P', h)], writes=['llxp'])
                          for h in range(2):
                              lpn.feat(s, 128, h, 2 + h)
                              tk.op('act', lambda e, h=h: e.activation(out=lgt[:, HS[h]], in_=P[2 + h][:, :], func=AF.Copy),
                                    reads=[('P', 2 + h)], writes=['llg'])
                          tk.op('dve', lambda e: e.tensor_scalar(out=lxp[:, 1:NSEG, 0:2], in0=lxp[:, 0:NSEG - 1, SEG:SEG + 2], scalar1=carry,
                                                                 scalar2=None, op0=ALU.mult), reads=['llxp', 'small'], writes=['llxp'], sync_same=True)
                          tk.op('dve', lambda e: e.tensor_scalar(out=lxp[:, 0:NSEG - 1, SEG + 2:SEG + 3], in0=lxp[:, 1:NSEG, 2:3], scalar1=carry,
                                                                 scalar2=None, op0=ALU.mult), reads=['llxp', 'small'], writes=['llxp'], sync_same=True)
                          xc3 = xc[:, :].rearrange("p (s t) -> p s t", t=SEG)
                          cw = lambda j: lruv[:, vb + j * 4 + cc: vb + j * 4 + cc + 1]
                          cb = lruv[:, vb + 16 + cc: vb + 16 + cc + 1]
                          tk.op('dve', lambda e: e.tensor_scalar(out=xc3, in0=lxp[:, :, 0:SEG], scalar1=cw(0), scalar2=cb, op0=ALU.mult, op1=ALU.add),
                                reads=['llxp', 'lruv'], writes=['lxc'], sync_same=True)
                          for j in range(1, 4):
                              tk.op('dve', lambda e, j=j: e.scalar_tensor_tensor(out=xc3, in0=lxp[:, :, j:j + SEG], scalar=cw(j), in1=xc3,
                                                                                op0=ALU.mult, op1=ALU.add), reads=['llxp', 'lxc', 'lruv'], writes=['lxc'])
                          tk.op('act', lambda e: e.activation(out=xcb[:, :], in_=xc[:, :], func=AF.Copy), reads=['lxc'], writes=['lxcb'])
                          tk.op('pool', lambda e: e.tensor_tensor(out=gl[:, :], in0=lgt[:, :], in1=lgt[:, :], op=ALU.mult), reads=['llg'], writes=['lgl'])
                          tk.op('pool', lambda e: e.tensor_scalar(out=gl[:, :], in0=gl[:, :], scalar1=0.044715, scalar2=1.0, op0=ALU.mult, op1=ALU.add),
                                reads=['lgl'], writes=['lgl'], sync_same=True)
                          tk.op('pool', lambda e: e.tensor_tensor(out=gl[:, :], in0=gl[:, :], in1=lgt[:, :], op=ALU.mult), reads=['lgl', 'llg'], writes=['lgl'], sync_same=True)
                          tk.op('act', lambda e: e.activation(out=gl[:, :], in_=gl[:, :], func=AF.Sigmoid, scale=1.5957691216057308), reads=['lgl'], writes=['lgl'])
                          tk.op('pool', lambda e: e.tensor_tensor(out=gl[:, :], in0=gl[:, :], in1=lgt[:, :], op=ALU.mult), reads=['lgl', 'llg'], writes=['lgl'])
                          for d in range(2):
                              ba = lruv[:, vb + 20 + d * 4 + cc: vb + 20 + d * 4 + cc + 1]
                              bx = lruv[:, vb + 28 + d * 4 + cc: vb + 28 + d * 4 + cc + 1]
                              n8 = nsp[:, l * 16 + d * 4 + cc: l * 16 + d * 4 + cc + 1]
                              n16 = nsp[:, l * 16 + 8 + d * 4 + cc: l * 16 + 8 + d * 4 + cc + 1]
                              bda = bd_bf[:, (d * 2 + 0) * 4 + cc, :]
                              bdx = bd_bf[:, (d * 2 + 1) * 4 + cc, :]
                              for h in range(2):
                                  tk.mm([lambda e, h=h: e.matmul(P[4][:, :], lhsT=bda, rhs=xcb[:, HS[h]], start=True, stop=True)],
                                        reads=['lxcb', 'bd'], writes=[('P', 4)])
                                  tk.op('act', lambda e, h=h: e.activation(out=rg[:, HS[h]], in_=P[4][:, :], func=AF.Sigmoid, bias=ba, scale=1.0),
                                        reads=[('P', 4), 'lruv'], writes=['lrg'])
                                  tk.mm([lambda e, h=h: e.matmul(P[5][:, :], lhsT=bdx, rhs=xcb[:, HS[h]], start=True, stop=True)],
                                        reads=['lxcb', 'bd'], writes=[('P', 5)])
                                  tk.op('act', lambda e, h=h: e.activation(out=ig[:, HS[h]], in_=P[5][:, :], func=AF.Sigmoid, bias=bx, scale=1.0),
                                        reads=[('P', 5), 'lruv'], writes=['lig'])
                              tk.op('act', lambda e: e.activation(out=at_[:, :], in_=rg[:, :], func=AF.Exp, scale=n8), reads=['lrg', ('nsp', l)], writes=['lat'])
                              tk.op('act', lambda e: e.activation(out=ut[:, :], in_=rg[:, :], func=AF.Exp, scale=n16), reads=['lrg', ('nsp2', l)], writes=['lut'])
                              tk.op('dve', lambda e: e.tensor_scalar(out=ut[:, :], in0=ut[:, :], scalar1=-1.0, scalar2=1.0, op0=ALU.mult, op1=ALU.add),
                                    reads=['lut'], writes=['lut'])
                              tk.op('act', lambda e: e.activation(out=ut[:, :], in_=ut[:, :], func=AF.Sqrt), reads=['lut'], writes=['lut'])
                              tk.op('dve', lambda e: e.tensor_tensor(out=ut[:, :], in0=ut[:, :], in1=ig[:, :], op=ALU.mult), reads=['lut', 'lig'], writes=['lut'])
                              tk.op('dve', lambda e: e.tensor_tensor(out=ut[:, :], in0=ut[:, :], in1=xc[:, :], op=ALU.mult), reads=['lut', 'lxc'], writes=['lut'], sync_same=True)
                              hs_ = hsf if d == 0 else hsb
                              hkey = 'lhs%d' % d
                              segs = list(range(NSEG)) if d == 0 else list(range(NSEG - 1, -1, -1))
                              for si, sg in enumerate(segs):
                                  h0c = h0t[:, ((l * 2 + d) * NSEG + sg) * 4 + cc: ((l * 2 + d) * NSEG + sg) * 4 + cc + 1]
                                  fcol = ((l * 2 + d) * NSEG + sg) * 4 + cc
                                  if si == 0:
                                      tk.op('dve', lambda e, h0c=h0c: e.tensor_copy(out=ini[:, :], in_=h0c), reads=['h0t', hkey], writes=['lini'], sync_same=True)
                                  else:
                                      psg = segs[si - 1]
                                      pcol = (psg * SEG + SEG - 1) if d == 0 else (psg * SEG)
                                      tk.op('dve', lambda e, h0c=h0c, pcol=pcol, hs_=hs_: e.scalar_tensor_tensor(
                                          out=ini[:, :], in0=hs_[:, pcol:pcol + 1], scalar=carry, in1=h0c, op0=ALU.mult, op1=ALU.add),
                                          reads=['h0t', hkey, 'small'], writes=['lini'], sync_same=True)
                                  sl = slice(sg * SEG, (sg + 1) * SEG)
                                  if d == 0:
                                      tk.op('dve', lambda e, sl=sl, hs_=hs_: e.tensor_tensor_scan(out=hs_[:, sl], data0=at_[:, sl], data1=ut[:, sl],
                                                                                                  initial=ini[:, 0:1], op0=ALU.mult, op1=ALU.add),
                                            reads=['lat', 'lut', 'lini'], writes=[hkey], sync_same=True)
                                      lcol = sg * SEG + SEG - 1
                                  else:
                                      rs_ = slice((sg + 1) * SEG - 1, sg * SEG - 1 if sg > 0 else None, -1)
                                      tk.op('dve', lambda e, rs_=rs_, hs_=hs_: e.tensor_tensor_scan(out=hs_[:, rs_], data0=at_[:, rs_], data1=ut[:, rs_],
                                                                                                    initial=ini[:, 0:1], op0=ALU.mult, op1=ALU.add),
                                            reads=['lat', 'lut', 'lini'], writes=[hkey], sync_same=True)
                                      lcol = sg * SEG
                                  tk.op('pool', lambda e, fcol=fcol, lcol=lcol, hs_=hs_: e.tensor_copy(out=lrufin[:, fcol:fcol + 1], in_=hs_[:, lcol:lcol + 1]),
                                        reads=[hkey], writes=['lrufin'])
                          tk.op('dve', lambda e: e.tensor_tensor(out=hsf[:, :], in0=hsf[:, :], in1=hsb[:, :], op=ALU.add), reads=['lhs0', 'lhs1'], writes=['lhs0'])
                          tk.op('dve', lambda e, cc=cc: e.tensor_tensor(out=oabc[:, 12 + cc, :], in0=hsf[:, :], in1=gl[:, :], op=ALU.mult),
                                reads=['lhs0', 'lgl'], writes=[('oabc', 12 + cc, 0), ('oabc', 12 + cc, 1)], sync_same=True)
                      tk.barrier()

                if DEBUG:
                    for c in range(NCH):
                        tk.dma('pool', 'dbg2', dbg_o[1, c * 128:(c + 1) * 128, :], oabc[:, c, :], reads=[('oabc', c, 0), ('oabc', c, 1)])
                    tk.barrier()
                with ExitStack() as go_:
                  if STAGE >= 5:
                    mT = sb("mT", [128, NCH, T], BF16, go_)
                    with ExitStack() as gs:
                      gpn = Panels("gp", 2, 384, gs)
                      gsl = {0: gpn.load(6656, 384)}
                      wp = sb("wp", [128, NCH, 128], BF16, gs)
                      gsb = [sb("gsb%d" % i, [128, 512], F32, gs) for i in range(2)]
                      macc = [sb("macc%d" % i, [128, 512], F32, gs) for i in range(2)]
                      gc = [0]
                      wsrc = [wpa_d[l].rearrange("(k p) d -> p k d", p=128), wpb_d[l].rearrange("(k p) d -> p k d", p=128),
                              wpc_d[l].rearrange("(k p) d -> p k d", p=128)]
                      krange = [(0, 4), (4, 12), (12, 16)]

                      def wp_load(br, dc):
                          k0, k1 = krange[br]
                          tk.dma('pool', 'wp' + 'abc'[br], wp[:, k0:k1, :], wsrc[br][:, :, dc * 128:(dc + 1) * 128], writes=[('wp', 0, br)])

                      for br in range(3):
                          wp_load(br, 0)
                      for dc in range(NCH):
                          if dc + 1 < NCH:
                              gsl[dc + 1] = gpn.load(6656 + (dc + 1) * 384, 384)
                          for br, (k0, k1) in enumerate(krange):
                              s = gsl[dc]
                              for h in range(2):
                                  gi = gc[0] % 2
                                  gc[0] += 1
                                  gpn.feat(s, br * 128, h, gi)
                                  tk.op('act', lambda e, gi=gi: e.activation(out=gsb[gi][:, :], in_=P[gi][:, :], func=AF.Sigmoid),
                                        reads=[('P', gi)], writes=[('gsb', gi)])
                                  tk.mm([lambda e, k=k, h=h, gi=gi: e.matmul(P[2 + gi][:, :], lhsT=wp[:, k, :], rhs=oabc[:, k, HS[h]],
                                                                            start=(k == k0), stop=(k == k1 - 1)) for k in range(k0, k1)],
                                        reads=[('wp', 0, br)] + [('oabc', k, h) for k in range(k0, k1)], writes=[('P', 2 + gi)])
                                  if br == 0:
                                      tk.op('dve', lambda e, gi=gi, h=h: e.tensor_tensor(out=macc[h][:, :], in0=gsb[gi][:, :], in1=P[2 + gi][:, :], op=ALU.mult),
                                            reads=[('gsb', gi), ('P', 2 + gi)], writes=[('macc', h)])
                                  else:
                                      tk.op('dve', lambda e, gi=gi: e.tensor_tensor(out=gsb[gi][:, :], in0=gsb[gi][:, :], in1=P[2 + gi][:, :], op=ALU.mult),
                                            reads=[('gsb', gi), ('P', 2 + gi)], writes=[('gsb', gi)])
                                      if br == 1:
                                          tk.op('pool', lambda e, gi=gi, h=h: e.tensor_tensor(out=macc[h][:, :], in0=macc[h][:, :], in1=gsb[gi][:, :], op=ALU.add),
                                                reads=[('gsb', gi), ('macc', h)], writes=[('macc', h)])
                                      else:
                                          tk.op('pool', lambda e, gi=gi, h=h, dc=dc: e.tensor_tensor(out=mT[:, dc, HS[h]], in0=macc[h][:, :], in1=gsb[gi][:, :], op=ALU.add),
                                                reads=[('gsb', gi), ('macc', h)], writes=[('mT', dc, h)])
                              if dc + 1 < NCH:
                                  wp_load(br, dc + 1)
                      tk.barrier()
                    with ExitStack() as ws_:
                      wo = [sb("wo%d" % i, [128, NCH, 128], BF16, ws_) for i in range(2)]
                      woutv = wout_d[l].rearrange("(k p) d -> p k d", p=128)
                      oc = [0]
                      tk.dma('pool', 'wo0', wo[0][:], woutv[:, :, 0:128], writes=[('wo', 0)])
                      for dc in range(NCH):
                          sp_ = dc % 2
                          if dc + 1 < NCH:
                              tk.dma('pool', 'wo%d' % (1 - sp_), wo[1 - sp_][:], woutv[:, :, (dc + 1) * 128:(dc + 2) * 128], writes=[('wo', 1 - sp_)])
                          for h in range(2):
                              pd = 4 + oc[0] % 3
                              oc[0] += 1
                              tk.mm([lambda e, k=k, h=h, pd=pd: e.matmul(P[pd][:, :], lhsT=wo[sp_][:, k, :], rhs=mT[:, k, HS[h]],
                                                                        start=(k == 0), stop=(k == NCH - 1)) for k in range(NCH)],
                                    reads=[('wo', sp_)] + [('mT', k, h) for k in range(NCH)], writes=[('P', pd)])
                              tk.op('dve', lambda e, dc=dc, h=h, pd=pd: e.scalar_tensor_tensor(
                                  out=xT[:, dc, HS[h]], in0=P[pd][:, :], scalar=g_ap(l, 1, dc), in1=xT[:, dc, HS[h]], op0=ALU.mult, op1=ALU.add),
                                  reads=[('P', pd), ('x', dc, h), ('gmod', l, 1)], writes=[('x', dc, h)])
                      tk.barrier()

        if DEBUG:
            tk.dma('sp', 'dbg', dbg_o[7, 0:128, 0:L * 144], modT[:], reads=[('modT', 0), ('modT', 1)])
        for l in range(L):
            if STAGE < 6 and l > 0:
                break
            if STAGE >= 1:
                ffn(l, 0, wup_d[0], wdn_d[0], nmods=24)
                if l == 0:
                    dbg_dump(0)
            if STAGE >= 2:
                mixer(l)
                if l == 0:
                    dbg_dump(2)
            if STAGE >= 6:
                ffn(l, 2, wup_d[1], wdn_d[1], nmods=(12 if l == 0 else 4))
                if l == 0:
                    dbg_dump(3)

        with ExitStack() as fs_:
            rstd = sb("rstd", [128, T], F32, fs_)
            rstd_box[0] = rstd
            sumsq_rstd(1.0 / D)
            yst = [sb("yst%d" % i, [128, T], F32, fs_) for i in range(2)]
            for c in range(NCH):
                yb = c % 2
                tk.op('dve', lambda e, c=c, yb=yb: e.scalar_tensor_tensor(out=yst[yb][:, :], in0=xT[:, c, :], scalar=fng[:, c:c + 1], in1=rstd[:, :],
                                                                           op0=ALU.mult, op1=ALU.mult),
                      reads=[('x', c, 0), ('x', c, 1), ('rstd', 0), ('rstd', 1), 'fng'], writes=[('yst', yb)])
                tk.dma('sp', 'yo%d' % yb, yT_o[c * 128:(c + 1) * 128, :], yst[yb][:], reads=[('yst', yb)])
            tk.dma('sp', 'lo', lru_o, lrufin[:], reads=['lrufin'])
            tk.final_wait()
    return nc


def _fm(v):
    v = np.asarray(v, np.float32)
    lead = v.shape[:-1]
    n = v.shape[-1] // 128
    a = v.reshape(lead + (n, 128))
    a = np.moveaxis(a, -1, 0)
    return np.ascontiguousarray(a.reshape(128, -1))


def _na_bias_sample(rpb_l):
    rows, W, KH, KW = 16, 64, 8, 16
    r = np.arange(rows)
    kr0 = np.clip(r - KH // 2, 0, rows - KH)
    qc = np.arange(W)
    ws = np.clip(qc - KW // 2, 0, W - KW)
    kr = np.arange(rows)
    kcol = np.arange(W)
    row_ok = (kr[None, :] >= kr0[:, None]) & (kr[None, :] < kr0[:, None] + KH)
    col_ok = (kcol[None, :] >= ws[:, None]) & (kcol[None, :] < ws[:, None] + KW)
    dy = np.clip(kr[None, :] - r[:, None] + KH - 1, 0, 2 * KH - 2)
    dx = np.clip(kcol[None, :] - qc[:, None] + KW - 1, 0, 2 * KW - 2)
    b = rpb_l[:, dy[:, None, :, None], dx[None, :, None, :]]
    ok = row_ok[:, None, :, None] & col_ok[None, :, None, :]
    b = np.where(ok[None], b, np.float32(NEGM)).astype(np.float32)
    b = b.reshape(8, 1024, 1024)
    return np.ascontiguousarray(b.transpose(0, 2, 1))


def _na_bias_prompt():
    seg = np.arange(1024) // 256
    ok = seg[:, None] == seg[None, :]
    b = np.where(ok, np.float32(0.0), np.float32(NEGM)).astype(np.float32)
    return np.broadcast_to(b[None], (8, 1024, 1024))


def _tile_bias(b):
    out = np.empty((8, len(QK_TILES), 128, 512), np.float32)
    for i, (kb, qh) in enumerate(QK_TILES):
        out[:, i] = b[:, kb * 128:(kb + 1) * 128, qh * 512:(qh + 1) * 512]
    return out


_NC_CACHE = {}


def _win_perm():
    p = []
    for hh in range(4):
        for off in (C_ZQ, C_ZI, C_ZFF, C_ZFB, C_ZO):
            p.extend(range(off + hh * 128, off + (hh + 1) * 128))
    for hd in range(8):
        for off in (C_NQ, C_NK, C_NV):
            p.extend(range(off + hd * 128, off + (hd + 1) * 128))
    for cc in range(4):
        for off in (C_LX, C_LG):
            p.extend(range(off + cc * 128, off + (cc + 1) * 128))
    for dc in range(16):
        for br in range(3):
            p.extend(range(C_GATE + br * D + dc * 128, C_GATE + br * D + (dc + 1) * 128))
    assert len(p) == DIN and len(set(p)) == DIN
    return np.asarray(p)


ROLES = [('p', 0), ('p', 1), ('p', 2), ('p', 3), ('s', 0), ('s', 1), ('p', 0), ('p', 1)]


def _build_in_maps(roles, x_prompt, x_sample, cache_na_k, cache_na_v, state_hgrn, state_lru, c, c_ctx,
                   mod_w, mod_b, norm_g, ffn1_w_up, ffn1_w_down, ffn2_w_up, ffn2_w_down, w_in,
                   hgrn_lb_logits, hgrn_norm_g, na_rpb, lru_conv_w, lru_conv_b, lru_w_a, lru_b_a,
                   lru_w_x, lru_b_x, lru_lambda, w_proj_a, w_proj_b, w_proj_c, w_out, final_norm_g):
    f32 = np.float32
    A = lambda a: np.ascontiguousarray(np.asarray(a, f32))
    x_prompt, x_sample = A(x_prompt), A(x_sample)
    cache_na_k, cache_na_v = A(cache_na_k), A(cache_na_v)
    state_hgrn, state_lru = A(state_hgrn), A(state_lru)
    c, c_ctx = A(c), A(c_ctx)
    na_rpb = A(na_rpb)

    shared = {
        "mod_w": A(mod_w), "ffn1_w_up": A(ffn1_w_up), "ffn1_w_down": A(ffn1_w_down),
        "ffn2_w_up": A(ffn2_w_up), "ffn2_w_down": A(ffn2_w_down), "w_in": np.ascontiguousarray(A(w_in)[:, :, _win_perm()]),
        "w_proj_a": A(w_proj_a), "w_proj_b": A(w_proj_b), "w_proj_c": A(w_proj_c), "w_out": A(w_out),
        "mod_bT": _fm(A(mod_b)).reshape(128, L, 144).reshape(128, L * 144),
        "normgT": _fm(A(norm_g)),
        "fngT": _fm(A(final_norm_g)),
        "ident": np.eye(128, dtype=f32),
        "hgngT": _fm(A(hgrn_norm_g)),
    }
    shared["hglogT"] = _fm(A(hgrn_lb_logits))
    s_i = np.arange(128)
    same = (s_i[:, None] // HC) == (s_i[None, :] // HC)
    hm = np.stack([same & (s_i[:, None] <= s_i[None, :]), same & (s_i[:, None] >= s_i[None, :])]).astype(f32)
    shared["hmask"] = hm
    rm = np.zeros((128, 4), f32)
    for m in range(4):
        rm[m * HC:(m + 1) * HC, m] = 1.0
    shared["rowmask"] = rm
    lv = np.zeros((128, L, 48), f32)
    cw = A(lru_conv_w)
    lv[:, :, 0:16] = _fm(cw).reshape(128, L, 16)
    lv[:, :, 16:20] = _fm(A(lru_conv_b)).reshape(128, L, 4)
    lv[:, :, 20:28] = _fm(A(lru_b_a)).reshape(128, L, 8)
    lv[:, :, 28:36] = _fm(A(lru_b_x)).reshape(128, L, 8)
    lv[:, :, 36:44] = _fm(A(lru_lambda)).reshape(128, L, 8)
    shared["lru_vecT"] = np.ascontiguousarray(lv.reshape(128, L * 48))
    bd = np.zeros((L, 2, 2, 4, 128, 128), f32)
    for wi, w in enumerate((A(lru_w_a), A(lru_w_x))):
        for cc in range(4):
            for kk in range(2):
                bd[:, :, wi, cc, kk * 64:(kk + 1) * 64, kk * 64:(kk + 1) * 64] = w[:, :, cc * 2 + kk]
    shared["lru_bd"] = bd

    bias_prompt = None
    bias_sample = None
    in_maps = []
    for kind, idx in roles:
        m = dict(shared)
        sm = np.zeros((128, 8), f32)
        if kind == 'p':
            xs = x_prompt[idx * 4:(idx + 1) * 4].reshape(T, D)
            cond = c_ctx
            sm[:, 0] = 0.0
            sm[:, 1] = NEGM
            s_init = np.zeros((L, 2, 4, 128, NSEG, 128), f32)
            kctx = np.zeros((L, 8, 128, 512), f32)
            vctx = np.zeros((L, 8, 128, 4, 128), f32)
            if bias_prompt is None:
                bp = _tile_bias(_na_bias_prompt())
                bias_prompt = np.ascontiguousarray(np.broadcast_to(bp[None], (L,) + bp.shape))
            nb = bias_prompt
            h0 = np.zeros((128, L, 2, NSEG, 4), f32)
        else:
            xs = x_sample[idx]
            cond = c[idx]
            sm[:, 0] = 1.0
            sm[:, 1] = 0.0
            s_init = np.zeros((L, 2, 4, 128, NSEG, 128), f32)
            st = state_hgrn[idx]
            s_init[:, 0, :, :, 0, :] = st[:, 0]
            s_init[:, 1, :, :, NSEG - 1, :] = st[:, 1]
            kctx = np.ascontiguousarray(cache_na_k[idx].transpose(0, 2, 3, 1))
            vv = cache_na_v[idx].reshape(L, 4, 128, 8, 128)
            vctx = np.ascontiguousarray(vv.transpose(0, 3, 2, 1, 4))
            if bias_sample is None:
                bias_sample = np.stack([_tile_bias(_na_bias_sample(na_rpb[l])) for l in range(L)])
            nb = bias_sample
            h0 = np.zeros((128, L, 2, NSEG, 4), f32)
            sl = state_lru[idx]
            slf = _fm(sl).reshape(128, L, 2, 4)
            h0[:, :, 0, 0, :] = slf[:, :, 0]
            h0[:, :, 1, NSEG - 1, :] = slf[:, :, 1]
        m["xT"] = np.ascontiguousarray(xs.T)
        m["condT"] = _fm(cond)
        m["small"] = sm
        m["s_init"] = s_init.reshape(L, 2, 4, 128, NSEG * 128)
        m["kctxT"] = kctx
        m["vctx"] = vctx.reshape(L, 8, 128, 512)
        m["na_bias"] = nb
        m["lru_h0T"] = np.ascontiguousarray(h0.reshape(128, -1))
        in_maps.append(m)
    return in_maps


def kernel(**inputs):
    f32 = np.float32
    in_maps = _build_in_maps(ROLES, **inputs)
    if 'nc' not in _NC_CACHE:
        _NC_CACHE['nc'] = build_nc()
    nc = _NC_CACHE['nc']
    res = run_bass_kernel_spmd(nc, in_maps, core_ids=list(range(8)))
    R = res.results

    y_prompt = np.empty((16, 256, D), f32)
    y_sample = np.empty((2, T, D), f32)
    nk = np.empty((16, L, 256, 8, 128), f32)
    nv = np.empty((16, L, 256, 8, 128), f32)
    nhg = np.empty((16, L, 2, 4, 128, 128), f32)
    nlru = np.empty((16, L, 2, 512), f32)
    for ci in range(4):
        r = R[ci]
        y = np.asarray(r["yT"]).T.reshape(4, 256, D)
        y_prompt[ci * 4:(ci + 1) * 4] = y
        kT = np.asarray(r["kT_out"])
        nk[ci * 4:(ci + 1) * 4] = kT.transpose(2, 0, 1).reshape(4, 256, L, 8, 128).transpose(0, 2, 1, 3, 4)
        vo = np.asarray(r["v_out"])
        nv[ci * 4:(ci + 1) * 4] = vo.reshape(L, 4, 256, 8, 128).transpose(1, 0, 2, 3, 4)
        hg = np.asarray(r["hg_out"]).reshape(L, 2, 4, 128, NSEG, 128)
        nhg[ci * 4:(ci + 1) * 4] = hg.transpose(4, 0, 1, 2, 3, 5)
        lo = np.asarray(r["lru_out"]).reshape(128, L, 2, NSEG, 4)
        nlru[ci * 4:(ci + 1) * 4] = lo.transpose(3, 1, 2, 4, 0).reshape(NSEG, L, 2, 512)
    for b in range(2):
        y_sample[b] = np.asarray(R[4 + b]["yT"]).T
    if DEBUG:
        kernel.debug = [np.asarray(R[i]["dbg"]) for i in range(8)]
    return (y_prompt, y_sample, nk, nv, nhg, nlru)
```

```python
import numpy as np
from contextlib import ExitStack
import concourse.bass as bass
import concourse.mybir as mybir
from concourse.bass_utils import run_bass_kernel_spmd

F32 = mybir.dt.float32
BF16 = mybir.dt.bfloat16
AF = mybir.ActivationFunctionType
ALU = mybir.AluOpType

L = 2
D = 2048
NCH = 16
T = 1024
NSEG = 4
SEG = 256
DFF = 5632
DIN = 12800
HC = 32
NEGM = -30000.0
EPS = 1e-6
C_ZQ, C_ZFF, C_ZFB, C_ZI, C_ZO = 0, 512, 1024, 1536, 2048
C_NQ, C_NK, C_NV = 2560, 3584, 4608
C_LX, C_LG = 5632, 6144
C_GATE = 6656
QK_TILES = [(kb, qh) for qh in range(2) for kb in range(8) if (qh == 0 and kb <= 5) or (qh == 1 and kb >= 2)]
DEBUG = False
STAGE = 99
HSTOP = 99


class Tracker:
    def __init__(self, nc, es):
        self.nc = nc
        self.es = es
        self.eng = {'pe': nc.tensor, 'act': nc.scalar, 'dve': nc.vector, 'pool': nc.gpsimd, 'sp': nc.sync}
        self.sem = {}
        self.cnt = {}
        for e in ('pe', 'act', 'dve', 'pool'):
            self.sem[e] = es.enter_context(nc.semaphore("c_" + e))
            self.cnt[e] = 0
        self.waited = {e: {} for e in self.eng}
        self.bufs = {}
        self.dma_sems = {}
        self.all_dma_tokens = {}

    @staticmethod
    def _fix(reads, writes):
        r2, w2 = [], list(writes)
        for k in reads:
            if k == 'PT' or (isinstance(k, tuple) and k[0] == 'P'):
                w2.append(k)
            else:
                r2.append(k)
        return r2, w2

    def _deps(self, reads, writes):
        deps = []
        for r in reads:
            b = self.bufs.get(r)
            if b and b['w'] is not None:
                deps.append(b['w'])
        for w in writes:
            b = self.bufs.get(w)
            if b:
                if b['w'] is not None:
                    deps.append(b['w'])
                deps.extend(b['r'])
        return deps

    def _wait(self, engine, deps, sync_same=False, same_ok=False):
        best = {}
        for (sem, val, src, key) in deps:
            if src == engine and (engine == 'pe' or same_ok):
                continue
            if self.waited[engine].get(key, 0) >= val:
                continue
            if key not in best or best[key][1] < val:
                best[key] = (sem, val)
        for key, (sem, val) in best.items():
            self.eng[engine].wait_ge(sem, val)
            self.waited[engine][key] = val

    def _record(self, token, reads, writes):
        for r in reads:
            b = self.bufs.setdefault(r, {'w': None, 'r': []})
            b['r'].append(token)
            if len(b['r']) > 24:
                latest = {}
                for t in b['r']:
                    if t[3] not in latest or latest[t[3]][1] < t[1]:
                        latest[t[3]] = t
                b['r'] = list(latest.values())
        for w in writes:
            self.bufs[w] = {'w': token, 'r': []}

    def op(self, engine, fn, reads=(), writes=(), sync_same=False, same_ok=False):
        reads, writes = self._fix(reads, writes)
        self._wait(engine, self._deps(reads, writes), sync_same, same_ok)
        inst = fn(self.eng[engine])
        self.cnt[engine] += 1
        inst.then_inc(self.sem[engine], 1)
        token = (self.sem[engine], self.cnt[engine], engine, engine)
        self._record(token, reads, writes)
        return token

    def mm(self, fns, reads=(), writes=()):
        reads, writes = self._fix(reads, writes)
        self._wait('pe', self._deps(reads, writes))
        inst = None
        for fn in fns:
            inst = fn(self.eng['pe'])
        self.cnt['pe'] += 1
        inst.then_inc(self.sem['pe'], 1)
        token = (self.sem['pe'], self.cnt['pe'], 'pe', 'pe')
        self._record(token, reads, writes)
        return token

    def dma(self, queue, slot, out, in_, reads=(), writes=()):
        if slot not in self.dma_sems:
            self.dma_sems[slot] = [self.es.enter_context(self.nc.semaphore("d_" + slot)), 0]
        ds = self.dma_sems[slot]
        self._wait(queue, self._deps(reads, writes), sync_same=True)
        inst = self.eng[queue].dma_start(out=out, in_=in_)
        ds[1] += 16
        inst.then_inc(ds[0], 16)
        token = (ds[0], ds[1], 'dma', 'd_' + slot)
        self._record(token, reads, writes)
        self.all_dma_tokens['d_' + slot] = token
        return token

    def barrier(self):
        toks = [(self.sem[e], self.cnt[e], e, e) for e in ('pe', 'act', 'dve', 'pool') if self.cnt[e] > 0]
        toks += list(self.all_dma_tokens.values())
        for e in ('pe', 'act', 'dve', 'pool', 'sp'):
            self._wait(e, toks)

    def final_wait(self):
        self._wait('sp', list(self.all_dma_tokens.values()) +
                   [(self.sem[e], self.cnt[e], e, e) for e in ('pe', 'act', 'dve', 'pool') if self.cnt[e] > 0])


def build_nc():
    nc = bass.Bass("TRN2", target_bir_lowering=False)

    def din(name, shape):
        return nc.dram_tensor(name, list(shape), F32, kind="ExternalInput").ap()

    def dout(name, shape):
        return nc.dram_tensor(name, list(shape), F32, kind="ExternalOutput").ap()

    xT_d = din("xT", [D, T])
    cond_d = din("condT", [128, NCH])
    modw_d = din("mod_w", [L, D, 9 * D])
    modb_d = din("mod_bT", [128, L * 144])
    normg_d = din("normgT", [128, L * 3 * NCH])
    fng_d = din("fngT", [128, NCH])
    wup_d = [din("ffn1_w_up", [L, D, 2 * DFF]), din("ffn2_w_up", [L, D, 2 * DFF])]
    wdn_d = [din("ffn1_w_down", [L, DFF, D]), din("ffn2_w_down", [L, DFF, D])]
    win_d = din("w_in", [L, D, DIN])
    wpa_d = din("w_proj_a", [L, 512, D])
    wpb_d = din("w_proj_b", [L, 1024, D])
    wpc_d = din("w_proj_c", [L, 512, D])
    wout_d = din("w_out", [L, D, D])
    small_d = din("small", [128, 8])
    ident_d = din("ident", [128, 128])
    hmask_d = din("hmask", [2, 128, 128])
    rowmask_d = din("rowmask", [128, 4])
    hglog_d = din("hglogT", [128, L * 8])
    hgng_d = din("hgngT", [128, L * 4])
    sinit_d = din("s_init", [L, 2, 4, 128, NSEG * 128])
    kctx_d = din("kctxT", [L, 8, 128, 512])
    vctx_d = din("vctx", [L, 8, 128, 4 * 128])
    bias_d = din("na_bias", [L, 8, len(QK_TILES), 128, 512])
    lruv_d = din("lru_vecT", [128, L * 48])
    lrubd_d = din("lru_bd", [L, 2, 2, 4, 128, 128])
    h0_d = din("lru_h0T", [128, L * 2 * NSEG * 4])

    yT_o = dout("yT", [D, T])
    kT_o = dout("kT_out", [L, 1024, T])
    v_o = dout("v_out", [L, T, 1024])
    hg_o = dout("hg_out", [L, 2, 4, 128, NSEG * 128])
    lru_o = dout("lru_out", [128, L * 2 * NSEG * 4])
    dbg_o = dout("dbg", [8, D, T]) if DEBUG else None

    with ExitStack() as es:
        tk = Tracker(nc, es)

        uid = [0]

        def sb(name, shape, dt=F32, stack=es):
            uid[0] += 1
            return stack.enter_context(nc.sbuf_tensor("s%d_%s" % (uid[0], name), list(shape), dt))

        xT = sb("xT", [128, NCH, T])
        hT = sb("hT", [128, NCH, T], BF16)
        ones_bf = sb("ones_bf", [128, 128], BF16)
        ident_bf = sb("ident_bf", [128, 128], BF16)
        hmask = sb("hmask", [128, 2, 128])
        rowmask = sb("rowmask", [128, 4])
        modT = sb("modT", [128, L * 144])
        modb = sb("modb", [128, L * 144])
        normg = sb("normg", [128, L * 3 * NCH])
        fng = sb("fng", [128, NCH])
        Amod = sb("Amod", [128, L * 3 * NCH])
        gmod = sb("gmod", [128, L * 3 * NCH])
        small = sb("small", [128, 8])
        epst = sb("epst", [128, 1])
        condT = sb("condT", [128, NCH])
        cond_bf = sb("cond_bf", [128, NCH], BF16)
        hglog = sb("hglog", [128, L * 8])
        hgng = sb("hgng", [128, L * 4])
        lbt = sb("lbt", [128, L * 8])
        omlb = sb("omlb", [128, L * 8])
        lruv = sb("lruv", [128, L * 48])
        nsp = sb("nsp", [128, L * 16])
        h0t = sb("h0t", [128, L * 2 * NSEG * 4])
        lrufin = sb("lrufin", [128, L * 2 * NSEG * 4])
        rstd_box = [None]
        P = [es.enter_context(nc.psum_tensor("P%d" % i, [128, 512], F32)) for i in range(8)]
        PT = P[7][:, :].bitcast(BF16)

        carry = small[:, 0:1]
        ctxb = small[:, 1:2]

        tk.op('dve', lambda e: e.memset(ones_bf[:], 1.0), writes=['ones'])
        tk.op('dve', lambda e: e.memset(epst[:], EPS), writes=['eps'])
        tk.dma('pool', 'c0', ident_bf[:], ident_d, writes=['ident'])
        tk.dma('sp', 'c1', hmask[:], hmask_d.rearrange("a s t -> s a t"), writes=['hmask'])
        tk.dma('sp', 'c1', rowmask[:], rowmask_d, writes=['rowmask'])
        tk.dma('sp', 'c1', modb[:], modb_d, writes=['modb'])
        tk.dma('sp', 'c1', normg[:], normg_d, writes=['normg'])
        tk.dma('sp', 'c1', fng[:], fng_d, writes=['fng'])
        tk.dma('sp', 'c1', small[:], small_d, writes=['small'])
        tk.dma('sp', 'c1', condT[:], cond_d, writes=['condT'])
        tk.dma('sp', 'c1', hglog[:], hglog_d, writes=['hglog'])
        tk.dma('sp', 'c1', hgng[:], hgng_d, writes=['hgng'])
        tk.dma('sp', 'c1', lruv[:], lruv_d, writes=['lruv'])
        tk.dma('sp', 'c1', h0t[:], h0_d, writes=['h0t'])
        for c in range(NCH):
            tk.dma('sp', 'c2', xT[:, c, :], xT_d[c * 128:(c + 1) * 128, :], writes=[('x', c, 0), ('x', c, 1)])

        tk.barrier()
        tk.op('dve', lambda e: e.memset(lbt[:, 0:8], 0.0), writes=['lbt0'])
        tk.op('dve', lambda e: e.tensor_tensor(out=lbt[:, 8:16], in0=hglog[:, 8:16], in1=hglog[:, 0:8], op=ALU.subtract),
              reads=['hglog'], writes=['lbt1'])
        tk.op('act', lambda e: e.activation(out=lbt[:, 8:16], in_=lbt[:, 8:16], func=AF.Sigmoid), reads=['lbt1'], writes=['lbt1'])
        tk.op('dve', lambda e: e.tensor_scalar(out=omlb[:], in0=lbt[:], scalar1=-1.0, scalar2=1.0, op0=ALU.mult, op1=ALU.add),
              reads=['lbt0', 'lbt1'], writes=['omlb'])
        for l in range(L):
            lam = lruv[:, l * 48 + 36: l * 48 + 44]
            tk.op('act', lambda e: e.activation(out=nsp[:, l * 16: l * 16 + 8], in_=lam, func=AF.Exp, scale=-1.0),
                  reads=['lruv'], writes=[('nsp', l)])
            tk.op('act', lambda e: e.activation(out=nsp[:, l * 16: l * 16 + 8], in_=nsp[:, l * 16: l * 16 + 8], func=AF.Ln, bias=1.0),
                  reads=[('nsp', l)], writes=[('nsp', l)], sync_same=True)
            tk.op('dve', lambda e: e.tensor_scalar(out=nsp[:, l * 16 + 8: l * 16 + 16], in0=nsp[:, l * 16: l * 16 + 8],
                                                   scalar1=-16.0, scalar2=None, op0=ALU.mult),
                  reads=[('nsp', l)], writes=[('nsp2', l)])
            tk.op('dve', lambda e: e.tensor_scalar(out=nsp[:, l * 16: l * 16 + 8], in0=nsp[:, l * 16: l * 16 + 8],
                                                   scalar1=-8.0, scalar2=None, op0=ALU.mult),
                  reads=[('nsp', l), ('nsp2', l)], writes=[('nsp', l)])

        tk.op('act', lambda e: e.activation(out=cond_bf[:], in_=condT[:], func=AF.Silu), reads=['condT'], writes=['cond_bf'])
        MODQ = [(l_, pn_) for l_ in range(L) for pn_ in range(36)]
        mod_next = [0]
        mod_cnt = [0]

        def mod_issue(mw_list):
            l_, pn_ = MODQ[mod_next[0]]
            mod_next[0] += 1
            s_ = mod_cnt[0] % len(mw_list)
            mod_cnt[0] += 1
            wv = modw_d[l_].rearrange("(k p) f -> p k f", p=128)
            tk.dma('pool', 'mw%d' % s_, mw_list[s_][:], wv[:, :, pn_ * 512:(pn_ + 1) * 512], writes=[('mw', s_)])
            return (l_, pn_, s_, mw_list[s_])

        def mod_compute(hd_):
            l_, pn_, s_, mwt = hd_
            base = l_ * 144 + pn_ * 4
            fns = []
            for n in range(4):
                for k in range(NCH):
                    fns.append(lambda e, n=n, k=k: e.matmul(
                        P[7][:, base + n:base + n + 1], lhsT=mwt[:, k, n * 128:(n + 1) * 128], rhs=cond_bf[:, k:k + 1],
                        start=(k == 0), stop=(k == NCH - 1)))
            tk.mm(fns, reads=[('mw', s_), 'cond_bf'], writes=[('P', 7)])
            tk.op('dve', lambda e: e.tensor_tensor(out=modT[:, base:base + 4], in0=P[7][:, base:base + 4],
                                                   in1=modb[:, base:base + 4], op=ALU.add),
                  reads=[('P', 7), 'modb'], writes=[('modT', l_)])
            if pn_ % 12 in (7, 11):
                s3 = pn_ // 12
                o = (l_ * 3 + s3) * NCH
                sc = modT[:, l_ * 144 + (3 * s3 + 1) * NCH: l_ * 144 + (3 * s3 + 2) * NCH]
                gt = modT[:, l_ * 144 + (3 * s3 + 2) * NCH: l_ * 144 + (3 * s3 + 3) * NCH]
                if pn_ % 12 == 7:
                    tk.op('dve', lambda e: e.scalar_tensor_tensor(out=Amod[:, o:o + NCH], in0=sc, scalar=1.0,
                                                                  in1=normg[:, o:o + NCH], op0=ALU.add, op1=ALU.mult),
                          reads=[('modT', l_), 'normg'], writes=[('Amod', l_, s3)])
                else:
                    tk.op('dve', lambda e: e.tensor_scalar(out=gmod[:, o:o + NCH], in0=gt,
                                                           scalar1=(1.0 if s3 == 1 else 0.5), scalar2=None, op0=ALU.mult),
                          reads=[('modT', l_)], writes=[('gmod', l_, s3)])

        with ExitStack() as ms:
            mw = [sb("mw%d" % i, [128, NCH, 512], BF16, ms) for i in range(2)]
            hprev = mod_issue(mw)
            for i in range(8):
                hnext = mod_issue(mw) if i + 1 < 8 else None
                mod_compute(hprev)
                hprev = hnext
            tk.barrier()

        def shift_ap(l, s3, c):
            o = l * 144 + (3 * s3) * NCH + c
            return modT[:, o:o + 1]

        def A_ap(l, s3, c):
            o = (l * 3 + s3) * NCH + c
            return Amod[:, o:o + 1]

        def g_ap(l, s3, c):
            o = (l * 3 + s3) * NCH + c
            return gmod[:, o:o + 1]

        HS = [slice(0, 512), slice(512, 1024)]

        def sumsq_rstd(nfeat_scale):
            rstd = rstd_box[0]
            for h in range(2):
                tk.op('act', lambda e, h=h: e.activation(out=hT[:, :, HS[h]], in_=xT[:, :, HS[h]], func=AF.Square),
                      reads=[('x', c, h) for c in range(NCH)], writes=[('h', c, h) for c in range(NCH)])
            for h in range(2):
                fns = [lambda e, k=k, h=h: e.matmul(P[h][:, :], lhsT=ones_bf[:, :], rhs=hT[:, k, HS[h]],
                                                    start=(k == 0), stop=(k == NCH - 1)) for k in range(NCH)]
                tk.mm(fns, reads=[('h', c, h) for c in range(NCH)] + ['ones'], writes=[('P', h)])
                tk.op('act', lambda e, h=h: e.activation(out=rstd[:, HS[h]], in_=P[h][:, :], func=AF.Ln,
                                                         bias=epst[:, 0:1], scale=nfeat_scale),
                      reads=[('P', h), 'eps'], writes=[('rstd', h)])
                tk.op('act', lambda e, h=h: e.activation(out=rstd[:, HS[h]], in_=rstd[:, HS[h]], func=AF.Exp, scale=-0.5),
                      reads=[('rstd', h)], writes=[('rstd', h)])

        def modnorm(l, s3):
          with ExitStack() as ns_:
            rstd = sb("rstd", [128, T], F32, ns_)
            rstd_box[0] = rstd
            sumsq_rstd(1.0 / D)
            tmpn = [sb("tmpn0", [128, 512], F32, ns_), sb("tmpn1", [128, 512], F32, ns_)]
            i = 0
            for h in range(2):
                for c in range(NCH):
                    tb = i % 2
                    i += 1
                    tk.op('dve', lambda e, c=c, h=h, tb=tb: e.scalar_tensor_tensor(
                        out=tmpn[tb][:, :], in0=xT[:, c, HS[h]], scalar=A_ap(l, s3, c), in1=rstd[:, HS[h]],
                        op0=ALU.mult, op1=ALU.mult),
                        reads=[('x', c, h), ('rstd', h), ('Amod', l, s3)], writes=[('tmpn', tb)])
                    tk.op('act', lambda e, c=c, h=h, tb=tb: e.activation(
                        out=hT[:, c, HS[h]], in_=tmpn[tb][:, :], func=AF.Identity, bias=shift_ap(l, s3, c), scale=1.0),
                        reads=[('tmpn', tb), ('modT', l)], writes=[('h', c, h)])
            tk.barrier()

        def dbg_dump(idx):
            if DEBUG:
                for c in range(NCH):
                    tk.dma('sp', 'dbg', dbg_o[idx, c * 128:(c + 1) * 128, :], xT[:, c, :],
                           reads=[('x', c, 0), ('x', c, 1)])

        def ffn(l, s3, wup, wdn, nmods=0):
            modnorm(l, s3)
            NG = 22
            with ExitStack() as fs:
                wa = [sb("wa%d" % i, [128, NCH, 256], BF16, fs) for i in range(2)]
                wu = [sb("wu%d" % i, [128, NCH, 256], BF16, fs) for i in range(2)]
                wd = [sb("wd%d" % i, [128, 2, D], BF16, fs) for i in range(3)]
                hid = [sb("hid%d" % i, [128, 2, T], BF16, fs) for i in range(2)]
                sa = [sb("sa%d" % i, [128, 512], F32, fs) for i in range(2)]
                mwf = [sb("mwf%d" % i, [128, NCH, 512], BF16, fs) for i in range(2)] if nmods > 0 else None
                mods_left = [nmods]
                wupv = wup[l].rearrange("(k p) f -> p k f", p=128)

                def load(g):
                    s2, s3_ = g % 2, g % 3
                    tk.dma('pool', 'wa%d' % s2, wa[s2][:], wupv[:, :, g * 256:(g + 1) * 256], writes=[('wa', s2)])
                    tk.dma('pool', 'wu%d' % s2, wu[s2][:], wupv[:, :, DFF + g * 256: DFF + (g + 1) * 256], writes=[('wu', s2)])
                    tk.dma('pool', 'wd%d' % s3_, wd[s3_][:],
                           wdn[l][g * 256:(g + 1) * 256, :].rearrange("(k p) d -> p k d", p=128), writes=[('wd', s3_)])

                cnt = [0, 0]

                def up(g):
                    s2 = g % 2
                    for h in range(2):
                        for j in range(2):
                            pa = cnt[0] % 2
                            cnt[0] += 1
                            tk.mm([lambda e, k=k, j=j, h=h, pa=pa: e.matmul(
                                P[pa][:, :], lhsT=wa[s2][:, k, j * 128:(j + 1) * 128], rhs=hT[:, k, HS[h]],
                                start=(k == 0), stop=(k == NCH - 1)) for k in range(NCH)],
                                reads=[('wa', s2)] + [('h', c, h) for c in range(NCH)], writes=[('P', pa)])
                            tk.mm([lambda e, k=k, j=j, h=h, pa=pa: e.matmul(
                                P[2 + pa][:, :], lhsT=wu[s2][:, k, j * 128:(j + 1) * 128], rhs=hT[:, k, HS[h]],
                                start=(k == 0), stop=(k == NCH - 1)) for k in range(NCH)],
                                reads=[('wu', s2)] + [('h', c, h) for c in range(NCH)], writes=[('P', 2 + pa)])
                            tk.op('act', lambda e, pa=pa: e.activation(out=sa[pa][:, :], in_=P[pa][:, :], func=AF.Silu),
                                  reads=[('P', pa)], writes=[('sa', pa)])
                            tk.op('dve', lambda e, pa=pa, j=j, h=h: e.tensor_tensor(
                                out=hid[s2][:, j, HS[h]], in0=sa[pa][:, :], in1=P[2 + pa][:, :], op=ALU.mult),
                                reads=[('sa', pa), ('P', 2 + pa)], writes=[('hid', s2, j, h)])

                def down(g):
                    s2, s3_ = g % 2, g % 3
                    for h in range(2):
                        for dc in range(NCH):
                            pd = 4 + cnt[1] % 3
                            cnt[1] += 1
                            tk.mm([lambda e, k=k, dc=dc, h=h, pd=pd: e.matmul(
                                P[pd][:, :], lhsT=wd[s3_][:, k, dc * 128:(dc + 1) * 128], rhs=hid[s2][:, k, HS[h]],
                                start=(k == 0), stop=(k == 1)) for k in range(2)],
                                reads=[('wd', s3_), ('hid', s2, 0, h), ('hid', s2, 1, h)], writes=[('P', pd)])
                            tk.op('dve', lambda e, dc=dc, h=h, pd=pd: e.scalar_tensor_tensor(
                                out=xT[:, dc, HS[h]], in0=P[pd][:, :], scalar=g_ap(l, s3, dc), in1=xT[:, dc, HS[h]],
                                op0=ALU.mult, op1=ALU.add),
                                reads=[('P', pd), ('x', dc, h), ('gmod', l, s3)], writes=[('x', dc, h)])

                load(0)
                for g in range(NG + 1):
                    if g + 1 < NG:
                        load(g + 1)
                    mh = []
                    if g < NG and mods_left[0] > 0:
                        done_ = nmods - mods_left[0]
                        tgt_ = min(nmods, -(-(g + 1) * nmods // NG))
                        if g < 2:
                            tgt_ = max(tgt_, min(nmods, 2 * (g + 1)))
                        kq = max(0, min(2, tgt_ - done_, mods_left[0]))
                        for _ in range(kq):
                            mh.append(mod_issue(mwf))
                        mods_left[0] -= kq
                    if g < NG:
                        up(g)
                    for hd_ in mh:
                        mod_compute(hd_)
                    if g >= 1:
                        down(g - 1)
                assert mods_left[0] == 0
                tk.barrier()

        def load_thin(wt, slotname, key, src_cols):
            tk.dma('pool', slotname, wt[:], src_cols, writes=[key])

        def mixer(l):
            modnorm(l, 1)
            winv = win_d[l].rearrange("(k p) f -> p k f", p=128)
            with ExitStack() as mx:
                oabc = sb("oabc", [128, 16, T], BF16, mx)
                class Panels:
                    def __init__(self, name, n, width, stack):
                        self.t = [sb(name + str(i), [128, NCH, width], BF16, stack) for i in range(n)]
                        self.n, self.c, self.name = n, 0, name

                    def load(self, col0, ncols):
                        s_ = self.c % self.n
                        self.c += 1
                        tk.dma('pool', '%s%d' % (self.name, s_), self.t[s_][:, :, 0:ncols], winv[:, :, col0:col0 + ncols],
                               writes=[(self.name, s_)])
                        return s_

                    def feat(self, s_, coff, h, pb):
                        w = self.t[s_]
                        tk.mm([lambda e, k=k: e.matmul(P[pb][:, :], lhsT=w[:, k, coff:coff + 128], rhs=hT[:, k, HS[h]],
                                                       start=(k == 0), stop=(k == NCH - 1)) for k in range(NCH)],
                              reads=[(self.name, s_)] + [('h', c, h) for c in range(NCH)], writes=[('P', pb)])

                    def tok(self, s_, coff, blk4, pb):
                        w = self.t[s_]
                        fns = []
                        for j in range(4):
                            b_ = blk4 * 4 + j
                            for k in range(NCH):
                                fns.append(lambda e, k=k, j=j, b_=b_: e.matmul(
                                    P[pb][:, j * 128:(j + 1) * 128], lhsT=hT[:, k, b_ * 128:(b_ + 1) * 128], rhs=w[:, k, coff:coff + 128],
                                    start=(k == 0), stop=(k == NCH - 1)))
                        tk.mm(fns, reads=[(self.name, s_)] + [('h', c, blk4) for c in range(NCH)], writes=[('P', pb)])

                with ExitStack() as hs:
                  if STAGE >= 2:
                      hp = Panels("hp", 2, 256, hs)
                      q = sb("hq", [128, T], BF16, hs)
                      vtok = sb("hvtok", [128, 5, 8, 128], BF16, hs)
                      ft = sb("hf", [128, T], F32, hs)
                      bt_ = sb("hb", [128, T], F32, hs)
                      ex = sb("hex", [128, T], F32, hs)
                      Sbf = [sb("hSbf%d" % i, [128, 128], BF16, hs) for i in range(2)]
                      qtb = sb("hqtb", [128, T], BF16, hs)
                      ktb = sb("hktb", [128, T], BF16, hs)
                      khT = sb("hkhT", [128, T], BF16, hs)
                      khtok = sb("hkhtok", [128, 8, 128], BF16, hs)
                      atm = sb("hatm", [128, 8, 128], BF16, hs)
                      oacc = sb("hoacc", [128, T], F32, hs)
                      dec = sb("hdec", [128, T // HC], F32, hs)
                      cmask = sb("hcm0", [128, T], F32, hs)
                      S = [sb("hS%d" % i, [128, 128], F32, hs) for i in range(3)]
                      sini = sb("hsini", [128, NSEG * 128], F32, hs)
                      sstage = sb("hsstage", [128, NSEG * 128], F32, hs)
                      tk.op('dve', lambda e: e.memset(cmask[:], 1.0), writes=['cm0'])
                      tk.op('dve', lambda e: e.memset(cmask[:, 0:T:HC], 0.0), writes=['cm0'], sync_same=True)
                      PIK = []

                      for hh in range(4 if HSTOP >= 99 else 1):
                          sA = hp.load(hh * 640, 256)
                          sB = hp.load(hh * 640 + 256, 256)
                          for h in range(2):
                              hp.feat(sA, 0, h, h)
                              tk.op('act', lambda e, h=h: e.activation(out=q[:, HS[h]], in_=P[h][:, :], func=AF.Silu),
                                    reads=[('P', h)], writes=[('hq', h)])
                          if HSTOP <= 1:
                              continue
                          for b4 in range(2):
                              hp.tok(sA, 128, b4, 4 + b4)
                              src = P[4 + b4][:, :].rearrange("p (j e) -> p j e", e=128)
                              tk.op('act', lambda e, b4=b4, src=src: e.activation(
                                  out=vtok[:, 0, b4 * 4:(b4 + 1) * 4, :], in_=src, func=AF.Copy),
                                  reads=[('P', 4 + b4)], writes=[('vtok', 0, b4)])
                              for m in range(4):
                                  tk.op('dve', lambda e, b4=b4, m=m, src=src: e.tensor_scalar(
                                      out=vtok[:, 1 + m, b4 * 4:(b4 + 1) * 4, :], in0=src, scalar1=rowmask[:, m:m + 1],
                                      scalar2=None, op0=ALU.mult),
                                      reads=[('P', 4 + b4), 'rowmask'], writes=[('vtok', 1 + m, b4)])
                          if HSTOP <= 2:
                              continue
                          for d in range(2 if HSTOP >= 99 else 1):
                              li = l * 8 + d * 4 + hh
                              if d == 0:
                                  sC = hp.load(hh * 640 + 512, 128)
                              for h in range(2):
                                  hp.feat(sB, 128 * d, h, h)
                                  tk.op('act', lambda e, h=h: e.activation(out=ft[:, HS[h]], in_=P[h][:, :], func=AF.Sigmoid),
                                        reads=[('P', h)], writes=['hf'])
                              tk.op('dve', lambda e, li=li: e.tensor_scalar(out=ft[:, :], in0=ft[:, :], scalar1=omlb[:, li:li + 1],
                                                                            scalar2=lbt[:, li:li + 1], op0=ALU.mult, op1=ALU.add),
                                    reads=['hf', 'omlb', 'lbt0', 'lbt1'], writes=['hf'])
                              tk.op('act', lambda e: e.activation(out=ex[:, :], in_=ft[:, :], func=AF.Ln), reads=['hf'], writes=['hex'])
                              tk.op('dve', lambda e: e.tensor_scalar(out=ft[:, :], in0=ft[:, :], scalar1=-1.0, scalar2=1.0,
                                                                     op0=ALU.mult, op1=ALU.add), reads=['hf'], writes=['hf'])
                              if d == 0:
                                  tk.op('dve', lambda e: e.tensor_tensor_scan(out=bt_[:, :], data0=cmask[:, :], data1=ex[:, :],
                                                                              initial=0.0, op0=ALU.mult, op1=ALU.add),
                                        reads=['hex', 'cm0'], writes=['hb'])
                                  blast = bt_[:, :].rearrange("p (c i) -> p c i", i=HC)[:, :, HC - 1:HC]
                              else:
                                  tk.op('dve', lambda e: e.tensor_tensor_scan(out=bt_[:, ::-1], data0=cmask[:, :], data1=ex[:, ::-1],
                                                                              initial=0.0, op0=ALU.mult, op1=ALU.add),
                                        reads=['hex', 'cm0'], writes=['hb'])
                                  blast = bt_[:, :].rearrange("p (c i) -> p c i", i=HC)[:, :, 0:1]
                              tk.op('act', lambda e: e.activation(out=ex[:, :], in_=bt_[:, :], func=AF.Exp), reads=['hb'], writes=['hex'])
                              tk.op('dve', lambda e: e.tensor_tensor(out=qtb[:, :], in0=q[:, :], in1=ex[:, :], op=ALU.mult),
                                    reads=['hex', ('hq', 0), ('hq', 1)], writes=['hqtb'])
                              tk.op('act', lambda e, blast=blast: e.activation(out=dec[:, :].rearrange("p (c o) -> p c o", o=1), in_=blast, func=AF.Exp),
                                    reads=['hb'], writes=['hdec'])
                              tk.op('act', lambda e: e.activation(out=ex[:, :], in_=bt_[:, :], func=AF.Exp, scale=-1.0),
                                    reads=['hb'], writes=['hex'])
                              tk.op('dve', lambda e: e.tensor_tensor(out=ktb[:, :], in0=ft[:, :], in1=ex[:, :], op=ALU.mult),
                                    reads=['hex', 'hf'], writes=['hktb'])
                              tk.op('dve', lambda e: e.tensor_tensor(
                                  out=khT[:, :].rearrange("p (c i) -> p c i", i=HC),
                                  in0=ktb[:, :].rearrange("p (c i) -> p c i", i=HC),
                                  in1=dec[:, :].rearrange("p (c o) -> p c o", o=1).to_broadcast([128, T // HC, HC]), op=ALU.mult),
                                  reads=['hktb', 'hdec'], writes=['hkhT'], sync_same=True)
                              if HSTOP <= 3:
                                  continue
                              tk.mm([lambda e, b=b: e.transpose(PT[:, b * 128:(b + 1) * 128], khT[:, b * 128:(b + 1) * 128], ident_bf[:, :])
                                     for b in range(8)], reads=['hkhT', 'ident'], writes=['PT'])
                              tk.op('act', lambda e: e.activation(out=khtok[:, :, :], in_=PT[:, :].rearrange("p (b d) -> p b d", d=128), func=AF.Copy),
                                    reads=['PT'], writes=['hkhtok'])
                              if HSTOP <= 4:
                                  continue
                              for b4 in range(2):
                                  tk.mm([lambda e, j=j, b4=b4: e.matmul(
                                      P[b4][:, j * 128:(j + 1) * 128], lhsT=ktb[:, (b4 * 4 + j) * 128:(b4 * 4 + j + 1) * 128],
                                      rhs=qtb[:, (b4 * 4 + j) * 128:(b4 * 4 + j + 1) * 128], start=True, stop=True) for j in range(4)],
                                      reads=['hktb', 'hqtb'], writes=[('P', b4)])
                                  tk.op('dve', lambda e, b4=b4, d=d: e.tensor_tensor(
                                      out=atm[:, b4 * 4:(b4 + 1) * 4, :], in0=P[b4][:, :].rearrange("p (j t) -> p j t", t=128),
                                      in1=hmask[:, d:d + 1, :].to_broadcast([128, 4, 128]), op=ALU.mult),
                                      reads=[('P', b4), 'hmask'], writes=[('hatm', b4)])
                                  tk.mm([lambda e, j=j, b4=b4: e.matmul(
                                      P[4 + b4][:, j * 128:(j + 1) * 128], lhsT=vtok[:, 0, b4 * 4 + j, :], rhs=atm[:, b4 * 4 + j, :],
                                      start=True, stop=True) for j in range(4)],
                                      reads=[('hatm', b4), ('vtok', 0, b4)], writes=[('P', 4 + b4)])
                                  if d == 0:
                                      tk.op('act', lambda e, b4=b4: e.activation(out=oacc[:, HS[b4]], in_=P[4 + b4][:, :], func=AF.Copy),
                                            reads=[('P', 4 + b4)], writes=[('hoacc', b4)])
                                  else:
                                      tk.op('dve', lambda e, b4=b4: e.tensor_tensor(out=oacc[:, HS[b4]], in0=oacc[:, HS[b4]], in1=P[4 + b4][:, :], op=ALU.add),
                                            reads=[('P', 4 + b4), ('hoacc', b4)], writes=[('hoacc', b4)])
                              if HSTOP <= 5:
                                  continue
                              tk.dma('sp', 'sini', sini[:], sinit_d[l, d, hh], writes=['hsini'])
                              tk._wait('pe', tk._deps((), [('P', 2), ('P', 3)] + PIK))
                              cur = 0
                              nchk = T // HC
                              cps = SEG // HC
                              order = list(range(nchk)) if d == 0 else list(range(nchk - 1, -1, -1))
                              KB = [6, 5, 4, 0]
                              LA = 3

                              def emit_kv(step_):
                                  c_ = order[step_]
                                  b_ = (c_ * HC) // 128
                                  m_ = ((c_ * HC) % 128) // HC
                                  kb_ = KB[step_ % 4]
                                  tk.mm([lambda e: e.matmul(P[kb_][:, 0:128], lhsT=khtok[:, b_, :], rhs=vtok[:, 1 + m_, b_, :],
                                                            start=True, stop=True)],
                                        reads=['hkhtok', ('vtok', 1 + m_, b_ // 4)], writes=[('P', kb_)])

                              for s0_ in range(LA):
                                  emit_kv(s0_)
                              for step, c in enumerate(order):
                                  seg = c // cps
                                  if step + LA < nchk:
                                      emit_kv(step + LA)
                                  if step % cps == 0:
                                      if step == 0:
                                          tk.op('dve', lambda e, seg=seg: e.tensor_copy(out=S[cur][:, :], in_=sini[:, seg * 128:(seg + 1) * 128]),
                                                reads=['hsini'], writes=[('hS', cur)])
                                      else:
                                          nxt = (cur + 1) % 3
                                          tk.op('dve', lambda e, seg=seg, cur=cur, nxt=nxt: e.scalar_tensor_tensor(
                                              out=S[nxt][:, :], in0=S[cur][:, :], scalar=carry, in1=sini[:, seg * 128:(seg + 1) * 128],
                                              op0=ALU.mult, op1=ALU.add), reads=[('hS', cur), 'hsini', 'small'], writes=[('hS', nxt)])
                                          cur = nxt
                                  sbi = step % 2
                                  tk.op('act', lambda e, cur=cur, sbi=sbi: e.activation(out=Sbf[sbi][:, :], in_=S[cur][:, :], func=AF.Copy),
                                        reads=[('hS', cur)], writes=[('hSbf', sbi)])
                                  pi = 2 + (c * HC) // 512
                                  col = (c * HC) % 512
                                  tk.mm([lambda e, c=c, sbi=sbi, pi=pi, col=col: e.matmul(
                                      P[pi][:, col:col + HC], lhsT=Sbf[sbi][:, :], rhs=qtb[:, c * HC:(c + 1) * HC], start=True, stop=True)],
                                      reads=[('hSbf', sbi), 'hqtb'], writes=[('P', pi)])
                                  kb = KB[step % 4]
                                  nxt = (cur + 1) % 3
                                  tk.op('dve', lambda e, c=c, cur=cur, nxt=nxt, kb=kb: e.scalar_tensor_tensor(
                                      out=S[nxt][:, :], in0=S[cur][:, :], scalar=dec[:, c:c + 1], in1=P[kb][:, 0:128],
                                      op0=ALU.mult, op1=ALU.add), reads=[('hS', cur), ('P', kb), 'hdec'], writes=[('hS', nxt)], same_ok=True)
                                  cur = nxt
                                  if step % cps == cps - 1:
                                      tk.op('dve', lambda e, seg=seg, cur=cur: e.tensor_copy(out=sstage[:, seg * 128:(seg + 1) * 128], in_=S[cur][:, :]),
                                            reads=[('hS', cur)], writes=['hsstage'])
                              tk.dma('sp', 'hgo', hg_o[l, d, hh], sstage[:], reads=['hsstage'])
                              for h in range(2):
                                  pk = []
                                  tk.op('dve', lambda e, h=h: e.tensor_tensor(out=oacc[:, HS[h]], in0=oacc[:, HS[h]], in1=P[2 + h][:, :], op=ALU.add),
                                        reads=pk + [('hoacc', h)], writes=[('hoacc', h), ('P', 2 + h)] + pk)
                          if HSTOP <= 6:
                              continue
                          for h in range(2):
                              hp.feat(sC, 0, h, 2 + h)
                              tk.op('act', lambda e, h=h: e.activation(out=ft[:, HS[h]], in_=P[2 + h][:, :], func=AF.Silu),
                                    reads=[('P', 2 + h)], writes=['hf'])
                          for h in range(2):
                              tk.op('act', lambda e, h=h: e.activation(out=khT[:, HS[h]], in_=oacc[:, HS[h]], func=AF.Square),
                                    reads=[('hoacc', h)], writes=['hkhT'])
                              tk.mm([lambda e, h=h: e.matmul(P[h][:, :], lhsT=ones_bf[:, :], rhs=khT[:, HS[h]], start=True, stop=True)],
                                    reads=['hkhT', 'ones'], writes=[('P', h)])
                              tk.op('act', lambda e, h=h: e.activation(out=ex[:, HS[h]], in_=P[h][:, :], func=AF.Ln, bias=epst[:, 0:1], scale=1.0 / 128),
                                    reads=[('P', h), 'eps'], writes=['hex'])
                              tk.op('act', lambda e, h=h: e.activation(out=ex[:, HS[h]], in_=ex[:, HS[h]], func=AF.Exp, scale=-0.5), reads=['hex'], writes=['hex'])
                              tk.op('dve', lambda e, h=h, hh=hh: e.scalar_tensor_tensor(
                                  out=oacc[:, HS[h]], in0=oacc[:, HS[h]], scalar=hgng[:, l * 4 + hh: l * 4 + hh + 1], in1=ex[:, HS[h]],
                                  op0=ALU.mult, op1=ALU.mult), reads=['hex', ('hoacc', h), 'hgng'], writes=[('hoacc', h)])
                              tk.op('dve', lambda e, h=h, hh=hh: e.tensor_tensor(out=oabc[:, hh, HS[h]], in0=oacc[:, HS[h]], in1=ft[:, HS[h]], op=ALU.mult),
                                    reads=[('hoacc', h), 'hf'], writes=[('oabc', hh, h)])
                      tk.barrier()

                with ExitStack() as ns:
                  if STAGE >= 3:
                      npn = Panels("np", 2, 384, ns)
                      nsl = {0: npn.load(2560, 384)}
                      qT = sb("nqT", [128, T], BF16, ns)
                      kTb = sb("nkT", [128, T], BF16, ns)
                      kTf = sb("nkTf", [128, T], F32, ns)
                      vt = sb("nvt", [128, 8, 128], BF16, ns)
                      vtf = sb("nvtf", [128, 8, 128], F32, ns)
                      kc = sb("nkc", [128, 512], BF16, ns)
                      vc = sb("nvc", [128, 4, 128], BF16, ns)
                      bias = sb("nbias", [128, len(QK_TILES), 512], BF16, ns)
                      stmp = [sb("nst%d" % i, [128, 512], F32, ns) for i in range(3)]
                      pT = [sb("npT%d" % i, [128, 512], BF16, ns) for i in range(4)]
                      SBK = [0, 1, 6]
                      rden = sb("nrden", [128, 512], F32, ns)
                      pc = [0, 0]
                      for hd in range(8):
                          tk.dma('pool', 'nb', bias[:], bias_d[l, hd].rearrange("t p q -> p t q"), writes=['nbias'])
                          tk.dma('pool', 'nkc', kc[:], kctx_d[l, hd], writes=['nkc'])
                          tk.dma('pool', 'nvc', vc[:], vctx_d[l, hd].rearrange("p (b e) -> p b e", e=128), writes=['nvc'])
                          if hd + 1 < 8:
                              nsl[hd + 1] = npn.load(2560 + (hd + 1) * 384, 384)
                          s = nsl[hd]
                          for h in range(2):
                              npn.feat(s, 0, h, h)
                              tk.op('act', lambda e, h=h: e.activation(out=qT[:, HS[h]], in_=P[h][:, :], func=AF.Copy),
                                    reads=[('P', h)], writes=[('nq', h)])
                          for h in range(2):
                              npn.feat(s, 128, h, 2 + h)
                              tk.op('act', lambda e, h=h: e.activation(out=kTb[:, HS[h]], in_=P[2 + h][:, :], func=AF.Copy),
                                    reads=[('P', 2 + h)], writes=[('nk', h)])
                              tk.op('dve', lambda e, h=h: e.tensor_copy(out=kTf[:, HS[h]], in_=P[2 + h][:, :]),
                                    reads=[('P', 2 + h)], writes=[('nkf', h)])
                          tk.dma('sp', 'ko', kT_o[l, hd * 128:(hd + 1) * 128, :], kTf[:], reads=[('nkf', 0), ('nkf', 1)])
                          for b4 in range(2):
                              npn.tok(s, 256, b4, 4 + b4)
                              src = P[4 + b4][:, :].rearrange("p (j e) -> p j e", e=128)
                              tk.op('act', lambda e, b4=b4, src=src: e.activation(out=vt[:, b4 * 4:(b4 + 1) * 4, :], in_=src, func=AF.Copy),
                                    reads=[('P', 4 + b4)], writes=[('nv', b4)])
                              tk.op('dve', lambda e, b4=b4, src=src: e.tensor_copy(out=vtf[:, b4 * 4:(b4 + 1) * 4, :], in_=src),
                                    reads=[('P', 4 + b4)], writes=[('nvf', b4)])
                          tk.dma('sp', 'vo', v_o[l].rearrange("(b p) f -> p b f", p=128)[:, :, hd * 128:(hd + 1) * 128], vtf[:],
                                 reads=[('nvf', 0), ('nvf', 1)])
                          for qh in range(2):
                              tiles = [(i, kb) for i, (kb, qh2) in enumerate(QK_TILES) if qh2 == qh]
                              items = [('loc', i, kb) for (i, kb) in tiles] + [('ctx', None, cb) for cb in range(4)]
                              n_it = len(items)
                              slots_ = []
                              for ii in range(n_it):
                                  slots_.append((pc[0] % 3, pc[1] % 4))
                                  pc[0] += 1
                                  pc[1] += 1
                              A0, A1 = (2, 3) if qh == 0 else (4, 5)

                              def qk(ii):
                                  kind, bi, kb = items[ii]
                                  ps = SBK[slots_[ii][0]]
                                  if kind == 'loc':
                                      tk.mm([lambda e: e.matmul(P[ps][:, :], lhsT=kTb[:, kb * 128:(kb + 1) * 128], rhs=qT[:, HS[qh]],
                                                                start=True, stop=True)],
                                            reads=[('nk', kb // 4), ('nq', qh)], writes=[('P', ps)])
                                  else:
                                      tk.mm([lambda e: e.matmul(P[ps][:, :], lhsT=kc[:, kb * 128:(kb + 1) * 128], rhs=qT[:, HS[qh]],
                                                                start=True, stop=True)],
                                            reads=['nkc', ('nq', qh)], writes=[('P', ps)])

                              qk(0)
                              qk(1)
                              for ii, (kind, bi, kb) in enumerate(items):
                                  si_, pp = slots_[ii]
                                  ps = SBK[si_]
                                  if ii + 2 < n_it:
                                      qk(ii + 2)
                                  if kind == 'loc':
                                      tk.op('dve', lambda e, ps=ps, bi=bi, si_=si_: e.scalar_tensor_tensor(
                                          out=stmp[si_][:, :], in0=P[ps][:, :], scalar=128.0 ** -0.5, in1=bias[:, bi, :], op0=ALU.mult, op1=ALU.add),
                                          reads=[('P', ps), 'nbias'], writes=[('nst', si_)])
                                      tk.op('act', lambda e, si_=si_, pp=pp: e.activation(out=pT[pp][:, :], in_=stmp[si_][:, :], func=AF.Exp),
                                            reads=[('nst', si_)], writes=[('npT', pp)])
                                      vl = vt[:, kb, :]
                                      vkey = ('nv', kb // 4)
                                  else:
                                      tk.op('act', lambda e, ps=ps, pp=pp: e.activation(out=pT[pp][:, :], in_=P[ps][:, :], func=AF.Exp,
                                                                                        bias=ctxb, scale=128.0 ** -0.5),
                                            reads=[('P', ps), 'small'], writes=[('npT', pp)])
                                      vl = vc[:, kb, :]
                                      vkey = 'nvc'
                                  tk.mm([lambda e, vl=vl, pp=pp, ii=ii: e.matmul(P[A0][:, :], lhsT=vl, rhs=pT[pp][:, :], start=(ii == 0), stop=(ii == n_it - 1)),
                                         lambda e, pp=pp, ii=ii: e.matmul(P[A1][:, :], lhsT=ones_bf[:, :], rhs=pT[pp][:, :], start=(ii == 0), stop=(ii == n_it - 1))],
                                        reads=[('npT', pp), vkey, 'ones'], writes=[('P', A0), ('P', A1)])
                              tk.op('dve', lambda e: e.reciprocal(out=rden[:, :], in_=P[A1][:, :]), reads=[('P', A1)], writes=['nrden'])
                              tk.op('dve', lambda e, qh=qh, hd=hd: e.tensor_tensor(out=oabc[:, 4 + hd, HS[qh]], in0=P[A0][:, :], in1=rden[:, :], op=ALU.mult),
                                    reads=[('P', A0), 'nrden'], writes=[('oabc', 4 + hd, qh)])
                      tk.barrier()

                with ExitStack() as ls:
                  if STAGE >= 4:
                      PADW = SEG + 3
                      lpn = Panels("lp", 2, 256, ls)
                      lsl = {0: lpn.load(5632, 256)}
                      lxp = sb("llxp", [128, NSEG, PADW], F32, ls)
                      xc = sb("lxc", [128, T], F32, ls)
                      xcb = sb("lxcb", [128, T], BF16, ls)
                      lgt = sb("llg", [128, T], F32, ls)
                      gl = sb("lgl", [128, T], F32, ls)
                      rg = sb("lrg", [128, T], F32, ls)
                      ig = sb("lig", [128, T], F32, ls)
                      at_ = sb("lat", [128, T], F32, ls)
                      ut = sb("lut", [128, T], F32, ls)
                      hsf = sb("lhsf", [128, T], F32, ls)
                      hsb = sb("lhsb", [128, T], F32, ls)
                      ini = sb("lini", [128, 1], F32, ls)
                      bd_bf = sb("bd_bf", [128, 16, 128], BF16, ls)
                      tk.dma('pool', 'c0', bd_bf[:], lrubd_d[l].rearrange("a b c p q -> p (a b c) q"), writes=['bd'])
                      tk.op('dve', lambda e: e.memset(lxp[:], 0.0), writes=['llxp'])
                      for cc in range(4):
                          vb = l * 48
                          if cc + 1 < 4:
                              lsl[cc + 1] = lpn.load(5632 + (cc + 1) * 256, 256)
                          s = lsl[cc]
                          for h in range(2):
                              lpn.feat(s, 0, h, h)
                              tk.op('act', lambda e, h=h: e.activation(out=lxp[:, 2 * h:2 * h + 2, 2:2 + SEG],
                                                                       in_=P[h][:, :].rearrange("p (s t) -> p s t", t=SEG), func=AF.Copy),
                                    reads=[('P', h)], writes=['llxp'])
                          for h in range(2):
                              lpn.feat(s, 128, h, 2 + h)
                              tk.op('act', lambda e, h=h: e.activation(out=lgt[:, HS[h]], in_=P[2 + h][:, :], func=AF.Copy),
                                    reads=[('P', 2 + h)], writes=['llg'])
                          tk.op('dve', lambda e: e.tensor_scalar(out=lxp[:, 1:NSEG, 0:2], in0=lxp[:, 0:NSEG - 1, SEG:SEG + 2], scalar1=carry,
                                                                 scalar2=None, op0=ALU.mult), reads=['llxp', 'small'], writes=['llxp'], sync_same=True)
                          tk.op('dve', lambda e: e.tensor_scalar(out=lxp[:, 0:NSEG - 1, SEG + 2:SEG + 3], in0=lxp[:, 1:NSEG, 2:3], scalar1=carry,
                                                                 scalar2=None, op0=ALU.mult), reads=['llxp', 'small'], writes=['llxp'], sync_same=True)
                          xc3 = xc[:, :].rearrange("p (s t) -> p s t", t=SEG)
                          cw = lambda j: lruv[:, vb + j * 4 + cc: vb + j * 4 + cc + 1]
                          cb = lruv[:, vb + 16 + cc: vb + 16 + cc + 1]
                          tk.op('dve', lambda e: e.tensor_scalar(out=xc3, in0=lxp[:, :, 0:SEG], scalar1=cw(0), scalar2=cb, op0=ALU.mult, op1=ALU.add),
                                reads=['llxp', 'lruv'], writes=['lxc'], sync_same=True)
                          for j in range(1, 4):
                              tk.op('dve', lambda e, j=j: e.scalar_tensor_tensor(out=xc3, in0=lxp[:, :, j:j + SEG], scalar=cw(j), in1=xc3,
                                                                                op0=ALU.mult, op1=ALU.add), reads=['llxp', 'lxc', 'lruv'], writes=['lxc'])
                          tk.op('act', lambda e: e.activation(out=xcb[:, :], in_=xc[:, :], func=AF.Copy), reads=['lxc'], writes=['lxcb'])
                          tk.op('pool', lambda e: e.tensor_tensor(out=gl[:, :], in0=lgt[:, :], in1=lgt[:, :], op=ALU.mult), reads=['llg'], writes=['lgl'])
                          tk.op('pool', lambda e: e.tensor_scalar(out=gl[:, :], in0=gl[:, :], scalar1=0.044715, scalar2=1.0, op0=ALU.mult, op1=ALU.add),
                                reads=['lgl'], writes=['lgl'], sync_same=True)
                          tk.op('pool', lambda e: e.tensor_tensor(out=gl[:, :], in0=gl[:, :], in1=lgt[:, :], op=ALU.mult), reads=['lgl', 'llg'], writes=['lgl'], sync_same=True)
                          tk.op('act', lambda e: e.activation(out=gl[:, :], in_=gl[:, :], func=AF.Sigmoid, scale=1.5957691216057308), reads=['lgl'], writes=['lgl'])
                          tk.op('pool', lambda e: e.tensor_tensor(out=gl[:, :], in0=gl[:, :], in1=lgt[:, :], op=ALU.mult), reads=['lgl', 'llg'], writes=['lgl'])
                          for d in range(2):
                              ba = lruv[:, vb + 20 + d * 4 + cc: vb + 20 + d * 4 + cc + 1]
                              bx = lruv[:, vb + 28 + d * 4 + cc: vb + 28 + d * 4 + cc + 1]
                              n8 = nsp[:, l * 16 + d * 4 + cc: l * 16 + d * 4 + cc + 1]
                              n16 = nsp[:, l * 16 + 8 + d * 4 + cc: l * 16 + 8 + d * 4 + cc + 1]
                              bda = bd_bf[:, (d * 2 + 0) * 4 + cc, :]
                              bdx = bd_bf[:, (d * 2 + 1) * 4 + cc, :]
                              for h in range(2):
                                  tk.mm([lambda e, h=h: e.matmul(P[4][:, :], lhsT=bda, rhs=xcb[:, HS[h]], start=True, stop=True)],
                                        reads=['lxcb', 'bd'], writes=[('P', 4)])
                                  tk.op('act', lambda e, h=h: e.activation(out=rg[:, HS[h]], in_=P[4][:, :], func=AF.Sigmoid, bias=ba, scale=1.0),
                                        reads=[('P', 4), 'lruv'], writes=['lrg'])
                                  tk.mm([lambda e, h=h: e.matmul(P[5][:, :], lhsT=bdx, rhs=xcb[:, HS[h]], start=True, stop=True)],
                                        reads=['lxcb', 'bd'], writes=[('P', 5)])
                                  tk.op('act', lambda e, h=h: e.activation(out=ig[:, HS[h]], in_=P[5][:, :], func=AF.Sigmoid, bias=bx, scale=1.0),
                                        reads=[('P', 5), 'lruv'], writes=['lig'])
                              tk.op('act', lambda e: e.activation(out=at_[:, :], in_=rg[:, :], func=AF.Exp, scale=n8), reads=['lrg', ('nsp', l)], writes=['lat'])
                              tk.op('act', lambda e: e.activation(out=ut[:, :], in_=rg[:, :], func=AF.Exp, scale=n16), reads=['lrg', ('nsp2', l)], writes=['lut'])
                              tk.op('dve', lambda e: e.tensor_scalar(out=ut[:, :], in0=ut[:, :], scalar1=-1.0, scalar2=1.0, op0=ALU.mult, op1=ALU.add),
                                    reads=['lut'], writes=['lut'])
                              tk.op('act', lambda e: e.activation(out=ut[:, :], in_=ut[:, :], func=AF.Sqrt), reads=['lut'], writes=['lut'])
                              tk.op('dve', lambda e: e.tensor_tensor(out=ut[:, :], in0=ut[:, :], in1=ig[:, :], op=ALU.mult), reads=['lut', 'lig'], writes=['lut'])
                              tk.op('dve', lambda e: e.tensor_tensor(out=ut[:, :], in0=ut[:, :], in1=xc[:, :], op=ALU.mult), reads=['lut', 'lxc'], writes=['lut'], sync_same=True)
                              hs_ = hsf if d == 0 else hsb
                              hkey = 'lhs%d' % d
                              segs = list(range(NSEG)) if d == 0 else list(range(NSEG - 1, -1, -1))
                              for si, sg in enumerate(segs):
                                  h0c = h0t[:, ((l * 2 + d) * NSEG + sg) * 4 + cc: ((l * 2 + d) * NSEG + sg) * 4 + cc + 1]
                                  fcol = ((l * 2 + d) * NSEG + sg) * 4 + cc
                                  if si == 0:
                                      tk.op('dve', lambda e, h0c=h0c: e.tensor_copy(out=ini[:, :], in_=h0c), reads=['h0t', hkey], writes=['lini'], sync_same=True)
                                  else:
                                      psg = segs[si - 1]
                                      pcol = (psg * SEG + SEG - 1) if d == 0 else (psg * SEG)
                                      tk.op('dve', lambda e, h0c=h0c, pcol=pcol, hs_=hs_: e.scalar_tensor_tensor(
                                          out=ini[:, :], in0=hs_[:, pcol:pcol + 1], scalar=carry, in1=h0c, op0=ALU.mult, op1=ALU.add),
                                          reads=['h0t', hkey, 'small'], writes=['lini'], sync_same=True)
                                  sl = slice(sg * SEG, (sg + 1) * SEG)
                                  if d == 0:
                                      tk.op('dve', lambda e, sl=sl, hs_=hs_: e.tensor_tensor_scan(out=hs_[:, sl], data0=at_[:, sl], data1=ut[:, sl],
                                                                                                  initial=ini[:, 0:1], op0=ALU.mult, op1=ALU.add),
                                            reads=['lat', 'lut', 'lini'], writes=[hkey], sync_same=True)
                                      lcol = sg * SEG + SEG - 1
                                  else:
                                      rs_ = slice((sg + 1) * SEG - 1, sg * SEG - 1 if sg > 0 else None, -1)
                                      tk.op('dve', lambda e, rs_=rs_, hs_=hs_: e.tensor_tensor_scan(out=hs_[:, rs_], data0=at_[:, rs_], data1=ut[:, rs_],
                                                                                                    initial=ini[:, 0:1], op0=ALU.mult, op1=ALU.add),
                                            reads=['lat', 'lut', 'lini'], writes=[hkey], sync_same=True)
                                      lcol = sg * SEG
                                  tk.op('pool', lambda e, fcol=fcol, lcol=lcol, hs_=hs_: e.tensor_copy(out=lrufin[:, fcol:fcol + 1], in_=hs_[:, lcol:lcol + 1]),
                                        reads=[hkey], writes=['lrufin'])
                          tk.op('dve', lambda e: e.tensor_tensor(out=hsf[:, :], in0=hsf[:, :], in1=hsb[:, :], op=ALU.add), reads=['lhs0', 'lhs1'], writes=['lhs0'])
                          tk.op('dve', lambda e, cc=cc: e.tensor_tensor(out=oabc[:, 12 + cc, :], in0=hsf[:, :], in1=gl[:, :], op=ALU.mult),
                                reads=['lhs0', 'lgl'], writes=[('oabc', 12 + cc, 0), ('oabc', 12 + cc, 1)], sync_same=True)
                      tk.barrier()

                if DEBUG:
                    for c in range(NCH):
                        tk.dma('pool', 'dbg2', dbg_o[1, c * 128:(c + 1) * 128, :], oabc[:, c, :], reads=[('oabc', c, 0), ('oabc', c, 1)])
                    tk.barrier()
                with ExitStack() as go_:
                  if STAGE >= 5:
                    mT = sb("mT", [128, NCH, T], BF16, go_)
                    with ExitStack() as gs:
                      gpn = Panels("gp", 2, 384, gs)
                      gsl = {0: gpn.load(6656, 384)}
                      wp = sb("wp", [128, NCH, 128], BF16, gs)
                      gsb = [sb("gsb%d" % i, [128, 512], F32, gs) for i in range(2)]
                      macc = [sb("macc%d" % i, [128, 512], F32, gs) for i in range(2)]
                      gc = [0]
                      wsrc = [wpa_d[l].rearrange("(k p) d -> p k d", p=128), wpb_d[l].rearrange("(k p) d -> p k d", p=128),
                              wpc_d[l].rearrange("(k p) d -> p k d", p=128)]
                      krange = [(0, 4), (4, 12), (12, 16)]

                      def wp_load(br, dc):
                          k0, k1 = krange[br]
                          tk.dma('pool', 'wp' + 'abc'[br], wp[:, k0:k1, :], wsrc[br][:, :, dc * 128:(dc + 1) * 128], writes=[('wp', 0, br)])

                      for br in range(3):
                          wp_load(br, 0)
                      for dc in range(NCH):
                          if dc + 1 < NCH:
                              gsl[dc + 1] = gpn.load(6656 + (dc + 1) * 384, 384)
                          for br, (k0, k1) in enumerate(krange):
                              s = gsl[dc]
                              for h in range(2):
                                  gi = gc[0] % 2
                                  gc[0] += 1
                                  gpn.feat(s, br * 128, h, gi)
                                  tk.op('act', lambda e, gi=gi: e.activation(out=gsb[gi][:, :], in_=P[gi][:, :], func=AF.Sigmoid),
                                        reads=[('P', gi)], writes=[('gsb', gi)])
                                  tk.mm([lambda e, k=k, h=h, gi=gi: e.matmul(P[2 + gi][:, :], lhsT=wp[:, k, :], rhs=oabc[:, k, HS[h]],
                                                                            start=(k == k0), stop=(k == k1 - 1)) for k in range(k0, k1)],
                                        reads=[('wp', 0, br)] + [('oabc', k, h) for k in range(k0, k1)], writes=[('P', 2 + gi)])
                                  if br == 0:
                                      tk.op('dve', lambda e, gi=gi, h=h: e.tensor_tensor(out=macc[h][:, :], in0=gsb[gi][:, :], in1=P[2 + gi][:, :], op=ALU.mult),
                                            reads=[('gsb', gi), ('P', 2 + gi)], writes=[('macc', h)])
                                  else:
                                      tk.op('dve', lambda e, gi=gi: e.tensor_tensor(out=gsb[gi][:, :], in0=gsb[gi][:, :], in1=P[2 + gi][:, :], op=ALU.mult),
                                            reads=[('gsb', gi), ('P', 2 + gi)], writes=[('gsb', gi)])
                                      if br == 1:
                                          tk.op('pool', lambda e, gi=gi, h=h: e.tensor_tensor(out=macc[h][:, :], in0=macc[h][:, :], in1=gsb[gi][:, :], op=ALU.add),
                                                reads=[('gsb', gi), ('macc', h)], writes=[('macc', h)])
                                      else:
                                          tk.op('pool', lambda e, gi=gi, h=h, dc=dc: e.tensor_tensor(out=mT[:, dc, HS[h]], in0=macc[h][:, :], in1=gsb[gi][:, :], op=ALU.add),
                                                reads=[('gsb', gi), ('macc', h)], writes=[('mT', dc, h)])
                              if dc + 1 < NCH:
                                  wp_load(br, dc + 1)
                      tk.barrier()
                    with ExitStack() as ws_:
                      wo = [sb("wo%d" % i, [128, NCH, 128], BF16, ws_) for i in range(2)]
                      woutv = wout_d[l].rearrange("(k p) d -> p k d", p=128)
                      oc = [0]
                      tk.dma('pool', 'wo0', wo[0][:], woutv[:, :, 0:128], writes=[('wo', 0)])
                      for dc in range(NCH):
                          sp_ = dc % 2
                          if dc + 1 < NCH:
                              tk.dma('pool', 'wo%d' % (1 - sp_), wo[1 - sp_][:], woutv[:, :, (dc + 1) * 128:(dc + 2) * 128], writes=[('wo', 1 - sp_)])
                          for h in range(2):
                              pd = 4 + oc[0] % 3
                              oc[0] += 1
                              tk.mm([lambda e, k=k, h=h, pd=pd: e.matmul(P[pd][:, :], lhsT=wo[sp_][:, k, :], rhs=mT[:, k, HS[h]],
                                                                        start=(k == 0), stop=(k == NCH - 1)) for k in range(NCH)],
                                    reads=[('wo', sp_)] + [('mT', k, h) for k in range(NCH)], writes=[('P', pd)])
                              tk.op('dve', lambda e, dc=dc, h=h, pd=pd: e.scalar_tensor_tensor(
                                  out=xT[:, dc, HS[h]], in0=P[pd][:, :], scalar=g_ap(l, 1, dc), in1=xT[:, dc, HS[h]], op0=ALU.mult, op1=ALU.add),
                                  reads=[('P', pd), ('x', dc, h), ('gmod', l, 1)], writes=[('x', dc, h)])
                      tk.barrier()

        if DEBUG:
            tk.dma('sp', 'dbg', dbg_o[7, 0:128, 0:L * 144], modT[:], reads=[('modT', 0), ('modT', 1)])
        for l in range(L):
            if STAGE < 6 and l > 0:
                break
            if STAGE >= 1:
                ffn(l, 0, wup_d[0], wdn_d[0], nmods=(32 if l == 0 else 0))
                if l == 0:
                    dbg_dump(0)
            if STAGE >= 2:
                mixer(l)
                if l == 0:
                    dbg_dump(2)
            if STAGE >= 6:
                ffn(l, 2, wup_d[1], wdn_d[1], nmods=(32 if l == 0 else 0))
                if l == 0:
                    dbg_dump(3)

        with ExitStack() as fs_:
            rstd = sb("rstd", [128, T], F32, fs_)
            rstd_box[0] = rstd
            sumsq_rstd(1.0 / D)
            yst = [sb("yst%d" % i, [128, T], F32, fs_) for i in range(2)]
            for c in range(NCH):
                yb = c % 2
                tk.op('dve', lambda e, c=c, yb=yb: e.scalar_tensor_tensor(out=yst[yb][:, :], in0=xT[:, c, :], scalar=fng[:, c:c + 1], in1=rstd[:, :],
                                                                           op0=ALU.mult, op1=ALU.mult),
                      reads=[('x', c, 0), ('x', c, 1), ('rstd', 0), ('rstd', 1), 'fng'], writes=[('yst', yb)])
                tk.dma('sp', 'yo%d' % yb, yT_o[c * 128:(c + 1) * 128, :], yst[yb][:], reads=[('yst', yb)])
            tk.dma('sp', 'lo', lru_o, lrufin[:], reads=['lrufin'])
            tk.final_wait()
    return nc


def _fm(v):
    v = np.asarray(v, np.float32)
    lead = v.shape[:-1]
    n = v.shape[-1] // 128
    a = v.reshape(lead + (n, 128))
    a = np.moveaxis(a, -1, 0)
    return np.ascontiguousarray(a.reshape(128, -1))


def _na_bias_sample(rpb_l):
    rows, W, KH, KW = 16, 64, 8, 16
    r = np.arange(rows)
    kr0 = np.clip(r - KH // 2, 0, rows - KH)
    qc = np.arange(W)
    ws = np.clip(qc - KW // 2, 0, W - KW)
    kr = np.arange(rows)
    kcol = np.arange(W)
    row_ok = (kr[None, :] >= kr0[:, None]) & (kr[None, :] < kr0[:, None] + KH)
    col_ok = (kcol[None, :] >= ws[:, None]) & (kcol[None, :] < ws[:, None] + KW)
    dy = np.clip(kr[None, :] - r[:, None] + KH - 1, 0, 2 * KH - 2)
    dx = np.clip(kcol[None, :] - qc[:, None] + KW - 1, 0, 2 * KW - 2)
    b = rpb_l[:, dy[:, None, :, None], dx[None, :, None, :]]
    ok = row_ok[:, None, :, None] & col_ok[None, :, None, :]
    b = np.where(ok[None], b, np.float32(NEGM)).astype(np.float32)
    b = b.reshape(8, 1024, 1024)
    return np.ascontiguousarray(b.transpose(0, 2, 1))


def _na_bias_prompt():
    seg = np.arange(1024) // 256
    ok = seg[:, None] == seg[None, :]
    b = np.where(ok, np.float32(0.0), np.float32(NEGM)).astype(np.float32)
    return np.broadcast_to(b[None], (8, 1024, 1024))


def _tile_bias(b):
    out = np.empty((8, len(QK_TILES), 128, 512), np.float32)
    for i, (kb, qh) in enumerate(QK_TILES):
        out[:, i] = b[:, kb * 128:(kb + 1) * 128, qh * 512:(qh + 1) * 512]
    return out


_NC_CACHE = {}


def _win_perm():
    p = []
    for hh in range(4):
        for off in (C_ZQ, C_ZI, C_ZFF, C_ZFB, C_ZO):
            p.extend(range(off + hh * 128, off + (hh + 1) * 128))
    for hd in range(8):
        for off in (C_NQ, C_NK, C_NV):
            p.extend(range(off + hd * 128, off + (hd + 1) * 128))
    for cc in range(4):
        for off in (C_LX, C_LG):
            p.extend(range(off + cc * 128, off + (cc + 1) * 128))
    for dc in range(16):
        for br in range(3):
            p.extend(range(C_GATE + br * D + dc * 128, C_GATE + br * D + (dc + 1) * 128))
    assert len(p) == DIN and len(set(p)) == DIN
    return np.asarray(p)


ROLES = [('p', 0), ('p', 1), ('p', 2), ('p', 3), ('s', 0), ('s', 1), ('p', 0), ('p', 1)]


def _build_in_maps(roles, x_prompt, x_sample, cache_na_k, cache_na_v, state_hgrn, state_lru, c, c_ctx,
                   mod_w, mod_b, norm_g, ffn1_w_up, ffn1_w_down, ffn2_w_up, ffn2_w_down, w_in,
                   hgrn_lb_logits, hgrn_norm_g, na_rpb, lru_conv_w, lru_conv_b, lru_w_a, lru_b_a,
                   lru_w_x, lru_b_x, lru_lambda, w_proj_a, w_proj_b, w_proj_c, w_out, final_norm_g):
    f32 = np.float32
    A = lambda a: np.ascontiguousarray(np.asarray(a, f32))
    x_prompt, x_sample = A(x_prompt), A(x_sample)
    cache_na_k, cache_na_v = A(cache_na_k), A(cache_na_v)
    state_hgrn, state_lru = A(state_hgrn), A(state_lru)
    c, c_ctx = A(c), A(c_ctx)
    na_rpb = A(na_rpb)

    shared = {
        "mod_w": A(mod_w), "ffn1_w_up": A(ffn1_w_up), "ffn1_w_down": A(ffn1_w_down),
        "ffn2_w_up": A(ffn2_w_up), "ffn2_w_down": A(ffn2_w_down), "w_in": np.ascontiguousarray(A(w_in)[:, :, _win_perm()]),
        "w_proj_a": A(w_proj_a), "w_proj_b": A(w_proj_b), "w_proj_c": A(w_proj_c), "w_out": A(w_out),
        "mod_bT": _fm(A(mod_b)).reshape(128, L, 144).reshape(128, L * 144),
        "normgT": _fm(A(norm_g)),
        "fngT": _fm(A(final_norm_g)),
        "ident": np.eye(128, dtype=f32),
        "hgngT": _fm(A(hgrn_norm_g)),
    }
    shared["hglogT"] = _fm(A(hgrn_lb_logits))
    s_i = np.arange(128)
    same = (s_i[:, None] // HC) == (s_i[None, :] // HC)
    hm = np.stack([same & (s_i[:, None] <= s_i[None, :]), same & (s_i[:, None] >= s_i[None, :])]).astype(f32)
    shared["hmask"] = hm
    rm = np.zeros((128, 4), f32)
    for m in range(4):
        rm[m * HC:(m + 1) * HC, m] = 1.0
    shared["rowmask"] = rm
    lv = np.zeros((128, L, 48), f32)
    cw = A(lru_conv_w)
    lv[:, :, 0:16] = _fm(cw).reshape(128, L, 16)
    lv[:, :, 16:20] = _fm(A(lru_conv_b)).reshape(128, L, 4)
    lv[:, :, 20:28] = _fm(A(lru_b_a)).reshape(128, L, 8)
    lv[:, :, 28:36] = _fm(A(lru_b_x)).reshape(128, L, 8)
    lv[:, :, 36:44] = _fm(A(lru_lambda)).reshape(128, L, 8)
    shared["lru_vecT"] = np.ascontiguousarray(lv.reshape(128, L * 48))
    bd = np.zeros((L, 2, 2, 4, 128, 128), f32)
    for wi, w in enumerate((A(lru_w_a), A(lru_w_x))):
        for cc in range(4):
            for kk in range(2):
                bd[:, :, wi, cc, kk * 64:(kk + 1) * 64, kk * 64:(kk + 1) * 64] = w[:, :, cc * 2 + kk]
    shared["lru_bd"] = bd

    bias_prompt = None
    bias_sample = None
    in_maps = []
    for kind, idx in roles:
        m = dict(shared)
        sm = np.zeros((128, 8), f32)
        if kind == 'p':
            xs = x_prompt[idx * 4:(idx + 1) * 4].reshape(T, D)
            cond = c_ctx
            sm[:, 0] = 0.0
            sm[:, 1] = NEGM
            s_init = np.zeros((L, 2, 4, 128, NSEG, 128), f32)
            kctx = np.zeros((L, 8, 128, 512), f32)
            vctx = np.zeros((L, 8, 128, 4, 128), f32)
            if bias_prompt is None:
                bp = _tile_bias(_na_bias_prompt())
                bias_prompt = np.ascontiguousarray(np.broadcast_to(bp[None], (L,) + bp.shape))
            nb = bias_prompt
            h0 = np.zeros((128, L, 2, NSEG, 4), f32)
        else:
            xs = x_sample[idx]
            cond = c[idx]
            sm[:, 0] = 1.0
            sm[:, 1] = 0.0
            s_init = np.zeros((L, 2, 4, 128, NSEG, 128), f32)
            st = state_hgrn[idx]
            s_init[:, 0, :, :, 0, :] = st[:, 0]
            s_init[:, 1, :, :, NSEG - 1, :] = st[:, 1]
            kctx = np.ascontiguousarray(cache_na_k[idx].transpose(0, 2, 3, 1))
            vv = cache_na_v[idx].reshape(L, 4, 128, 8, 128)
            vctx = np.ascontiguousarray(vv.transpose(0, 3, 2, 1, 4))
            if bias_sample is None:
                bias_sample = np.stack([_tile_bias(_na_bias_sample(na_rpb[l])) for l in range(L)])
            nb = bias_sample
            h0 = np.zeros((128, L, 2, NSEG, 4), f32)
            sl = state_lru[idx]
            slf = _fm(sl).reshape(128, L, 2, 4)
            h0[:, :, 0, 0, :] = slf[:, :, 0]
            h0[:, :, 1, NSEG - 1, :] = slf[:, :, 1]
        m["xT"] = np.ascontiguousarray(xs.T)
        m["condT"] = _fm(cond)
        m["small"] = sm
        m["s_init"] = s_init.reshape(L, 2, 4, 128, NSEG * 128)
        m["kctxT"] = kctx
        m["vctx"] = vctx.reshape(L, 8, 128, 512)
        m["na_bias"] = nb
        m["lru_h0T"] = np.ascontiguousarray(h0.reshape(128, -1))
        in_maps.append(m)
    return in_maps


def kernel(**inputs):
    f32 = np.float32
    in_maps = _build_in_maps(ROLES, **inputs)
    if 'nc' not in _NC_CACHE:
        _NC_CACHE['nc'] = build_nc()
    nc = _NC_CACHE['nc']
    res = run_bass_kernel_spmd(nc, in_maps, core_ids=list(range(8)))
    R = res.results

    y_prompt = np.empty((16, 256, D), f32)
    y_sample = np.empty((2, T, D), f32)
    nk = np.empty((16, L, 256, 8, 128), f32)
    nv = np.empty((16, L, 256, 8, 128), f32)
    nhg = np.empty((16, L, 2, 4, 128, 128), f32)
    nlru = np.empty((16, L, 2, 512), f32)
    for ci in range(4):
        r = R[ci]
        y = np.asarray(r["yT"]).T.reshape(4, 256, D)
        y_prompt[ci * 4:(ci + 1) * 4] = y
        kT = np.asarray(r["kT_out"])
        nk[ci * 4:(ci + 1) * 4] = kT.transpose(2, 0, 1).reshape(4, 256, L, 8, 128).transpose(0, 2, 1, 3, 4)
        vo = np.asarray(r["v_out"])
        nv[ci * 4:(ci + 1) * 4] = vo.reshape(L, 4, 256, 8, 128).transpose(1, 0, 2, 3, 4)
        hg = np.asarray(r["hg_out"]).reshape(L, 2, 4, 128, NSEG, 128)
        nhg[ci * 4:(ci + 1) * 4] = hg.transpose(4, 0, 1, 2, 3, 5)
        lo = np.asarray(r["lru_out"]).reshape(128, L, 2, NSEG, 4)
        nlru[ci * 4:(ci + 1) * 4] = lo.transpose(3, 1, 2, 4, 0).reshape(NSEG, L, 2, 512)
    for b in range(2):
        y_sample[b] = np.asarray(R[4 + b]["yT"]).T
    if DEBUG:
        kernel.debug = [np.asarray(R[i]["dbg"]) for i in range(8)]
    return (y_prompt, y_sample, nk, nv, nhg, nlru)
```
